# Optimizing a Trainium2 kernel written in Bass

```python
import math
import jax, jax.numpy as jnp
from jax import lax
import numpy as np

D_MODEL = 2048
BATCH = 8
SEQ = 2048
DEPTH = 4

GRID_W = 64
MIX_WIDTH = D_MODEL
HALF = MIX_WIDTH // 2
HEAD_DIM = 128
NA_HEADS = HALF // HEAD_DIM
NA_WIN_ROWS_MAX = 8
NA_WIN_COLS = 16
HG_HEADS = HALF // HEAD_DIM
HG_EXPAND = 128
HG_FDIM = HG_HEADS * HG_EXPAND
HG_VDIM = HALF // HG_HEADS
HG_CHUNK = 64
POOL_WINDOWS = (2, 4, 8, 16)
POOL_GROUPS = len(POOL_WINDOWS)
POOL_GROUP_DIM = HALF // POOL_GROUPS
GQA_Q_HEADS = HALF // HEAD_DIM
GQA_KV_HEADS = 2
KV_W = GQA_KV_HEADS * HEAD_DIM
Q_BLOCK = 128
ROPE_THETA = 10000.0
FFN_HIDDEN = -(-8 * D_MODEL // (3 * 256)) * 256
AB_IN = 3 * HALF + 2 * HG_FDIM + HG_FDIM + 2 * HALF
CD_IN = HALF + HALF + 2 * KV_W
N_AB = (DEPTH + 1) // 2
N_CD = DEPTH // 2
DN_ALPHA = (2 * DEPTH) ** 0.25
DN_BETA = (8 * DEPTH) ** -0.25
LN_EPS = 1e-5
RMS_EPS = 1e-6

kernel_name = "hybrid_natten_hgrn2_pool_gqa_encoder"


def layer_norm(x, g, b):
    xf = x.astype(jnp.float32)
    mu = jnp.mean(xf, axis=-1, keepdims=True)
    var = jnp.mean(jnp.square(xf - mu), axis=-1, keepdims=True)
    return ((xf - mu) * lax.rsqrt(var + LN_EPS)).astype(x.dtype) * g + b


def rms_norm(x, g):
    xf = x.astype(jnp.float32)
    ms = jnp.mean(jnp.square(xf), axis=-1, keepdims=True)
    return (xf * lax.rsqrt(ms + RMS_EPS)).astype(x.dtype) * g


def neighbourhood_attention(q, k, v, rpb):
    b, t, h, dh = q.shape
    rows = t // GRID_W
    kh = min(NA_WIN_ROWS_MAX, rows)
    qg = q.reshape(b, rows, GRID_W, h, dh) * (dh ** -0.5)
    kg = k.reshape(b, rows, GRID_W, h, dh)
    vg = v.reshape(b, rows, GRID_W, h, dh)
    col = jnp.arange(GRID_W)
    col_start = jnp.clip(col - NA_WIN_COLS // 2, 0, GRID_W - NA_WIN_COLS)
    col_mask = (col[None, :] >= col_start[:, None]) & (col[None, :] < col_start[:, None] + NA_WIN_COLS)
    col_idx = jnp.clip(col[None, :] - col[:, None] + NA_WIN_COLS - 1, 0, 2 * NA_WIN_COLS - 2)

    def row_step(r):
        r0 = jnp.clip(r - kh // 2, 0, rows - kh)
        k_band = lax.dynamic_slice_in_dim(kg, r0, kh, axis=1)
        v_band = lax.dynamic_slice_in_dim(vg, r0, kh, axis=1)
        q_row = lax.dynamic_index_in_dim(qg, r, axis=1, keepdims=False)
        s = jnp.einsum('bqhd,bjkhd->bhqjk', q_row, k_band).astype(jnp.float32)
        row_idx = r0 + jnp.arange(kh) - r + NA_WIN_ROWS_MAX - 1
        bias = rpb[:, row_idx[None, :, None], col_idx[:, None, :]]
        s = s + bias[None].astype(jnp.float32)
        s = jnp.where(col_mask[None, None, :, None, :], s, -jnp.inf)
        p = jax.nn.softmax(s.reshape(b, h, GRID_W, kh * GRID_W), axis=-1)
        p = p.astype(v.dtype).reshape(b, h, GRID_W, kh, GRID_W)
        return jnp.einsum('bhqjk,bjkhd->bqhd', p, v_band)

    out = lax.map(row_step, jnp.arange(rows))
    return out.transpose(1, 0, 2, 3, 4).reshape(b, t, h * dh)


def chunked_gated_recurrence(q, k, v, log_f):
    b, h, t, dk = q.shape
    dv = v.shape[-1]
    c = HG_CHUNK
    n = t // c

    def to_chunks(a):
        return a.astype(jnp.float32).reshape(b, h, n, c, a.shape[-1]).transpose(2, 0, 1, 3, 4)

    qc, kc, vc, gc = (to_chunks(a) for a in (q, k, v, log_f))
    lower = jnp.tril(jnp.ones((c, c), dtype=bool))

    def step(S, inp):
        qi, ki, vi, gi = inp
        G = jnp.cumsum(gi, axis=2)
        diff = G[:, :, :, None, :] - G[:, :, None, :, :]
        decay = jnp.exp(jnp.where(lower[None, None, :, :, None], diff, -jnp.inf))
        A = jnp.einsum('bhtd,bhtsd,bhsd->bhts', qi, decay, ki)
        o = jnp.einsum('bhts,bhsv->bhtv', A, vi) + jnp.einsum('bhtd,bhdv->bhtv', qi * jnp.exp(G), S)
        G_last = G[:, :, -1:, :]
        S = jnp.exp(G_last[:, :, 0, :])[..., None] * S + jnp.einsum('bhsd,bhsv->bhdv', ki * jnp.exp(G_last - G), vi)
        return S, o

    S0 = jnp.zeros((b, h, dk, dv), jnp.float32)
    _, o = lax.scan(step, S0, (qc, kc, vc, gc))
    return o.transpose(1, 2, 0, 3, 4).reshape(b, h, t, dv)


def hgrn2_mixer(q_in, ff_in, fb_in, i_in, g_in, lb, norm_w):
    b, t, _ = q_in.shape

    def heads(z, d):
        return z.reshape(b, t, HG_HEADS, d).transpose(0, 2, 1, 3)

    q = heads(jax.nn.silu(q_in), HG_EXPAND)
    v = heads(i_in, HG_VDIM)

    def direction(z, reverse):
        zf = z.astype(jnp.float32)
        f = lb + (1.0 - lb) * jax.nn.sigmoid(zf)
        k = (1.0 - lb) * jax.nn.sigmoid(-zf)
        args = (q, heads(k, HG_EXPAND), v, heads(jnp.log(f), HG_EXPAND))
        if reverse:
            args = tuple(jnp.flip(a, axis=2) for a in args)
            return jnp.flip(chunked_gated_recurrence(*args), axis=2)
        return chunked_gated_recurrence(*args)

    o = direction(ff_in, False) + direction(fb_in, True)
    o = rms_norm(o.transpose(0, 2, 1, 3), norm_w.reshape(HG_HEADS, HG_VDIM))
    return o.reshape(b, t, HALF).astype(g_in.dtype) * jax.nn.silu(g_in)


def multiscale_pool(x, w_groups, scale):
    b, t, _ = x.shape
    xf = x.astype(jnp.float32)
    cs = jnp.concatenate([jnp.zeros((b, 1, HALF), jnp.float32), jnp.cumsum(xf, axis=1)], axis=1)
    pos = jnp.arange(t)
    outs = []
    for gi, w in enumerate(POOL_WINDOWS):
        sl = slice(gi * POOL_GROUP_DIM, (gi + 1) * POOL_GROUP_DIM)
        lo = jnp.clip(pos - w // 2, 0, t)
        hi = jnp.clip(pos + w // 2, 0, t)
        seg = cs[:, :, sl]
        mean = (seg[:, hi] - seg[:, lo]) / (hi - lo).astype(jnp.float32)[None, :, None]
        outs.append(mean - xf[:, :, sl])
    pooled = jnp.stack(outs, axis=2).astype(x.dtype)
    y = jnp.einsum('btgc,gcd->btgd', pooled, w_groups).reshape(b, t, HALF)
    return y * scale


def axial_rope_tables(t):
    pos = jnp.arange(t)
    row = (pos // GRID_W).astype(jnp.float32)
    col = (pos % GRID_W).astype(jnp.float32)
    n_freq = HEAD_DIM // 4
    inv = ROPE_THETA ** (-jnp.arange(n_freq, dtype=jnp.float32) / n_freq)
    ang = jnp.concatenate([row[:, None] * inv, col[:, None] * inv], axis=-1)
    return jnp.cos(ang), jnp.sin(ang)


def apply_axial_rope(x, cos, sin):
    xr = x.astype(jnp.float32).reshape(*x.shape[:-1], HEAD_DIM // 2, 2)
    x0, x1 = xr[..., 0], xr[..., 1]
    c = cos[None, :, None, :]
    s = sin[None, :, None, :]
    out = jnp.stack([x0 * c - x1 * s, x0 * s + x1 * c], axis=-1).reshape(x.shape)
    return out.astype(x.dtype)


def gqa_attention(q, k, v):
    b, t, hq, dh = q.shape
    hkv = k.shape[2]
    g = hq // hkv
    qb = (q * (dh ** -0.5)).reshape(b, t // Q_BLOCK, Q_BLOCK, hkv, g, dh).transpose(1, 0, 2, 3, 4, 5)

    def block(qi):
        s = jnp.einsum('bqkgd,bskd->bkgqs', qi, k).astype(jnp.float32)
        p = jax.nn.softmax(s, axis=-1).astype(v.dtype)
        return jnp.einsum('bkgqs,bskd->bqkgd', p, v)

    o = lax.map(block, qb)
    return o.transpose(1, 0, 2, 3, 4, 5).reshape(b, t, hq * dh)


def ab_mixer(x, w_in, w_out, rpb, lb, hg_norm_w):
    b, t, _ = x.shape
    proj = x @ w_in
    a_q, a_k, a_v, h_q, h_ff, h_fb, h_i, h_g = jnp.split(proj, 8, axis=-1)

    def heads(z):
        return z.reshape(b, t, NA_HEADS, HEAD_DIM)

    y_a = neighbourhood_attention(heads(a_q), heads(a_k), heads(a_v), rpb)
    y_b = hgrn2_mixer(h_q, h_ff, h_fb, h_i, h_g, lb, hg_norm_w)
    return jnp.concatenate([y_a, y_b], axis=-1) @ w_out


def cd_mixer(x, w_in, w_out, pool_w, pool_scale, q_norm, k_norm, cos, sin):
    b, t, _ = x.shape
    proj = x @ w_in
    c_x, d_q, d_k, d_v = jnp.split(proj, [HALF, 2 * HALF, 2 * HALF + KV_W], axis=-1)
    y_c = multiscale_pool(c_x, pool_w, pool_scale)
    q = apply_axial_rope(rms_norm(d_q.reshape(b, t, GQA_Q_HEADS, HEAD_DIM), q_norm), cos, sin)
    k = apply_axial_rope(rms_norm(d_k.reshape(b, t, GQA_KV_HEADS, HEAD_DIM), k_norm), cos, sin)
    v = d_v.reshape(b, t, GQA_KV_HEADS, HEAD_DIM)
    y_d = gqa_attention(q, k, v)
    return jnp.concatenate([y_c, y_d], axis=-1) @ w_out


def swiglu(x, w_gate, w_up, w_down):
    return (jax.nn.silu(x @ w_gate) * (x @ w_up)) @ w_down


def setup_inputs(seed: int = 0) -> dict:
    key = jax.random.key(seed)
    ks = jax.random.split(key, 19)
    n = jax.random.normal
    f32 = jnp.float32
    return {
        "x": n(ks[0], (BATCH, SEQ, D_MODEL), f32),
        "ab_w_in": n(ks[1], (N_AB, D_MODEL, AB_IN), f32) * D_MODEL ** -0.5,
        "ab_w_out": n(ks[2], (N_AB, MIX_WIDTH, D_MODEL), f32) * (MIX_WIDTH ** -0.5 * DN_BETA),
        "na_rpb": 0.1 * n(ks[3], (N_AB, NA_HEADS, 2 * NA_WIN_ROWS_MAX - 1, 2 * NA_WIN_COLS - 1), f32),
        "hg_lb_logits": 0.5 * n(ks[4], (N_AB, HG_FDIM), f32),
        "hg_norm_w": 1.0 + 0.1 * n(ks[5], (N_AB, HALF), f32),
        "cd_w_in": n(ks[6], (N_CD, D_MODEL, CD_IN), f32) * D_MODEL ** -0.5,
        "cd_w_out": n(ks[7], (N_CD, MIX_WIDTH, D_MODEL), f32) * (MIX_WIDTH ** -0.5 * DN_BETA),
        "pool_w": n(ks[8], (N_CD, POOL_GROUPS, POOL_GROUP_DIM, POOL_GROUP_DIM), f32) * POOL_GROUP_DIM ** -0.5,
        "pool_scale": 1.0 + 0.1 * n(ks[9], (N_CD, HALF), f32),
        "d_q_norm": 1.0 + 0.1 * n(ks[10], (N_CD, HEAD_DIM), f32),
        "d_k_norm": 1.0 + 0.1 * n(ks[11], (N_CD, HEAD_DIM), f32),
        "ln_mix_g": 1.0 + 0.1 * n(ks[12], (DEPTH, D_MODEL), f32),
        "ln_mix_b": 0.02 * n(ks[13], (DEPTH, D_MODEL), f32),
        "ffn_w_gate": n(ks[14], (DEPTH, D_MODEL, FFN_HIDDEN), f32) * D_MODEL ** -0.5,
        "ffn_w_up": n(ks[15], (DEPTH, D_MODEL, FFN_HIDDEN), f32) * D_MODEL ** -0.5,
        "ffn_w_down": n(ks[16], (DEPTH, FFN_HIDDEN, D_MODEL), f32) * (FFN_HIDDEN ** -0.5 * DN_BETA),
        "ln_ffn_g": 1.0 + 0.1 * n(ks[17], (DEPTH, D_MODEL), f32),
        "ln_ffn_b": 0.02 * n(ks[18], (DEPTH, D_MODEL), f32),
    }


def reference(x, ab_w_in, ab_w_out, na_rpb, hg_lb_logits, hg_norm_w, cd_w_in, cd_w_out,
              pool_w, pool_scale, d_q_norm, d_k_norm, ln_mix_g, ln_mix_b,
              ffn_w_gate, ffn_w_up, ffn_w_down, ln_ffn_g, ln_ffn_b):
    t = x.shape[1]
    cos, sin = axial_rope_tables(t)
    lb_p = jax.nn.softmax(hg_lb_logits.astype(jnp.float32), axis=0)
    lower_bounds = jnp.cumsum(lb_p, axis=0) - lb_p[0]
    for layer in range(DEPTH):
        j = layer // 2
        if layer % 2 == 0:
            y = ab_mixer(x, ab_w_in[j], ab_w_out[j], na_rpb[j], lower_bounds[j], hg_norm_w[j])
        else:
            y = cd_mixer(x, cd_w_in[j], cd_w_out[j], pool_w[j], pool_scale[j],
                         d_q_norm[j], d_k_norm[j], cos, sin)
        x = layer_norm(DN_ALPHA * x + y, ln_mix_g[layer], ln_mix_b[layer])
        f = swiglu(x, ffn_w_gate[layer], ffn_w_up[layer], ffn_w_down[layer])
        x = layer_norm(DN_ALPHA * x + f, ln_ffn_g[layer], ln_ffn_b[layer])
    return x
```

```python
import numpy as np
import ml_dtypes
from contextlib import ExitStack
import concourse.bass as bass
import concourse.mybir as mybir
from concourse.bass_utils import run_bass_kernel_spmd

F32, BF16 = mybir.dt.float32, mybir.dt.bfloat16
AF = mybir.ActivationFunctionType
ALU = mybir.AluOpType
AX = mybir.AxisListType

T = 2048; D = 2048; HID = 5632; NKC = 16; NB = 4; BLK = 512
DEPTH = 4
ALPHA = (2 * DEPTH) ** 0.25
LN_EPS = 1e-5; RMS_EPS = 1e-6
NEG = -30000.0

VC = {}
_c = 0
for _n, _w in (("ln_mix_g", 64), ("ln_mix_b", 64), ("ln_ffn_g", 64), ("ln_ffn_b", 64),
               ("lb_logit", 16), ("hg_norm_w", 16), ("pool_scale", 16), ("q_norm", 2), ("k_norm", 2),
               ("eps_ln", 1), ("eps_rms", 1), ("zero", 1), ("one", 1)):
    VC[_n] = _c; _c += _w
NV = _c


class Sched:
    def __init__(self, nc, es):
        self.nc = nc; self.es = es
        self.eng = {}
        for name, h in (("pe", nc.tensor), ("act", nc.scalar), ("dve", nc.vector), ("sp", nc.sync), ("pool", nc.gpsimd)):
            self.eng[name] = dict(h=h, sem=None, cnt=0, waited={}, name=name)
        self.sems = {}; self.latest = {}; self.tok = {}; self.nobar = set()
        self.epoch = -1
        self.new_epoch()

    def _newsem(self, name):
        s = self.es.enter_context(self.nc.semaphore(name))
        self.sems[name] = s; self.latest[name] = 0
        return name

    def new_epoch(self):
        self.epoch += 1
        for n in ("pe", "act", "dve"):
            e = self.eng[n]; e["sem"] = self._newsem(f"e{self.epoch}_{n}"); e["cnt"] = 0

    def _wait(self, e, needs):
        for k, v in needs.items():
            if v <= 0 or e["waited"].get(k, 0) >= v:
                continue
            e["h"].wait_ge(self.sems[k], v); e["waited"][k] = v

    def _needs(self, reads, writes, own, skip_own_raw):
        needs = {}
        for t in reads:
            w, r = self.tok.setdefault(t, ({}, {}))
            for k, v in w.items():
                if k == own and skip_own_raw:
                    continue
                if needs.get(k, 0) < v: needs[k] = v
        for t in writes:
            w, r = self.tok.setdefault(t, ({}, {}))
            for d in (w, r):
                for k, v in d.items():
                    if k == own:
                        continue
                    if needs.get(k, 0) < v: needs[k] = v
        return needs

    def op(self, ename, fn, reads=(), writes=(), inc=True):
        e = self.eng[ename]; own = e["sem"]
        self._wait(e, self._needs(reads, writes, own, ename == "pe"))
        ins = fn(e["h"])
        if inc:
            e["cnt"] += 1
            ins.then_inc(self.sems[own], 1)
            self.latest[own] = e["cnt"]
            val = e["cnt"]
        else:
            val = e["cnt"] + 1
        for t in reads: self.tok[t][1][own] = val
        for t in writes: self.tok[t][0][own] = val
        return ins

    def dma(self, qname, out, in_, reads, writes, key, nobar=False, wval=None):
        e = self.eng[qname]
        if key not in self.sems:
            self._newsem(key)
            if nobar: self.nobar.add(key)
        self._wait(e, self._needs(reads, writes, None, False))
        ins = e["h"].dma_start(out=out, in_=in_)
        self.latest[key] += 16
        ins.then_inc(self.sems[key], 16)
        v = self.latest[key]
        for t in reads: self.tok.setdefault(t, ({}, {}))[1][key] = v
        for t in writes: self.tok.setdefault(t, ({}, {}))[0][key] = v
        return v

    def barrier(self):
        need = {k: v for k, v in self.latest.items() if k not in self.nobar}
        for n in ("pe", "act", "dve", "sp"):
            self._wait(self.eng[n], need)


def build_program(debug=(), n_layers=DEPTH, stop_after=None):
    nc = bass.Bass("TRN2", target_bir_lowering=False)
    dbg = set(debug)

    def din(name, shape, dt=F32):
        return nc.dram_tensor(name, list(shape), dt, kind="ExternalInput").ap()

    def dscr(name, shape, dt):
        kind = "ExternalOutput" if name in dbg else "Internal"
        return nc.dram_tensor(name, list(shape), dt, kind=kind).ap()

    xT = din("xT", [D, T])
    ab_w_in = din("ab_w_in", [2, D, 8192]); ab_w_out = din("ab_w_out", [2, D, D])
    cd_w_in = din("cd_w_in", [2, D, 2560]); cd_w_out = din("cd_w_out", [2, D, D])
    pool_w = din("pool_w", [2, 4, 256, 256])
    w_gate = din("ffn_w_gate", [4, D, HID]); w_up = din("ffn_w_up", [4, D, HID]); w_down = din("ffn_w_down", [4, HID, D])
    vecs = din("vecs", [128, NV])
    na_bias = din("na_bias", [2, 8, 64, 8 * 512])
    cm = din("cm", [128, 4 * 128 + 64])
    cb = din("cb", [128, 128], BF16)
    scanmask = din("scanmask", [128, T], BF16)
    ropeC = din("ropeC", [128, T]); ropeS = din("ropeS", [128, T])
    poolinv = din("poolinv", [128, 4 * T])
    outT = nc.dram_tensor("outT", [D, T], F32, kind="ExternalOutput").ap()

    XR = dscr("XR", [D, T], F32)
    WINs = [dscr(f"WIN{l}", [32 if l % 2 == 0 else 10, 128, 16 * 256], BF16) for l in range(4)]
    WOUTs = [dscr(f"WOUT{l}", [16, 128, 16 * 128], BF16) for l in range(4)]
    WGs = [dscr(f"WG{l}", [22, 128, 16 * 256], BF16) for l in range(4)]
    WUs = [dscr(f"WU{l}", [22, 128, 16 * 256], BF16) for l in range(4)]
    WDs = [dscr(f"WD{l}", [16, 128, 44 * 128], BF16) for l in range(4)]
    WPs_ = [dscr(f"WP{j}", [128, 4 * 2 * 256], BF16) for j in range(2)]
    AQK = dscr("AQK", [2048, T], BF16)
    AV = dscr("AV", [T, 1024], BF16)
    HQ = dscr("HQ", [1024, T], BF16)
    HF = dscr("HF", [2048, T], F32)
    HI = dscr("HI", [T, 1024], BF16)
    HG = dscr("HG", [1024, T], BF16)
    CX = dscr("CX", [1024, T], F32)
    DQK = dscr("DQK", [1280, T], F32)
    DV = dscr("DV", [T, 256], BF16)
    YT = dscr("YT", [2048, T], BF16)
    HT = dscr("HT", [HID, T], BF16)

    with ExitStack() as es:
        S = Sched(nc, es)
        block = es.enter_context(nc.Block())

        def sb(name, shape, dt):
            return es.enter_context(nc.sbuf_tensor(name, list(shape), dt))

        _uid = [0]

        def nc_sbuf(name, shape, dt):
            _uid[0] += 1
            return nc.sbuf_tensor(f"{name}_u{_uid[0]}", shape, dt)

        def nc_psum(name, shape, dt):
            _uid[0] += 1
            return nc.psum_tensor(f"{name}_u{_uid[0]}", shape, dt)

        XB = sb("XB", [128, NKC, T], BF16)
        VEC = sb("VEC", [128, NV], F32)
        CM = sb("CM", [128, 4 * 128 + 64], F32)
        CB = sb("CB", [128, 128], BF16)
        WS = [sb(f"WS{i}", [128, 5632], BF16) for i in range(3)]
        ws_ctr = [0]

        S.dma("sp", VEC[:], vecs[:, :], [], ["VEC"], "c_vec")
        S.dma("sp", CM[:], cm[:, :], [], ["CM"], "c_cm")
        S.dma("sp", CB[:], cb[:, :], [], ["CB"], "c_cb")
        ONES_D = CM[:, 0:128]; ONES_H = CM[:, 128:256]; ROTP = CM[:, 256:384]
        TRIU = CM[0:64, 384:448]; TRIL = CM[0:64, 448:512]

        def vcol(name, i=0, n=1):
            return VEC[:, VC[name] + i: VC[name] + i + n]

        cast_jobs = []

        def cj(tokn, out, in_, nd):
            cast_jobs.append((tokn, out, in_, nd))

        def wset(l):
            return (WINs[l], WOUTs[l], WGs[l], WUs[l], WDs[l])

        for l in range(n_layers):
            j = l // 2
            win, wout, wg, wu, wd = wset(l)
            if l % 2 == 0:
                src = ab_w_in[j].rearrange("(kc p) (b c) -> b p kc c", p=128, c=256)
                for b in range(32):
                    cj(("win", l), win[b].rearrange("p (kc c) -> p kc c", c=256), src[b], 128)
                so = ab_w_out[j].rearrange("(kc p) (m c) -> m p kc c", p=128, c=128)
            else:
                src = cd_w_in[j].rearrange("(kc p) (b c) -> b p kc c", p=128, c=256)
                for b in range(10):
                    cj(("win", l), win[b].rearrange("p (kc c) -> p kc c", c=256), src[b], 128)
                cj(("wp", l), WPs_[j][:].rearrange("p (g kc d) -> p g kc d", g=4, kc=2),
                   pool_w[j].rearrange("g (kc p) d -> p g kc d", p=128), 64)
                so = cd_w_out[j].rearrange("(kc p) (m c) -> m p kc c", p=128, c=128)
            for m in range(16):
                cj(("wout", l), wout[m].rearrange("p (kc c) -> p kc c", c=128), so[m], 128)
            sg = w_gate[l].rearrange("(kc p) (b c) -> b p kc c", p=128, c=256)
            su = w_up[l].rearrange("(kc p) (b c) -> b p kc c", p=128, c=256)
            for b in range(22):
                cj(("wg", l), wg[b].rearrange("p (kc c) -> p kc c", c=256), sg[b], 128)
                cj(("wg", l), wu[b].rearrange("p (kc c) -> p kc c", c=256), su[b], 128)
            sd = w_down[l].rearrange("(kc p) (m c) -> m p kc c", p=128, c=128)
            for m in range(16):
                for hh in range(2):
                    cj(("wd", l), wd[m].rearrange("p (kc c) -> p kc c", c=128)[:, hh * 22:(hh + 1) * 22, :],
                       sd[m][:, hh * 22:(hh + 1) * 22, :], 176)
        S._newsem("cast"); S.nobar.add("cast")
        pool = S.eng["pool"]
        inflight = []; tot = 0
        ends = {}
        for i, (tokn, out, in_, nd) in enumerate(cast_jobs):
            while inflight and sum(x[1] for x in inflight) + nd > 850:
                v0, _ = inflight.pop(0)
                pool["h"].wait_ge(S.sems["cast"], v0)
            ins = None
            ins = pool["h"].dma_start(out=out, in_=in_)
            tot += 16
            ins.then_inc(S.sems["cast"], 16)
            inflight.append((tot, nd))
            ends[tokn] = tot
        S.latest["cast"] = tot
        cast_total = tot
        for tokn, v in ends.items():
            vv = min(cast_total, v + 16 * 8)
            S.tok.setdefault(("wc",) + tokn, ({}, {}))[0]["cast"] = vv

        def load_w(scr_ap, nel, tokc, shape3):
            i = ws_ctr[0] % 3; ws_ctr[0] += 1
            S.dma("sp", WS[i][:, 0:nel], scr_ap, [tokc], [("WS", i)], f"ws{i}")
            kc, c = shape3
            return WS[i][:, 0:nel].rearrange("p (kc c) -> p kc c", c=c), ("WS", i)

        evac_ctr = [0]

        def evac(out, in_, reads, writes, eng=None):
            if eng is None:
                eng = "act" if evac_ctr[0] % 2 == 0 else "dve"; evac_ctr[0] += 1
            if eng == "act":
                S.op("act", lambda e: e.activation(out=out, in_=in_, func=AF.Copy), reads, writes)
            else:
                S.op("dve", lambda e: e.tensor_copy(out=out, in_=in_), reads, writes)

        def mm(out, lhsT, rhs, start, stop, reads, writes, inc):
            S.op("pe", lambda e: e.matmul(out, lhsT, rhs, start=start, stop=stop), reads, writes, inc=inc)

        def phase_proj(l, nblk, spec):
            win = wset(l)[0]
            with ExitStack() as ps:
                PSB = [ps.enter_context(nc_psum(f"pp{i}", [128, 512], F32)) for i in range(4)]
                SGB = [ps.enter_context(nc_sbuf(f"sgb{i}", [128, T], BF16)) for i in range(2)]
                SGF = [ps.enter_context(nc_sbuf(f"sgf{i}", [128, T], F32)) for i in range(2)]
                SGT = [ps.enter_context(nc_sbuf(f"sgt{i}", [128, 16, 256], BF16)) for i in range(2)]
                pc = 0; sc = {"b": 0, "f": 0, "t": 0}
                nxt = load_w(win[0], 4096, ("wc", "win", l), (16, 256))
                for b in range(nblk):
                    W, wtok = nxt
                    if b + 1 < nblk:
                        nxt = load_w(win[b + 1], 4096, ("wc", "win", l), (16, 256))
                    mode, dt, dst, off = spec(b)
                    if mode == "fm":
                        for ci in range(2):
                            kk = "b" if dt == BF16 else "f"
                            si = sc[kk] % 2; sc[kk] += 1
                            stg = (SGB if dt == BF16 else SGF)[si]; stok = ("stg", kk, si)
                            for n in range(NB):
                                pt = PSB[pc % 4]; ptok = ("pp", pc % 4); pc += 1
                                for kc in range(NKC):
                                    mm(pt[:], W[:, kc, ci * 128:(ci + 1) * 128], XB[:, kc, n * BLK:(n + 1) * BLK],
                                       kc == 0, kc == NKC - 1, [wtok, ("XB", n)], [ptok], kc == NKC - 1)
                                evac(stg[:, n * BLK:(n + 1) * BLK], pt[:], [ptok], [stok])
                            r0 = off + ci * 128
                            S.dma("sp", dst[r0:r0 + 128, :], stg[:], [stok], [("scr", id(dst))], f"st{kk}{si}")
                    else:
                        si = sc["t"] % 2; sc["t"] += 1
                        stg = SGT[si]; stok = ("stg", "t", si)
                        for tt in range(16):
                            pt = PSB[pc % 4]; ptok = ("pp", pc % 4); pc += 1
                            for kc in range(NKC):
                                mm(pt[:, 0:256], XB[:, kc, tt * 128:(tt + 1) * 128], W[:, kc, :],
                                   kc == 0, kc == NKC - 1, [wtok, ("XB", tt // 4)], [ptok], kc == NKC - 1)
                            evac(stg[:, tt, :], pt[:, 0:256], [ptok], [stok])
                        S.dma("sp", dst.rearrange("(tt p) f -> p tt f", p=128)[:, :, off:off + 256], stg[:],
                              [stok], [("scr", id(dst))], f"stt{si}")
                S.barrier()

        def phase_proj_ln(l, src, nkc, wscr, wtokc, xres, gname, bname, final_out=None):
            with ExitStack() as ps:
                PSB = [ps.enter_context(nc_psum(f"lp{i}", [128, 512], F32)) for i in range(4)]
                PMEAN = ps.enter_context(nc_psum("lpm", [128, 512], F32))
                PMSQ = ps.enter_context(nc_psum("lpq", [128, 512], F32))
                SRC = ps.enter_context(nc_sbuf("lsrc", [128, nkc, BLK], BF16))
                ZB = ps.enter_context(nc_sbuf("lzb", [128, NKC, BLK], F32))
                XRS = [ps.enter_context(nc_sbuf(f"lxr{i}", [128, BLK], F32)) for i in range(3)]
                SQ = [ps.enter_context(nc_sbuf(f"lsq{i}", [128, BLK], F32)) for i in range(2)]
                TMP = [ps.enter_context(nc_sbuf(f"ltm{i}", [128, BLK], F32)) for i in range(2)]
                OST = [ps.enter_context(nc_sbuf(f"los{i}", [128, BLK], F32)) for i in range(2)]
                MEAN = ps.enter_context(nc_sbuf("lmean", [128, BLK], F32))
                M2 = ps.enter_context(nc_sbuf("lm2", [128, BLK], F32))
                RSTD = ps.enter_context(nc_sbuf("lrstd", [128, BLK], F32))
                pc = 0; xc = 0; oc = 0
                nel = nkc * 128
                for n in range(NB):
                    cols = slice(n * BLK, (n + 1) * BLK)
                    q4 = nkc // 4
                    for qi in range(4):
                        S.dma("sp", SRC[:, qi * q4:(qi + 1) * q4, :],
                              src.rearrange("(kc p) t -> p kc t", p=128)[:, qi * q4:(qi + 1) * q4, cols],
                              [("scr", id(src))], ["lsrc"], f"lsrc{qi}")
                    nxt = load_w(wscr[0], nel, wtokc, (nkc, 128))
                    for m in range(NKC):
                        W, wtok = nxt
                        if m + 1 < NKC:
                            nxt = load_w(wscr[m + 1], nel, wtokc, (nkc, 128))
                        xi = xc % 3; xc += 1
                        S.dma("sp", XRS[xi][:], xres[m * 128:(m + 1) * 128, cols], [("xr", n)], [("lxr", xi)], f"lxr{xi}")
                        pt = PSB[pc % 4]; ptok = ("lp", pc % 4); pc += 1
                        for kc in range(nkc):
                            mm(pt[:], W[:, kc, :], SRC[:, kc, :], kc == 0, kc == nkc - 1, [wtok, "lsrc"], [ptok], kc == nkc - 1)
                        S.op("dve", lambda e: e.scalar_tensor_tensor(out=ZB[:, m, :], in0=XRS[xi][:], scalar=float(ALPHA),
                                                                      in1=pt[:], op0=ALU.mult, op1=ALU.add),
                             [ptok, ("lxr", xi)], [("zb", m)])
                    for m in range(NKC):
                        si = m % 2
                        S.op("act", lambda e: e.activation(out=SQ[si][:], in_=ZB[:, m, :], func=AF.Square),
                             [("zb", m)], [("lsq", si)])
                        mm(PMEAN[:], ONES_D, ZB[:, m, :], m == 0, m == NKC - 1, [("zb", m), "CM"], ["lpm"], m == NKC - 1)
                        mm(PMSQ[:], ONES_D, SQ[si][:], m == 0, m == NKC - 1, [("lsq", si), "CM"], ["lpq"], True)
                    S.op("act", lambda e: e.activation(out=MEAN[:], in_=PMEAN[:], func=AF.Copy), ["lpm"], ["lmean"])
                    S.op("dve", lambda e: e.tensor_tensor(out=M2[:], in0=MEAN[:], in1=MEAN[:], op=ALU.mult), ["lmean"], ["lm2"])
                    S.op("dve", lambda e: e.tensor_tensor(out=M2[:], in0=PMSQ[:], in1=M2[:], op=ALU.subtract), ["lpq", "lm2"], ["lm2"])
                    S.op("act", lambda e: e.activation(out=M2[:], in_=M2[:], func=AF.Sqrt, bias=vcol("eps_ln"), scale=1.0),
                         ["lm2", "VEC"], ["lm2"])
                    S.op("dve", lambda e: e.reciprocal(out=RSTD[:], in_=M2[:]), ["lm2"], ["lrstd"])
                    for m in range(NKC):
                        ti = m % 2
                        S.op("dve", lambda e: e.tensor_tensor(out=TMP[ti][:], in0=ZB[:, m, :], in1=MEAN[:], op=ALU.subtract),
                             [("zb", m), "lmean"], [("ltm", ti)])
                        S.op("dve", lambda e: e.tensor_tensor(out=TMP[ti][:], in0=TMP[ti][:], in1=RSTD[:], op=ALU.mult),
                             [("ltm", ti), "lrstd"], [("ltm", ti)])
                        oi = oc % 2; oc += 1
                        S.op("act", lambda e: e.activation(out=OST[oi][:], in_=TMP[ti][:], func=AF.Identity,
                                                           scale=vcol(gname, l * 16 + m), bias=vcol(bname, l * 16 + m)),
                             [("ltm", ti), "VEC"], [("los", oi)])
                        if final_out is None:
                            S.op("act", lambda e: e.activation(out=XB[:, m, cols], in_=OST[oi][:], func=AF.Copy),
                                 [("los", oi)], [("XB", n)])
                            S.dma("sp", XR[m * 128:(m + 1) * 128, cols], OST[oi][:], [("los", oi)], [("xr", n)], f"los{oi}")
                        else:
                            S.dma("sp", final_out[m * 128:(m + 1) * 128, cols], OST[oi][:], [("los", oi)], [("out", n)], f"los{oi}")
                S.barrier()

        def phase_ffn1(l):
            wg, wu = wset(l)[2], wset(l)[3]
            with ExitStack() as ps:
                PG = [ps.enter_context(nc_psum(f"fg{i}", [128, 512], F32)) for i in range(2)]
                PU = [ps.enter_context(nc_psum(f"fu{i}", [128, 512], F32)) for i in range(2)]
                SGT = [ps.enter_context(nc_sbuf(f"fsg{i}", [128, BLK], F32)) for i in range(2)]
                HST = [ps.enter_context(nc_sbuf(f"fhs{i}", [128, T], BF16)) for i in range(2)]
                pc = 0; hc = 0
                for b in range(22):
                    Wg, gtok = load_w(wg[b], 4096, ("wc", "wg", l), (16, 256))
                    Wu, utok = load_w(wu[b], 4096, ("wc", "wg", l), (16, 256))
                    for ci in range(2):
                        hi = hc % 2; hc += 1
                        for n in range(NB):
                            pi = pc % 2; pc += 1
                            for kc in range(NKC):
                                mm(PG[pi][:], Wg[:, kc, ci * 128:(ci + 1) * 128], XB[:, kc, n * BLK:(n + 1) * BLK],
                                   kc == 0, kc == NKC - 1, [gtok, ("XB", n)], [("fg", pi)], kc == NKC - 1)
                            for kc in range(NKC):
                                mm(PU[pi][:], Wu[:, kc, ci * 128:(ci + 1) * 128], XB[:, kc, n * BLK:(n + 1) * BLK],
                                   kc == 0, kc == NKC - 1, [utok, ("XB", n)], [("fu", pi)], kc == NKC - 1)
                            S.op("act", lambda e: e.activation(out=SGT[pi][:], in_=PG[pi][:], func=AF.Silu),
                                 [("fg", pi)], [("fsg", pi)])
                            S.op("dve", lambda e: e.tensor_tensor(out=HST[hi][:, n * BLK:(n + 1) * BLK], in0=SGT[pi][:],
                                                                  in1=PU[pi][:], op=ALU.mult),
                                 [("fsg", pi), ("fu", pi)], [("fhs", hi)])
                        r0 = (b * 2 + ci) * 128
                        S.dma("sp", HT[r0:r0 + 128, :], HST[hi][:], [("fhs", hi)], [("scr", id(HT))], f"fhs{hi}")
                S.barrier()

        def phase_na(l):
            j = l // 2
            with ExitStack() as ps:
                PS_S = [ps.enter_context(nc_psum(f"nas{i}", [64, 512], F32)) for i in range(2)]
                PS_T = [ps.enter_context(nc_psum(f"nat{i}", [64, 1024], BF16))[:, 0:512].rearrange("p (a b) -> p a b", b=64) for i in range(2)]
                PS_O = [ps.enter_context(nc_psum(f"nao{i}", [128, 512], F32))[:, 0:64] for i in range(2)]
                QT = [ps.enter_context(nc_sbuf(f"naq{i}", [128, T], BF16)) for i in range(2)]
                KT = [ps.enter_context(nc_sbuf(f"nak{i}", [128, T], BF16)) for i in range(2)]
                VV = [ps.enter_context(nc_sbuf(f"nav{i}", [64, 32, 128], BF16)) for i in range(2)]
                BI = [ps.enter_context(nc_sbuf(f"nab{i}", [64, 8 * 512], F32)) for i in range(2)]
                SS = [ps.enter_context(nc_sbuf(f"nass{i}", [64, 512], F32)) for i in range(2)]
                PP = [ps.enter_context(nc_sbuf(f"napp{i}", [64, 512], BF16)) for i in range(2)]
                PN = [ps.enter_context(nc_sbuf(f"napn{i}", [64, 512], BF16)) for i in range(2)]
                PT = [ps.enter_context(nc_sbuf(f"napt{i}", [64, 512], BF16)) for i in range(2)]
                ST = [ps.enter_context(nc_sbuf(f"nast{i}", [64, 4], F32)) for i in range(2)]
                YO = [ps.enter_context(nc_sbuf(f"nayo{i}", [128, T], BF16)) for i in range(2)]
                it = 0
                for h in range(8):
                    hb = h % 2
                    S.dma("sp", QT[hb][:], AQK[h * 128:(h + 1) * 128, :], [("scr", id(AQK))], [("naq", hb)], f"naq{hb}")
                    S.dma("sp", KT[hb][:], AQK[1024 + h * 128:1024 + (h + 1) * 128, :], [("scr", id(AQK))], [("nak", hb)], f"nak{hb}")
                    S.dma("sp", VV[hb][:], AV.rearrange("(r p) f -> p r f", p=64)[:, :, h * 128:(h + 1) * 128],
                          [("scr", id(AV))], [("nav", hb)], f"nav{hb}")
                    S.dma("sp", BI[hb][:], na_bias[j, h], [], [("nab", hb)], f"nab{hb}")
                    for r in range(32):
                        i2 = it % 2; it += 1
                        r0 = min(max(r - 4, 0), 24)
                        var = r0 - r + 7
                        k0 = r0 * 64
                        S.op("pe", lambda e: e.matmul(PS_S[i2][:], QT[hb][:, r * 64:(r + 1) * 64], KT[hb][:, k0:k0 + 512],
                                                      start=True, stop=True),
                             [("naq", hb), ("nak", hb)], [("nas", i2)])
                        S.op("dve", lambda e: e.scalar_tensor_tensor(out=SS[i2][:], in0=PS_S[i2][:], scalar=float(128 ** -0.5),
                                                                      in1=BI[hb][:, var * 512:(var + 1) * 512],
                                                                      op0=ALU.mult, op1=ALU.add),
                             [("nas", i2), ("nab", hb)], [("nass", i2)])
                        S.op("dve", lambda e: e.tensor_reduce(out=ST[i2][:, 0:1], in_=SS[i2][:], axis=AX.X, op=ALU.max, negate=True),
                             [("nass", i2)], [("nast", i2)])
                        S.op("act", lambda e: e.activation(out=PP[i2][:], in_=SS[i2][:], func=AF.Exp, bias=ST[i2][:, 0:1],
                                                           scale=1.0, accum_out=ST[i2][:, 1:2]),
                             [("nass", i2), ("nast", i2)], [("napp", i2), ("nast2", i2)])
                        S.op("dve", lambda e: e.reciprocal(out=ST[i2][:, 2:3], in_=ST[i2][:, 1:2]), [("nast2", i2)], [("nast3", i2)])
                        S.op("dve", lambda e: e.tensor_scalar(out=PN[i2][:], in0=PP[i2][:], scalar1=ST[i2][:, 2:3], scalar2=None,
                                                              op0=ALU.mult),
                             [("napp", i2), ("nast3", i2)], [("napn", i2)])
                        for jj in range(8):
                            S.op("pe", lambda e: e.transpose(PS_T[i2][:, jj, :], PN[i2][:, jj * 64:(jj + 1) * 64], CB[0:64, 0:64]),
                                 [("napn", i2), "CB"], [("nat", i2)], inc=(jj == 7))
                        S.op("act", lambda e: e.activation(out=PT[i2][:], in_=PS_T[i2][:].rearrange("p a b -> p (a b)"), func=AF.Copy),
                             [("nat", i2)], [("napt", i2)])
                        for jj in range(8):
                            S.op("pe", lambda e: e.matmul(PS_O[i2][:], VV[hb][:, r0 + jj, :], PT[i2][:, jj * 64:(jj + 1) * 64],
                                                          start=(jj == 0), stop=(jj == 7)),
                                 [("nav", hb), ("napt", i2)], [("nao", i2)], inc=(jj == 7))
                        S.op("act", lambda e: e.activation(out=YO[hb][:, r * 64:(r + 1) * 64], in_=PS_O[i2][:], func=AF.Copy),
                             [("nao", i2)], [("nayo", hb)])
                    S.dma("sp", YT[h * 128:(h + 1) * 128, :], YO[hb][:], [("nayo", hb)], [("scr", id(YT))], f"nayo{hb}")
                S.barrier()

        def phase_hgrn(l):
            j = l // 2
            with ExitStack() as ps:
                P_A = [ps.enter_context(nc_psum(f"hga{i}", [64, 512], F32))[:, 0:64] for i in range(2)]
                P_K = [ps.enter_context(nc_psum(f"hgk{i}", [64, 1024], BF16))[:, 0:128] for i in range(2)]
                P_O = [ps.enter_context(nc_psum(f"hgo{i}", [128, 512], F32))[:, 0:64] for i in range(2)]
                P_SF = [ps.enter_context(nc_psum(f"hgs{i}", [128, 512], F32)) for i in range(2)]
                P_S = [x[:, 0:128] for x in P_SF]
                HQs = ps.enter_context(nc_sbuf("hq", [128, T], BF16))
                HGs = ps.enter_context(nc_sbuf("hg", [128, T], BF16))
                VVs = ps.enter_context(nc_sbuf("hv", [64, 32, 128], BF16))
                SQs = ps.enter_context(nc_sbuf("hsq", [128, T], F32))
                Z = ps.enter_context(nc_sbuf("hz", [128, T], F32))
                Bk = ps.enter_context(nc_sbuf("hbk", [128, T], F32))
                Cg = ps.enter_context(nc_sbuf("hcg", [128, T], F32))
                EA = [ps.enter_context(nc_sbuf(f"hea{d}", [128, T], F32)) for d in range(2)]
                QG = [WS[0][:, d * T:(d + 1) * T] for d in range(2)]
                KG = [WS[1][:, d * T:(d + 1) * T] for d in range(2)]
                KD = [ps.enter_context(nc_sbuf(f"hkd{d}", [128, T], BF16)) for d in range(2)]
                O32 = ps.enter_context(nc_sbuf("ho32", [128, T], F32))
                S32 = [ps.enter_context(nc_sbuf(f"hs32{d}", [128, 128], F32)) for d in range(2)]
                SBF = [ps.enter_context(nc_sbuf(f"hsbf{d}", [128, 128], BF16)) for d in range(2)]
                ATM = [ps.enter_context(nc_sbuf(f"hatm{d}", [64, 64], BF16)) for d in range(2)]
                KDT = [ps.enter_context(nc_sbuf(f"hkdt{d}", [64, 128], BF16)) for d in range(2)]
                LB = ps.enter_context(nc_sbuf("hlb", [128, 4], F32))
                SMK = ps.enter_context(nc_sbuf("hsmk", [128, T], BF16))
                YO = KD[0]
                RS = ps.enter_context(nc_sbuf("hrs", [128, BLK], F32))
                S.dma("sp", SMK[:], scanmask[:, :], [], ["hsmk"], "hsmk")
                for h in range(8):
                    S.dma("sp", HQs[:], HQ[h * 128:(h + 1) * 128, :], [("scr", id(HQ))], ["hq"], "hq")
                    S.dma("sp", HGs[:], HG[h * 128:(h + 1) * 128, :], [("scr", id(HG))], ["hg"], "hg")
                    S.dma("sp", VVs[:], HI.rearrange("(r p) f -> p r f", p=64)[:, :, h * 128:(h + 1) * 128],
                          [("scr", id(HI))], ["hv"], "hv")
                    if j == 0:
                        S.op("dve", lambda e: e.tensor_copy(out=LB[:, 0:1], in_=vcol("zero")), ["VEC"], ["hlb"])
                    else:
                        S.op("dve", lambda e: e.tensor_tensor(out=LB[:, 2:3], in0=vcol("lb_logit", 8 + h), in1=vcol("lb_logit", h),
                                                              op=ALU.subtract), ["VEC"], ["hlb2"])
                        S.op("act", lambda e: e.activation(out=LB[:, 0:1], in_=LB[:, 2:3], func=AF.Sigmoid), ["hlb2"], ["hlb"])
                    S.op("dve", lambda e: e.tensor_scalar(out=LB[:, 1:2], in0=LB[:, 0:1], scalar1=-1.0, scalar2=1.0,
                                                          op0=ALU.mult, op1=ALU.add), ["hlb"], ["hlb1"])
                    S.op("act", lambda e: e.activation(out=SQs[:], in_=HQs[:], func=AF.Silu), ["hq"], ["hsq"])
                    for d in range(2):
                        S.dma("sp", Z[:], HF[d * 1024 + h * 128: d * 1024 + (h + 1) * 128, :], [("scr", id(HF))], ["hz"], "hz")
                        S.op("act", lambda e: e.activation(out=EA[d][:], in_=Z[:], func=AF.Sigmoid), ["hz"], [("hea", d)])
                        S.op("dve", lambda e: e.tensor_scalar(out=EA[d][:], in0=EA[d][:], scalar1=LB[:, 1:2], scalar2=LB[:, 0:1],
                                                              op0=ALU.mult, op1=ALU.add), [("hea", d), "hlb", "hlb1"], [("hea", d)])
                        S.op("dve", lambda e: e.tensor_scalar(out=Bk[:], in0=EA[d][:], scalar1=-1.0, scalar2=1.0,
                                                              op0=ALU.mult, op1=ALU.add), [("hea", d)], ["hbk"])
                        S.op("act", lambda e: e.activation(out=EA[d][:], in_=EA[d][:], func=AF.Ln), [("hea", d)], [("hea", d)])
                        if d == 0:
                            S.op("dve", lambda e: e.tensor_tensor_scan(out=Cg[:], data0=SMK[:], data1=EA[d][:], initial=0.0,
                                                                       op0=ALU.mult, op1=ALU.add), [("hea", d), "hsmk"], ["hcg"])
                        else:
                            S.op("dve", lambda e: e.tensor_tensor_scan(out=Cg[:, ::-1], data0=SMK[:], data1=EA[d][:, ::-1], initial=0.0,
                                                                       op0=ALU.mult, op1=ALU.add), [("hea", d), "hsmk"], ["hcg"])
                        S.op("act", lambda e: e.activation(out=EA[d][:], in_=Cg[:], func=AF.Exp), ["hcg"], [("hea", d)])
                        S.op("act", lambda e: e.activation(out=Z[:], in_=Cg[:], func=AF.Exp, scale=-1.0), ["hcg"], ["hz"])
                        S.op("dve", lambda e: e.tensor_tensor(out=QG[d][:], in0=SQs[:], in1=EA[d][:], op=ALU.mult),
                             ["hsq", ("hea", d)], [("hqg", d)])
                        S.op("dve", lambda e: e.tensor_tensor(out=KG[d][:], in0=Bk[:], in1=Z[:], op=ALU.mult),
                             ["hbk", "hz"], [("hkg", d)])
                        e3 = EA[d][:].rearrange("p (c t) -> p c t", t=64)
                        eend = e3[:, :, 63:64] if d == 0 else e3[:, :, 0:1]
                        S.op("dve", lambda e: e.tensor_tensor(out=KD[d][:].rearrange("p (c t) -> p c t", t=64),
                                                              in0=KG[d][:].rearrange("p (c t) -> p c t", t=64),
                                                              in1=eend.to_broadcast([128, 32, 64]), op=ALU.mult),
                             [("hkg", d), ("hea", d)], [("hkd", d)])
                    for step in range(32):
                        for d in range(2):
                            c = step if d == 0 else 31 - step
                            tc_ = slice(c * 64, (c + 1) * 64)
                            eidx = c * 64 + 63 if d == 0 else c * 64
                            tri = TRIU if d == 0 else TRIL
                            S.op("pe", lambda e: e.matmul(P_A[d][:], KG[d][:, tc_], QG[d][:, tc_], start=True, stop=True),
                                 [("hkg", d), ("hqg", d)], [("hga", d)])
                            S.op("dve", lambda e: e.tensor_tensor(out=ATM[d][:], in0=P_A[d][:], in1=tri, op=ALU.mult),
                                 [("hga", d), "CM"], [("hatm", d)])
                            S.op("pe", lambda e: e.transpose(P_K[d][:], KD[d][:, tc_], CB[:, :]), [("hkd", d), "CB"], [("hgk", d)])
                            S.op("act", lambda e: e.activation(out=KDT[d][:], in_=P_K[d][:], func=AF.Copy), [("hgk", d)], [("hkdt", d)])
                            S.op("pe", lambda e: e.matmul(P_O[d][:], VVs[:, c, :], ATM[d][:], start=True, stop=(step == 0)),
                                 ["hv", ("hatm", d)], [("hgo", d)], inc=(step == 0))
                            if step > 0:
                                S.op("pe", lambda e: e.matmul(P_O[d][:], SBF[d][:], QG[d][:, tc_], start=False, stop=True),
                                     [("hsbf", d), ("hqg", d)], [("hgo", d)])
                            if step < 16:
                                S.op("act", lambda e: e.activation(out=O32[:, tc_], in_=P_O[d][:], func=AF.Copy),
                                     [("hgo", d)], [("ho32", c)])
                            else:
                                S.op("dve", lambda e: e.tensor_tensor(out=O32[:, tc_], in0=O32[:, tc_], in1=P_O[d][:], op=ALU.add),
                                     [("hgo", d), ("ho32", c)], [("ho32", c)])
                            if step < 31:
                                S.op("pe", lambda e: e.matmul(P_S[d][:], KDT[d][:], VVs[:, c, :], start=True, stop=True),
                                     [("hkdt", d), "hv"], [("hgs", d)])
                                if step == 0:
                                    S.op("dve", lambda e: e.tensor_copy(out=S32[d][:], in_=P_S[d][:]), [("hgs", d)], [("hs32", d)])
                                else:
                                    S.op("dve", lambda e: e.scalar_tensor_tensor(out=S32[d][:], in0=S32[d][:], scalar=EA[d][:, eidx:eidx + 1],
                                                                                  in1=P_S[d][:], op0=ALU.mult, op1=ALU.add),
                                         [("hgs", d), ("hs32", d), ("hea", d)], [("hs32", d)])
                                S.op("act", lambda e: e.activation(out=SBF[d][:], in_=S32[d][:], func=AF.Copy), [("hs32", d)], [("hsbf", d)])
                    allo = [("ho32", c) for c in range(32)]
                    S.op("act", lambda e: e.activation(out=Z[:], in_=O32[:], func=AF.Square), allo, ["hz"])
                    S.op("act", lambda e: e.activation(out=SQs[:], in_=HGs[:], func=AF.Silu), ["hg"], ["hsq"])
                    for n in range(NB):
                        cols = slice(n * BLK, (n + 1) * BLK)
                        S.op("pe", lambda e: e.matmul(P_SF[0][:], ONES_H, Z[:, cols], start=True, stop=True), ["hz", "CM"], [("hgs", 0)])
                        S.op("act", lambda e: e.activation(out=RS[:], in_=P_SF[0][:], func=AF.Sqrt, bias=vcol("eps_rms"), scale=1.0),
                             [("hgs", 0), "VEC"], ["hrs"])
                        S.op("dve", lambda e: e.reciprocal(out=RS[:], in_=RS[:]), ["hrs"], ["hrs"])
                        S.op("dve", lambda e: e.tensor_tensor(out=RS[:], in0=O32[:, cols], in1=RS[:], op=ALU.mult), allo + ["hrs"], ["hrs"])
                        S.op("dve", lambda e: e.scalar_tensor_tensor(out=YO[:, cols], in0=RS[:], scalar=vcol("hg_norm_w", j * 8 + h),
                                                                      in1=SQs[:, cols], op0=ALU.mult, op1=ALU.mult),
                             ["hrs", "hsq", "VEC"], [("hkd", 0)])
                    S.dma("sp", YT[1024 + h * 128:1024 + (h + 1) * 128, :], YO[:], [("hkd", 0)], [("scr", id(YT))], "hyo")
                S.barrier()

        def phase_pool(l):
            j = l // 2
            PADL = 8
            with ExitStack() as ps:
                PSB = [ps.enter_context(nc_psum(f"plp{i}", [128, 512], F32)) for i in range(2)]
                XP = [ps.enter_context(nc_sbuf(f"plx{i}", [128, T + 16], F32)) for i in range(2)]
                SA = ps.enter_context(nc_sbuf("plsa", [128, T + 16], F32))
                SBt = ps.enter_context(nc_sbuf("plsb", [128, T + 16], F32))
                PL = [ps.enter_context(nc_sbuf(f"plp{i}", [128, T], BF16)) for i in range(2)]
                PINV = ps.enter_context(nc_sbuf("plinv", [128, T], F32))
                WPs = ps.enter_context(nc_sbuf("plw", [128, 4, 2, 256], BF16))
                YO = [ps.enter_context(nc_sbuf(f"plyo{i}", [128, T], BF16)) for i in range(2)]
                S.dma("sp", WPs[:].rearrange("p g k d -> p (g k d)"), WPs_[j][:, :], [("wc", "wp", l)], ["plw"], "plw")
                for i in range(2):
                    S.op("dve", lambda e: e.memset(XP[i][:], 0.0), [], [("plx", i)])
                S.op("dve", lambda e: e.memset(SA[:], 0.0), [], ["plsa"])
                S.op("dve", lambda e: e.memset(SBt[:], 0.0), [], ["plsb"])
                yc = 0
                for g in range(4):
                    w = (2, 4, 8, 16)[g]
                    S.dma("sp", PINV[:], poolinv[:, g * T:(g + 1) * T], [], ["plinv"], "plinv")
                    for kc in range(2):
                        ch = g * 2 + kc
                        S.dma("sp", XP[kc][:, PADL:PADL + T], CX[ch * 128:(ch + 1) * 128, :], [("scr", id(CX))], [("plx", kc)], f"plx{kc}")
                        L = T + 16
                        cur, curtok, ln_ = XP[kc], ("plx", kc), L
                        bufs = [(SA, "plsa"), (SBt, "plsb")]
                        bi = 0; span = 1
                        while span < w:
                            dst, dtok = bufs[bi % 2]; bi += 1
                            nl = ln_ - span
                            S.op("dve", lambda e: e.tensor_tensor(out=dst[:, 0:nl], in0=cur[:, 0:nl], in1=cur[:, span:span + nl], op=ALU.add),
                                 [curtok], [dtok])
                            cur, curtok, ln_ = dst, dtok, nl
                            span *= 2
                        o0 = PADL - w // 2
                        dst, dtok = bufs[bi % 2]
                        S.op("dve", lambda e: e.tensor_tensor(out=dst[:, 0:T], in0=cur[:, o0:o0 + T], in1=PINV[:], op=ALU.mult),
                             [curtok, "plinv"], [dtok])
                        S.op("dve", lambda e: e.tensor_tensor(out=PL[kc][:], in0=dst[:, 0:T], in1=XP[kc][:, PADL:PADL + T], op=ALU.subtract),
                             [dtok, ("plx", kc)], [("plpl", kc)])
                    for dc in range(2):
                        yi = yc % 2; yc += 1
                        for n in range(NB):
                            pi = n % 2
                            for kc in range(2):
                                mm(PSB[pi][:], WPs[:, g, kc, dc * 128:(dc + 1) * 128], PL[kc][:, n * BLK:(n + 1) * BLK],
                                   kc == 0, kc == 1, ["plw", ("plpl", kc)], [("plps", pi)], kc == 1)
                            S.op("act", lambda e: e.activation(out=YO[yi][:, n * BLK:(n + 1) * BLK], in_=PSB[pi][:], func=AF.Identity,
                                                               scale=vcol("pool_scale", j * 8 + g * 2 + dc)),
                                 [("plps", pi), "VEC"], [("plyo", yi)])
                        r0 = g * 256 + dc * 128
                        S.dma("sp", YT[r0:r0 + 128, :], YO[yi][:], [("plyo", yi)], [("scr", id(YT))], f"plyo{yi}")
                S.barrier()

        def phase_gqa(l):
            j = l // 2
            scale = float(128 ** -0.5)
            with ExitStack() as ps:
                PS_S = ps.enter_context(nc_psum("gqs", [128, T], F32))
                PS_T = ps.enter_context(nc_psum("gqt", [128, 2048], BF16)).rearrange("p (a b) -> p a b", b=128)
                PS_O = ps.enter_context(nc_psum("gqo", [128, 512], F32))
                PS_X = ps.enter_context(nc_psum("gqx", [128, 512], F32))
                QTs = ps.enter_context(nc_sbuf("gq", [128, 8, T], BF16))
                KTs = ps.enter_context(nc_sbuf("gk", [128, 2, T], BF16))
                VVs = ps.enter_context(nc_sbuf("gv", [128, 16, 256], BF16))
                RC = ps.enter_context(nc_sbuf("grc", [128, T], F32))
                RSn = ps.enter_context(nc_sbuf("grs", [128, T], F32))
                ZZ = [ps.enter_context(nc_sbuf(f"gz{i}", [128, BLK], F32)) for i in range(2)]
                T1 = ps.enter_context(nc_sbuf("gt1", [128, BLK], F32))
                T2 = ps.enter_context(nc_sbuf("gt2", [128, BLK], F32))
                T3 = ps.enter_context(nc_sbuf("gt3", [128, BLK], F32))
                PB = [ps.enter_context(nc_sbuf(f"gp{i}", [128, T], BF16)) for i in range(2)]
                PTs = [ps.enter_context(nc_sbuf(f"gpt{i}", [128, 16, 128], BF16)) for i in range(2)]
                STt = [ps.enter_context(nc_sbuf(f"gst{i}", [128, 4], F32)) for i in range(2)]
                OB = [ps.enter_context(nc_sbuf(f"gob{i}", [128, 128], BF16)) for i in range(2)]
                YO = [ps.enter_context(nc_sbuf(f"gyo{i}", [128, T], BF16)) for i in range(2)]
                S.dma("sp", RC[:], ropeC[:, :], [], ["grc"], "grc")
                S.dma("sp", RSn[:], ropeS[:, :], [], ["grs"], "grs")
                S.dma("sp", VVs[:], DV.rearrange("(tt p) f -> p tt f", p=128), [("scr", id(DV))], ["gv"], "gv")
                zc = 0
                for hh in range(10):
                    gcol = vcol("q_norm", j) if hh < 8 else vcol("k_norm", j)
                    for n in range(NB):
                        cols = slice(n * BLK, (n + 1) * BLK)
                        zi = zc % 2; zc += 1
                        S.dma("sp", ZZ[zi][:], DQK[hh * 128:(hh + 1) * 128, cols], [("scr", id(DQK))], [("gz", zi)], f"gz{zi}")
                        S.op("act", lambda e: e.activation(out=T1[:], in_=ZZ[zi][:], func=AF.Square), [("gz", zi)], ["gt1"])
                        S.op("pe", lambda e: e.matmul(PS_O[:], ONES_H, T1[:], start=True, stop=True), ["gt1", "CM"], ["gqo"])
                        S.op("act", lambda e: e.activation(out=T2[:], in_=PS_O[:], func=AF.Sqrt, bias=vcol("eps_rms"), scale=1.0),
                             ["gqo", "VEC"], ["gt2"])
                        S.op("dve", lambda e: e.reciprocal(out=T2[:], in_=T2[:]), ["gt2"], ["gt2"])
                        S.op("dve", lambda e: e.scalar_tensor_tensor(out=T1[:], in0=ZZ[zi][:], scalar=gcol, in1=T2[:],
                                                                      op0=ALU.mult, op1=ALU.mult), [("gz", zi), "gt2", "VEC"], ["gt1"])
                        S.op("pe", lambda e: e.matmul(PS_X[:], ROTP, T1[:], start=True, stop=True), ["gt1", "CM"], ["gqx"])
                        S.op("dve", lambda e: e.tensor_tensor(out=T3[:], in0=PS_X[:], in1=RSn[:, cols], op=ALU.mult), ["gqx", "grs"], ["gt3"])
                        S.op("dve", lambda e: e.tensor_tensor(out=T2[:], in0=T1[:], in1=RC[:, cols], op=ALU.mult), ["gt1", "grc"], ["gt2"])
                        dst = QTs[:, hh, cols] if hh < 8 else KTs[:, hh - 8, cols]
                        S.op("dve", lambda e: e.tensor_tensor(out=dst, in0=T2[:], in1=T3[:], op=ALU.add), ["gt2", "gt3"], ["gqk"])
                it = 0
                for hq in range(8):
                    kv = hq // 4
                    yb = hq % 2
                    for qt in range(16):
                        i2 = it % 2; it += 1
                        qs = slice(qt * 128, (qt + 1) * 128)
                        for kb in range(4):
                            S.op("pe", lambda e: e.matmul(PS_S[:, kb * 512:(kb + 1) * 512], QTs[:, hq, qs], KTs[:, kv, kb * 512:(kb + 1) * 512],
                                                          start=True, stop=True), ["gqk"], ["gqs"], inc=(kb == 3))
                        S.op("dve", lambda e: e.tensor_reduce(out=STt[i2][:, 0:1], in_=PS_S[:], axis=AX.X, op=ALU.max, negate=True),
                             ["gqs"], [("gst", i2)])
                        S.op("dve", lambda e: e.tensor_scalar(out=STt[i2][:, 1:2], in0=STt[i2][:, 0:1], scalar1=scale, scalar2=None, op0=ALU.mult),
                             [("gst", i2)], [("gst1", i2)])
                        S.op("act", lambda e: e.activation(out=PB[i2][:], in_=PS_S[:], func=AF.Exp, bias=STt[i2][:, 1:2], scale=scale,
                                                           accum_out=STt[i2][:, 2:3]),
                             ["gqs", ("gst1", i2)], [("gp", i2), ("gst2", i2)])
                        S.op("dve", lambda e: e.reciprocal(out=STt[i2][:, 3:4], in_=STt[i2][:, 2:3]), [("gst2", i2)], [("gst3", i2)])
                        for kc in range(16):
                            S.op("pe", lambda e: e.transpose(PS_T[:, kc, :], PB[i2][:, kc * 128:(kc + 1) * 128], CB[:, :]),
                                 [("gp", i2), "CB"], ["gqt"], inc=(kc == 15))
                        S.op("act", lambda e: e.activation(out=PTs[i2][:, 0:8, :], in_=PS_T[:, 0:8, :], func=AF.Copy), ["gqt"], [("gpt", i2)])
                        S.op("dve", lambda e: e.tensor_copy(out=PTs[i2][:, 8:16, :], in_=PS_T[:, 8:16, :]), ["gqt"], [("gpt", i2)])
                        for kc in range(16):
                            S.op("pe", lambda e: e.matmul(PS_O[:, 0:128], PTs[i2][:, kc, :], VVs[:, kc, kv * 128:(kv + 1) * 128],
                                                          start=(kc == 0), stop=(kc == 15)), [("gpt", i2), "gv"], ["gqo"], inc=(kc == 15))
                        S.op("act", lambda e: e.activation(out=OB[i2][:], in_=PS_O[:, 0:128], func=AF.Identity, scale=STt[i2][:, 3:4]),
                             ["gqo", ("gst3", i2)], [("gob", i2)])
                        S.op("pe", lambda e: e.transpose(PS_X[:, 0:64].bitcast(BF16), OB[i2][:], CB[:, :]), [("gob", i2), "CB"], ["gqx"])
                        S.op("dve", lambda e: e.tensor_copy(out=YO[yb][:, qs], in_=PS_X[:, 0:64].bitcast(BF16)), ["gqx"], [("gyo", yb)])
                    S.dma("sp", YT[1024 + hq * 128:1024 + (hq + 1) * 128, :], YO[yb][:], [("gyo", yb)], [("scr", id(YT))], f"gyo{yb}")
                S.barrier()

        with ExitStack() as ps:
            XS = [ps.enter_context(nc_sbuf(f"ixs{i}", [128, T], F32)) for i in range(2)]
            for m in range(NKC):
                i2 = m % 2
                S.dma("sp", XS[i2][:], xT[m * 128:(m + 1) * 128, :], [], [("ixs", i2)], f"ixs{i2}")
                evac(XB[:, m, :], XS[i2][:], [("ixs", i2)], [("XB", n) for n in range(NB)])
            S.barrier()

        def ab_spec(b):
            if b < 8: return ("fm", BF16, AQK, b * 256)
            if b < 12: return ("tm", BF16, AV, (b - 8) * 256)
            if b < 16: return ("fm", BF16, HQ, (b - 12) * 256)
            if b < 24: return ("fm", F32, HF, (b - 16) * 256)
            if b < 28: return ("tm", BF16, HI, (b - 24) * 256)
            return ("fm", BF16, HG, (b - 28) * 256)

        def cd_spec(b):
            if b < 4: return ("fm", F32, CX, b * 256)
            if b < 9: return ("fm", F32, DQK, (b - 4) * 256)
            return ("tm", BF16, DV, 0)

        for l in range(n_layers):
            if l > 0:
                S.new_epoch()
            xres = xT if l == 0 else XR
            wout, wd = wset(l)[1], wset(l)[4]
            if l % 2 == 0:
                phase_proj(l, 32, ab_spec)
                if stop_after == ("proj", l): break
                phase_na(l)
                if stop_after == ("na", l): break
                phase_hgrn(l)
                if stop_after == ("hgrn", l): break
            else:
                phase_proj(l, 10, cd_spec)
                if stop_after == ("proj", l): break
                phase_pool(l)
                if stop_after == ("pool", l): break
                phase_gqa(l)
                if stop_after == ("gqa", l): break
            phase_proj_ln(l, YT, 16, wout, ("wc", "wout", l), xres, "ln_mix_g", "ln_mix_b")
            if stop_after == ("mixln", l): break
            phase_ffn1(l)
            if stop_after == ("ffn1", l): break
            last = (l == n_layers - 1)
            phase_proj_ln(l, HT, 44, wd, ("wc", "wd", l), XR, "ln_ffn_g", "ln_ffn_b", final_out=outT if last else None)

        S.barrier()
        fin = {k: v for k, v in S.latest.items()}
        S._wait(S.eng["sp"], fin)
    return nc


def _consts():
    cmat = np.zeros((128, 4 * 128 + 64), np.float32)
    cmat[:, 0:128] = 1.0 / 2048.0
    cmat[:, 128:256] = 1.0 / 128.0
    P = np.zeros((128, 128), np.float32)
    for i in range(64):
        P[2 * i + 1, 2 * i] = -1.0
        P[2 * i, 2 * i + 1] = 1.0
    cmat[:, 256:384] = P
    s = np.arange(64)[:, None]; t = np.arange(64)[None, :]
    cmat[0:64, 384:448] = (s <= t).astype(np.float32)
    cmat[0:64, 448:512] = (s >= t).astype(np.float32)
    ident = np.eye(128, dtype=np.float32).astype(ml_dtypes.bfloat16)
    smask = np.ones((128, T), np.float32); smask[:, ::64] = 0.0
    pos = np.arange(T)
    row = (pos // 64).astype(np.float32); col = (pos % 64).astype(np.float32)
    inv = (np.float32(10000.0) ** (-np.arange(32, dtype=np.float32) / np.float32(32))).astype(np.float32)
    ang = np.concatenate([row[:, None] * inv, col[:, None] * inv], axis=-1).astype(np.float32)
    cos = np.cos(ang).astype(np.float32); sin = np.sin(ang).astype(np.float32)
    ropeC = np.repeat(cos.T, 2, axis=0).astype(np.float32)
    ropeS = np.repeat(sin.T, 2, axis=0).astype(np.float32)
    pinv = np.zeros((4, T), np.float32)
    for gi, w in enumerate((2, 4, 8, 16)):
        lo = np.clip(pos - w // 2, 0, T); hi = np.clip(pos + w // 2, 0, T)
        pinv[gi] = 1.0 / (hi - lo).astype(np.float32)
    poolinv = np.broadcast_to(pinv.reshape(1, 4 * T), (128, 4 * T)).copy()
    return cmat, ident, smask, ropeC, ropeS, poolinv


def _na_bias(rpb):
    col = np.arange(64)
    cs = np.clip(col - 8, 0, 48)
    cmask = (col[None, :] >= cs[:, None]) & (col[None, :] < cs[:, None] + 16)
    cidx = np.clip(col[None, :] - col[:, None] + 15, 0, 30)
    out = np.full((2, 8, 64, 8, 8, 64), NEG, np.float32)
    for var in range(8):
        for jj in range(8):
            ridx = var + jj
            if ridx < 0 or ridx > 14:
                continue
            g = rpb[:, :, ridx, :][:, :, cidx]
            out[:, :, :, var, jj, :] = np.where(cmask[None, None], g, np.float32(NEG))
    return out.reshape(2, 8, 64, 8 * 512)


def _vecs(inp):
    v = np.zeros((128, NV), np.float32)

    def put(name, arr, off=0):
        a = np.asarray(arr, np.float32).reshape(-1, 128).T
        v[:, VC[name] + off: VC[name] + off + a.shape[1]] = a
    for l in range(4):
        put("ln_mix_g", inp["ln_mix_g"][l], l * 16); put("ln_mix_b", inp["ln_mix_b"][l], l * 16)
        put("ln_ffn_g", inp["ln_ffn_g"][l], l * 16); put("ln_ffn_b", inp["ln_ffn_b"][l], l * 16)
    for j in range(2):
        put("lb_logit", inp["hg_lb_logits"][j], j * 8)
        put("hg_norm_w", inp["hg_norm_w"][j], j * 8)
        put("pool_scale", inp["pool_scale"][j], j * 8)
        put("q_norm", inp["d_q_norm"][j], j); put("k_norm", inp["d_k_norm"][j], j)
    v[:, VC["eps_ln"]] = LN_EPS; v[:, VC["eps_rms"]] = RMS_EPS; v[:, VC["zero"]] = 0.0; v[:, VC["one"]] = 1.0
    return v


def make_in_maps(inp, cores):
    cmat, ident, smask, ropeC, ropeS, poolinv = _consts()
    f = lambda k: np.ascontiguousarray(np.asarray(inp[k], np.float32))
    shared = {
        "ab_w_in": f("ab_w_in"), "ab_w_out": f("ab_w_out"), "cd_w_in": f("cd_w_in"), "cd_w_out": f("cd_w_out"),
        "pool_w": f("pool_w"), "ffn_w_gate": f("ffn_w_gate"), "ffn_w_up": f("ffn_w_up"), "ffn_w_down": f("ffn_w_down"),
        "vecs": _vecs(inp), "na_bias": _na_bias(np.asarray(inp["na_rpb"], np.float32)),
        "cm": cmat, "cb": ident, "scanmask": smask.astype(ml_dtypes.bfloat16), "ropeC": ropeC, "ropeS": ropeS, "poolinv": poolinv,
    }
    x = np.asarray(inp["x"], np.float32)
    maps = []
    for c in cores:
        m = dict(shared); m["xT"] = np.ascontiguousarray(x[c].T)
        maps.append(m)
    return maps


def kernel(**inputs):
    nc = build_program()
    in_maps = make_in_maps(inputs, list(range(8)))
    res = run_bass_kernel_spmd(nc, in_maps, core_ids=list(range(8)))
    out = np.stack([np.ascontiguousarray(r["outT"].T) for r in res.results], axis=0)
    return out.astype(np.float32)
```

```python
import numpy as np
import ml_dtypes
from contextlib import ExitStack
import concourse.bass as bass
import concourse.mybir as mybir
from concourse.bass_utils import run_bass_kernel_spmd

F32, BF16 = mybir.dt.float32, mybir.dt.bfloat16
AF = mybir.ActivationFunctionType
ALU = mybir.AluOpType
AX = mybir.AxisListType

T = 2048; D = 2048; HID = 5632; NKC = 16; NB = 4; BLK = 512
DEPTH = 4
ALPHA = (2 * DEPTH) ** 0.25
LN_EPS = 1e-5; RMS_EPS = 1e-6
NEG = -30000.0
GQA_PIPE = False

VC = {}
_c = 0
for _n, _w in (("ln_mix_g", 64), ("ln_mix_b", 64), ("ln_ffn_g", 64), ("ln_ffn_b", 64),
               ("lb_logit", 16), ("hg_norm_w", 16), ("pool_scale", 16), ("q_norm", 2), ("k_norm", 2),
               ("eps_ln", 1), ("eps_rms", 1), ("zero", 1), ("one", 1)):
    VC[_n] = _c; _c += _w
NV = _c


class Sched:
    def __init__(self, nc, es):
        self.nc = nc; self.es = es
        self.eng = {}
        for name, h in (("pe", nc.tensor), ("act", nc.scalar), ("dve", nc.vector), ("sp", nc.sync), ("pool", nc.gpsimd)):
            self.eng[name] = dict(h=h, sem=None, cnt=0, waited={}, name=name)
        self.sems = {}; self.latest = {}; self.tok = {}; self.nobar = set()
        self.epoch = -1
        self.new_epoch()

    def _newsem(self, name):
        s = self.es.enter_context(self.nc.semaphore(name))
        self.sems[name] = s; self.latest[name] = 0
        return name

    def new_epoch(self):
        self.epoch += 1
        for n in ("pe", "act", "dve"):
            e = self.eng[n]; e["sem"] = self._newsem(f"e{self.epoch}_{n}"); e["cnt"] = 0

    def _wait(self, e, needs):
        for k, v in needs.items():
            if v <= 0 or e["waited"].get(k, 0) >= v:
                continue
            e["h"].wait_ge(self.sems[k], v); e["waited"][k] = v

    def _needs(self, reads, writes, own, skip_own_raw):
        needs = {}
        for t in reads:
            w, r = self.tok.setdefault(t, ({}, {}))
            for k, v in w.items():
                if k == own and skip_own_raw:
                    continue
                if needs.get(k, 0) < v: needs[k] = v
        for t in writes:
            w, r = self.tok.setdefault(t, ({}, {}))
            for d in (w, r):
                for k, v in d.items():
                    if k == own:
                        continue
                    if needs.get(k, 0) < v: needs[k] = v
        return needs

    def op(self, ename, fn, reads=(), writes=(), inc=True):
        e = self.eng[ename]; own = e["sem"]
        self._wait(e, self._needs(reads, writes, own, ename == "pe"))
        ins = fn(e["h"])
        if inc:
            e["cnt"] += 1
            ins.then_inc(self.sems[own], 1)
            self.latest[own] = e["cnt"]
            val = e["cnt"]
        else:
            val = e["cnt"] + 1
        for t in reads: self.tok[t][1][own] = val
        for t in writes: self.tok[t][0][own] = val
        return ins

    def dma(self, qname, out, in_, reads, writes, key, nobar=False, wval=None):
        e = self.eng[qname]
        if key not in self.sems:
            self._newsem(key)
            if nobar: self.nobar.add(key)
        self._wait(e, self._needs(reads, writes, None, False))
        ins = e["h"].dma_start(out=out, in_=in_)
        self.latest[key] += 16
        ins.then_inc(self.sems[key], 16)
        v = self.latest[key]
        for t in reads: self.tok.setdefault(t, ({}, {}))[1][key] = v
        for t in writes: self.tok.setdefault(t, ({}, {}))[0][key] = v
        return v

    def barrier(self):
        need = {k: v for k, v in self.latest.items() if k not in self.nobar}
        for n in ("pe", "act", "dve", "sp"):
            self._wait(self.eng[n], need)


def build_program(debug=(), n_layers=DEPTH, stop_after=None):
    nc = bass.Bass("TRN2", target_bir_lowering=False)
    dbg = set(debug)

    def din(name, shape, dt=F32):
        return nc.dram_tensor(name, list(shape), dt, kind="ExternalInput").ap()

    def dscr(name, shape, dt):
        kind = "ExternalOutput" if name in dbg else "Internal"
        return nc.dram_tensor(name, list(shape), dt, kind=kind).ap()

    xT = din("xT", [D, T])
    ab_w_in = din("ab_w_in", [2, D, 8192]); ab_w_out = din("ab_w_out", [2, D, D])
    cd_w_in = din("cd_w_in", [2, D, 2560]); cd_w_out = din("cd_w_out", [2, D, D])
    pool_w = din("pool_w", [2, 4, 256, 256])
    w_gate = din("ffn_w_gate", [4, D, HID]); w_up = din("ffn_w_up", [4, D, HID]); w_down = din("ffn_w_down", [4, HID, D])
    vecs = din("vecs", [128, NV])
    na_bias = din("na_bias", [2, 8, 64, 8 * 512])
    cm = din("cm", [128, 4 * 128 + 64])
    cb = din("cb", [128, 128], BF16)
    scanmask = din("scanmask", [128, T], BF16)
    ropeC = din("ropeC", [128, T]); ropeS = din("ropeS", [128, T])
    poolinv = din("poolinv", [128, 4 * T])
    outT = nc.dram_tensor("outT", [D, T], F32, kind="ExternalOutput").ap()

    XR = dscr("XR", [D, T], F32)
    WINs = [dscr(f"WIN{l}", [32 if l % 2 == 0 else 10, 128, 16 * 256], BF16) for l in range(4)]
    WOUTs = [dscr(f"WOUT{l}", [16, 128, 16 * 128], BF16) for l in range(4)]
    WGs = [dscr(f"WG{l}", [22, 128, 16 * 256], BF16) for l in range(4)]
    WUs = [dscr(f"WU{l}", [22, 128, 16 * 256], BF16) for l in range(4)]
    WDs = [dscr(f"WD{l}", [16, 128, 44 * 128], BF16) for l in range(4)]
    WPs_ = [dscr(f"WP{j}", [128, 4 * 2 * 256], BF16) for j in range(2)]
    AQK = dscr("AQK", [2048, T], BF16)
    AV = dscr("AV", [T, 1024], BF16)
    HQ = dscr("HQ", [1024, T], BF16)
    HF = dscr("HF", [2048, T], F32)
    HI = dscr("HI", [T, 1024], BF16)
    HG = dscr("HG", [1024, T], BF16)
    CX = dscr("CX", [1024, T], F32)
    DQK = dscr("DQK", [1280, T], F32)
    DV = dscr("DV", [T, 256], BF16)
    YT = dscr("YT", [2048, T], BF16)
    HT = dscr("HT", [HID, T], BF16)

    with ExitStack() as es:
        S = Sched(nc, es)
        block = es.enter_context(nc.Block())

        def sb(name, shape, dt):
            return es.enter_context(nc.sbuf_tensor(name, list(shape), dt))

        _uid = [0]

        def nc_sbuf(name, shape, dt):
            _uid[0] += 1
            return nc.sbuf_tensor(f"{name}_u{_uid[0]}", shape, dt)

        def nc_psum(name, shape, dt):
            _uid[0] += 1
            return nc.psum_tensor(f"{name}_u{_uid[0]}", shape, dt)

        XB = sb("XB", [128, NKC, T], BF16)
        VEC = sb("VEC", [128, NV], F32)
        CM = sb("CM", [128, 4 * 128 + 64], F32)
        CB = sb("CB", [128, 128], BF16)
        WS = [sb(f"WS{i}", [128, 5632], BF16) for i in range(3)]
        ws_ctr = [0]

        S.dma("sp", VEC[:], vecs[:, :], [], ["VEC"], "c_vec")
        S.dma("sp", CM[:], cm[:, :], [], ["CM"], "c_cm")
        S.dma("sp", CB[:], cb[:, :], [], ["CB"], "c_cb")
        ONES_D = CM[:, 0:128]; ONES_H = CM[:, 128:256]; ROTP = CM[:, 256:384]
        TRIU = CM[0:64, 384:448]; TRIL = CM[0:64, 448:512]

        def vcol(name, i=0, n=1):
            return VEC[:, VC[name] + i: VC[name] + i + n]

        cast_jobs = []

        def cj(tokn, out, in_, nd):
            cast_jobs.append((tokn, out, in_, nd))

        def wset(l):
            return (WINs[l], WOUTs[l], WGs[l], WUs[l], WDs[l])

        for l in range(n_layers):
            j = l // 2
            win, wout, wg, wu, wd = wset(l)
            if l % 2 == 0:
                src = ab_w_in[j].rearrange("(kc p) (b c) -> b p kc c", p=128, c=256)
                for b in range(32):
                    cj(("win", l, b), win[b].rearrange("p (kc c) -> p kc c", c=256), src[b], 128)
                so = ab_w_out[j].rearrange("(kc p) (m c) -> m p kc c", p=128, c=128)
            else:
                src = cd_w_in[j].rearrange("(kc p) (b c) -> b p kc c", p=128, c=256)
                for b in range(10):
                    cj(("win", l, b), win[b].rearrange("p (kc c) -> p kc c", c=256), src[b], 128)
                cj(("wp", l), WPs_[j][:].rearrange("p (g kc d) -> p g kc d", g=4, kc=2),
                   pool_w[j].rearrange("g (kc p) d -> p g kc d", p=128), 64)
                so = cd_w_out[j].rearrange("(kc p) (m c) -> m p kc c", p=128, c=128)
            for m in range(16):
                cj(("wout", l), wout[m].rearrange("p (kc c) -> p kc c", c=128), so[m], 128)
            sg = w_gate[l].rearrange("(kc p) (b c) -> b p kc c", p=128, c=256)
            su = w_up[l].rearrange("(kc p) (b c) -> b p kc c", p=128, c=256)
            for b in range(22):
                cj(("wg", l), wg[b].rearrange("p (kc c) -> p kc c", c=256), sg[b], 128)
                cj(("wg", l), wu[b].rearrange("p (kc c) -> p kc c", c=256), su[b], 128)
            sd = w_down[l].rearrange("(kc p) (m c) -> m p kc c", p=128, c=128)
            for m in range(16):
                for hh in range(2):
                    cj(("wd", l), wd[m].rearrange("p (kc c) -> p kc c", c=128)[:, hh * 22:(hh + 1) * 22, :],
                       sd[m][:, hh * 22:(hh + 1) * 22, :], 176)
        S._newsem("cast"); S.nobar.add("cast")
        pool = S.eng["pool"]
        inflight = []; tot = 0
        ends = {}
        for i, (tokn, out, in_, nd) in enumerate(cast_jobs):
            while inflight and sum(x[1] for x in inflight) + nd > 850:
                v0, _ = inflight.pop(0)
                pool["h"].wait_ge(S.sems["cast"], v0)
            ins = None
            ins = pool["h"].dma_start(out=out, in_=in_)
            tot += 16
            ins.then_inc(S.sems["cast"], 16)
            inflight.append((tot, nd))
            ends[tokn] = tot
        S.latest["cast"] = tot
        cast_total = tot
        for tokn, v in ends.items():
            vv = min(cast_total, v + 16 * 8)
            S.tok.setdefault(("wc",) + tokn, ({}, {}))[0]["cast"] = vv

        def load_w(scr_ap, nel, tokc, shape3):
            i = ws_ctr[0] % 3; ws_ctr[0] += 1
            S.dma("sp", WS[i][:, 0:nel], scr_ap, [tokc], [("WS", i)], f"ws{i}")
            kc, c = shape3
            return WS[i][:, 0:nel].rearrange("p (kc c) -> p kc c", c=c), ("WS", i)

        evac_ctr = [0]

        def evac(out, in_, reads, writes, eng=None):
            if eng is None:
                eng = "act" if evac_ctr[0] % 2 == 0 else "dve"; evac_ctr[0] += 1
            if eng == "act":
                S.op("act", lambda e: e.activation(out=out, in_=in_, func=AF.Copy), reads, writes)
            else:
                S.op("dve", lambda e: e.tensor_copy(out=out, in_=in_), reads, writes)

        def mm(out, lhsT, rhs, start, stop, reads, writes, inc):
            S.op("pe", lambda e: e.matmul(out, lhsT, rhs, start=start, stop=stop), reads, writes, inc=inc)

        def phase_proj(l, nblk, spec):
            win = wset(l)[0]
            with ExitStack() as ps:
                PSB = [ps.enter_context(nc_psum(f"pp{i}", [128, 512], F32)) for i in range(4)]
                SGB = [ps.enter_context(nc_sbuf(f"sgb{i}", [128, T], BF16)) for i in range(2)]
                SGF = [ps.enter_context(nc_sbuf(f"sgf{i}", [128, T], F32)) for i in range(2)]
                SGT = [ps.enter_context(nc_sbuf(f"sgt{i}", [128, 16, 256], BF16)) for i in range(2)]
                pc = 0; sc = {"b": 0, "f": 0, "t": 0}
                nxt = load_w(win[0], 4096, ("wc", "win", l, 0), (16, 256))
                for b in range(nblk):
                    W, wtok = nxt
                    if b + 1 < nblk:
                        nxt = load_w(win[b + 1], 4096, ("wc", "win", l, b + 1), (16, 256))
                    mode, dt, dst, off = spec(b)
                    if mode == "fm":
                        for ci in range(2):
                            kk = "b" if dt == BF16 else "f"
                            si = sc[kk] % 2; sc[kk] += 1
                            stg = (SGB if dt == BF16 else SGF)[si]; stok = ("stg", kk, si)
                            for n in range(NB):
                                pt = PSB[pc % 4]; ptok = ("pp", pc % 4); pc += 1
                                for kc in range(NKC):
                                    mm(pt[:], W[:, kc, ci * 128:(ci + 1) * 128], XB[:, kc, n * BLK:(n + 1) * BLK],
                                       kc == 0, kc == NKC - 1, [wtok, ("XB", n)], [ptok], kc == NKC - 1)
                                evac(stg[:, n * BLK:(n + 1) * BLK], pt[:], [ptok], [stok])
                            r0 = off + ci * 128
                            S.dma("sp", dst[r0:r0 + 128, :], stg[:], [stok], [("scr", id(dst))], f"st{kk}{si}")
                    else:
                        si = sc["t"] % 2; sc["t"] += 1
                        stg = SGT[si]; stok = ("stg", "t", si)
                        for tt in range(16):
                            pt = PSB[pc % 4]; ptok = ("pp", pc % 4); pc += 1
                            for kc in range(NKC):
                                mm(pt[:, 0:256], XB[:, kc, tt * 128:(tt + 1) * 128], W[:, kc, :],
                                   kc == 0, kc == NKC - 1, [wtok, ("XB", tt // 4)], [ptok], kc == NKC - 1)
                            evac(stg[:, tt, :], pt[:, 0:256], [ptok], [stok])
                        S.dma("sp", dst.rearrange("(tt p) f -> p tt f", p=128)[:, :, off:off + 256], stg[:],
                              [stok], [("scr", id(dst))], f"stt{si}")
                S.barrier()

        def phase_proj_ln(l, src, nkc, wscr, wtokc, xres, gname, bname, final_out=None):
            with ExitStack() as ps:
                PSB = [ps.enter_context(nc_psum(f"lp{i}", [128, 512], F32)) for i in range(4)]
                PMEAN = ps.enter_context(nc_psum("lpm", [128, 512], F32))
                PMSQ = ps.enter_context(nc_psum("lpq", [128, 512], F32))
                SRC = ps.enter_context(nc_sbuf("lsrc", [128, nkc, BLK], BF16))
                ZB = ps.enter_context(nc_sbuf("lzb", [128, NKC, BLK], F32))
                XRS = [ps.enter_context(nc_sbuf(f"lxr{i}", [128, BLK], F32)) for i in range(3)]
                SQ = [ps.enter_context(nc_sbuf(f"lsq{i}", [128, BLK], F32)) for i in range(2)]
                TMP = [ps.enter_context(nc_sbuf(f"ltm{i}", [128, BLK], F32)) for i in range(2)]
                OST = [ps.enter_context(nc_sbuf(f"los{i}", [128, BLK], F32)) for i in range(2)]
                MEAN = ps.enter_context(nc_sbuf("lmean", [128, BLK], F32))
                M2 = ps.enter_context(nc_sbuf("lm2", [128, BLK], F32))
                RSTD = ps.enter_context(nc_sbuf("lrstd", [128, BLK], F32))
                pc = 0; xc = 0; oc = 0
                nel = nkc * 128
                for n in range(NB):
                    cols = slice(n * BLK, (n + 1) * BLK)
                    q4 = nkc // 4
                    for qi in range(4):
                        S.dma("sp", SRC[:, qi * q4:(qi + 1) * q4, :],
                              src.rearrange("(kc p) t -> p kc t", p=128)[:, qi * q4:(qi + 1) * q4, cols],
                              [("scr", id(src))], ["lsrc"], f"lsrc{qi}")
                    nxt = load_w(wscr[0], nel, wtokc, (nkc, 128))
                    for m in range(NKC):
                        W, wtok = nxt
                        if m + 1 < NKC:
                            nxt = load_w(wscr[m + 1], nel, wtokc, (nkc, 128))
                        xi = xc % 3; xc += 1
                        S.dma("sp", XRS[xi][:], xres[m * 128:(m + 1) * 128, cols], [("xr", n)], [("lxr", xi)], f"lxr{xi}")
                        pt = PSB[pc % 4]; ptok = ("lp", pc % 4); pc += 1
                        for kc in range(nkc):
                            mm(pt[:], W[:, kc, :], SRC[:, kc, :], kc == 0, kc == nkc - 1, [wtok, "lsrc"], [ptok], kc == nkc - 1)
                        S.op("dve", lambda e: e.scalar_tensor_tensor(out=ZB[:, m, :], in0=XRS[xi][:], scalar=float(ALPHA),
                                                                      in1=pt[:], op0=ALU.mult, op1=ALU.add),
                             [ptok, ("lxr", xi)], [("zb", m)])
                    for m in range(NKC):
                        si = m % 2
                        S.op("act", lambda e: e.activation(out=SQ[si][:], in_=ZB[:, m, :], func=AF.Square),
                             [("zb", m)], [("lsq", si)])
                        mm(PMEAN[:], ONES_D, ZB[:, m, :], m == 0, m == NKC - 1, [("zb", m), "CM"], ["lpm"], m == NKC - 1)
                        mm(PMSQ[:], ONES_D, SQ[si][:], m == 0, m == NKC - 1, [("lsq", si), "CM"], ["lpq"], True)
                    S.op("act", lambda e: e.activation(out=MEAN[:], in_=PMEAN[:], func=AF.Copy), ["lpm"], ["lmean"])
                    S.op("dve", lambda e: e.tensor_tensor(out=M2[:], in0=MEAN[:], in1=MEAN[:], op=ALU.mult), ["lmean"], ["lm2"])
                    S.op("dve", lambda e: e.tensor_tensor(out=M2[:], in0=PMSQ[:], in1=M2[:], op=ALU.subtract), ["lpq", "lm2"], ["lm2"])
                    S.op("act", lambda e: e.activation(out=M2[:], in_=M2[:], func=AF.Sqrt, bias=vcol("eps_ln"), scale=1.0),
                         ["lm2", "VEC"], ["lm2"])
                    S.op("dve", lambda e: e.reciprocal(out=RSTD[:], in_=M2[:]), ["lm2"], ["lrstd"])
                    for m in range(NKC):
                        ti = m % 2
                        S.op("dve", lambda e: e.tensor_tensor(out=TMP[ti][:], in0=ZB[:, m, :], in1=MEAN[:], op=ALU.subtract),
                             [("zb", m), "lmean"], [("ltm", ti)])
                        S.op("dve", lambda e: e.tensor_tensor(out=TMP[ti][:], in0=TMP[ti][:], in1=RSTD[:], op=ALU.mult),
                             [("ltm", ti), "lrstd"], [("ltm", ti)])
                        oi = oc % 2; oc += 1
                        S.op("act", lambda e: e.activation(out=OST[oi][:], in_=TMP[ti][:], func=AF.Identity,
                                                           scale=vcol(gname, l * 16 + m), bias=vcol(bname, l * 16 + m)),
                             [("ltm", ti), "VEC"], [("los", oi)])
                        if final_out is None:
                            S.op("act", lambda e: e.activation(out=XB[:, m, cols], in_=OST[oi][:], func=AF.Copy),
                                 [("los", oi)], [("XB", n)])
                            S.dma("sp", XR[m * 128:(m + 1) * 128, cols], OST[oi][:], [("los", oi)], [("xr", n)], f"los{oi}")
                        else:
                            S.dma("sp", final_out[m * 128:(m + 1) * 128, cols], OST[oi][:], [("los", oi)], [("out", n)], f"los{oi}")
                S.barrier()

        def phase_ffn1(l):
            wg, wu = wset(l)[2], wset(l)[3]
            with ExitStack() as ps:
                PG = [ps.enter_context(nc_psum(f"fg{i}", [128, 512], F32)) for i in range(2)]
                PU = [ps.enter_context(nc_psum(f"fu{i}", [128, 512], F32)) for i in range(2)]
                SGT = [ps.enter_context(nc_sbuf(f"fsg{i}", [128, BLK], F32)) for i in range(2)]
                HST = [ps.enter_context(nc_sbuf(f"fhs{i}", [128, T], BF16)) for i in range(2)]
                pc = 0; hc = 0

                def ld(sidx):
                    b, ci = sidx // 2, sidx % 2
                    i = ws_ctr[0] % 3; ws_ctr[0] += 1
                    gsrc = wg[b].rearrange("p (kc c) -> p kc c", c=256)[:, :, ci * 128:(ci + 1) * 128]
                    usrc = wu[b].rearrange("p (kc c) -> p kc c", c=256)[:, :, ci * 128:(ci + 1) * 128]
                    S.dma("sp", WS[i][:, 0:2048].rearrange("p (kc c) -> p kc c", c=128), gsrc, [("wc", "wg", l)], [("WS", i)], f"ws{i}")
                    S.dma("sp", WS[i][:, 2048:4096].rearrange("p (kc c) -> p kc c", c=128), usrc, [("wc", "wg", l)], [("WS", i)], f"wsu{i}")
                    return (WS[i][:, 0:2048].rearrange("p (kc c) -> p kc c", c=128),
                            WS[i][:, 2048:4096].rearrange("p (kc c) -> p kc c", c=128), ("WS", i))
                pend = [ld(0), ld(1)]
                for sidx in range(44):
                    if sidx + 2 < 44:
                        pend.append(ld(sidx + 2))
                    Wg, Wu, wtok = pend.pop(0)
                    hi = hc % 2; hc += 1
                    for n in range(NB):
                        pi = pc % 2; pc += 1
                        for kc in range(NKC):
                            mm(PG[pi][:], Wg[:, kc, :], XB[:, kc, n * BLK:(n + 1) * BLK],
                               kc == 0, kc == NKC - 1, [wtok, ("XB", n)], [("fg", pi)], kc == NKC - 1)
                        for kc in range(NKC):
                            mm(PU[pi][:], Wu[:, kc, :], XB[:, kc, n * BLK:(n + 1) * BLK],
                               kc == 0, kc == NKC - 1, [wtok, ("XB", n)], [("fu", pi)], kc == NKC - 1)
                        S.op("act", lambda e: e.activation(out=SGT[pi][:], in_=PG[pi][:], func=AF.Silu),
                             [("fg", pi)], [("fsg", pi)])
                        S.op("dve", lambda e: e.tensor_tensor(out=HST[hi][:, n * BLK:(n + 1) * BLK], in0=SGT[pi][:],
                                                              in1=PU[pi][:], op=ALU.mult),
                             [("fsg", pi), ("fu", pi)], [("fhs", hi)])
                    r0 = sidx * 128
                    S.dma("sp", HT[r0:r0 + 128, :], HST[hi][:], [("fhs", hi)], [("scr", id(HT))], f"fhs{hi}")
                S.barrier()

        def phase_na(l):
            j = l // 2
            with ExitStack() as ps:
                PS_S = [ps.enter_context(nc_psum(f"nas{i}", [64, 512], F32)) for i in range(2)]
                PS_T = [ps.enter_context(nc_psum(f"nat{i}", [64, 1024], BF16))[:, 0:512].rearrange("p (a b) -> p a b", b=64) for i in range(2)]
                PS_O = [ps.enter_context(nc_psum(f"nao{i}", [128, 512], F32))[:, 0:64] for i in range(2)]
                QT = [ps.enter_context(nc_sbuf(f"naq{i}", [128, T], BF16)) for i in range(2)]
                KT = [ps.enter_context(nc_sbuf(f"nak{i}", [128, T], BF16)) for i in range(2)]
                VV = [ps.enter_context(nc_sbuf(f"nav{i}", [64, 32, 128], BF16)) for i in range(2)]
                BI = [ps.enter_context(nc_sbuf(f"nab{i}", [64, 8 * 512], F32)) for i in range(2)]
                SS = [ps.enter_context(nc_sbuf(f"nass{i}", [64, 512], F32)) for i in range(2)]
                PP = [ps.enter_context(nc_sbuf(f"napp{i}", [64, 512], BF16)) for i in range(2)]
                PN = [ps.enter_context(nc_sbuf(f"napn{i}", [64, 512], BF16)) for i in range(2)]
                PT = [ps.enter_context(nc_sbuf(f"napt{i}", [64, 512], BF16)) for i in range(2)]
                ST = [ps.enter_context(nc_sbuf(f"nast{i}", [64, 4], F32)) for i in range(2)]
                YO = [ps.enter_context(nc_sbuf(f"nayo{i}", [128, T], BF16)) for i in range(2)]
                it = 0
                for h in range(8):
                    hb = h % 2
                    S.dma("sp", QT[hb][:], AQK[h * 128:(h + 1) * 128, :], [("scr", id(AQK))], [("naq", hb)], f"naq{hb}")
                    S.dma("sp", KT[hb][:], AQK[1024 + h * 128:1024 + (h + 1) * 128, :], [("scr", id(AQK))], [("nak", hb)], f"nak{hb}")
                    S.dma("sp", VV[hb][:], AV.rearrange("(r p) f -> p r f", p=64)[:, :, h * 128:(h + 1) * 128],
                          [("scr", id(AV))], [("nav", hb)], f"nav{hb}")
                    S.dma("sp", BI[hb][:], na_bias[j, h], [], [("nab", hb)], f"nab{hb}")
                    for r in range(32):
                        i2 = it % 2; it += 1
                        r0 = min(max(r - 4, 0), 24)
                        var = r0 - r + 7
                        k0 = r0 * 64
                        S.op("pe", lambda e: e.matmul(PS_S[i2][:], QT[hb][:, r * 64:(r + 1) * 64], KT[hb][:, k0:k0 + 512],
                                                      start=True, stop=True),
                             [("naq", hb), ("nak", hb)], [("nas", i2)])
                        S.op("dve", lambda e: e.scalar_tensor_tensor(out=SS[i2][:], in0=PS_S[i2][:], scalar=float(128 ** -0.5),
                                                                      in1=BI[hb][:, var * 512:(var + 1) * 512],
                                                                      op0=ALU.mult, op1=ALU.add),
                             [("nas", i2), ("nab", hb)], [("nass", i2)])
                        S.op("dve", lambda e: e.tensor_reduce(out=ST[i2][:, 0:1], in_=SS[i2][:], axis=AX.X, op=ALU.max, negate=True),
                             [("nass", i2)], [("nast", i2)])
                        S.op("act", lambda e: e.activation(out=PP[i2][:], in_=SS[i2][:], func=AF.Exp, bias=ST[i2][:, 0:1],
                                                           scale=1.0, accum_out=ST[i2][:, 1:2]),
                             [("nass", i2), ("nast", i2)], [("napp", i2), ("nast2", i2)])
                        S.op("dve", lambda e: e.reciprocal(out=ST[i2][:, 2:3], in_=ST[i2][:, 1:2]), [("nast2", i2)], [("nast3", i2)])
                        S.op("act", lambda e: e.activation(out=PN[i2][:], in_=PP[i2][:], func=AF.Identity, scale=ST[i2][:, 2:3]),
                             [("napp", i2), ("nast3", i2)], [("napn", i2)])
                        for jj in range(8):
                            S.op("pe", lambda e: e.transpose(PS_T[i2][:, jj, :], PN[i2][:, jj * 64:(jj + 1) * 64], CB[0:64, 0:64]),
                                 [("napn", i2), "CB"], [("nat", i2)], inc=(jj == 7))
                        S.op("act", lambda e: e.activation(out=PT[i2][:], in_=PS_T[i2][:].rearrange("p a b -> p (a b)"), func=AF.Copy),
                             [("nat", i2)], [("napt", i2)])
                        for jj in range(8):
                            S.op("pe", lambda e: e.matmul(PS_O[i2][:], VV[hb][:, r0 + jj, :], PT[i2][:, jj * 64:(jj + 1) * 64],
                                                          start=(jj == 0), stop=(jj == 7)),
                                 [("nav", hb), ("napt", i2)], [("nao", i2)], inc=(jj == 7))
                        S.op("act", lambda e: e.activation(out=YO[hb][:, r * 64:(r + 1) * 64], in_=PS_O[i2][:], func=AF.Copy),
                             [("nao", i2)], [("nayo", hb)])
                    S.dma("sp", YT[h * 128:(h + 1) * 128, :], YO[hb][:], [("nayo", hb)], [("scr", id(YT))], f"nayo{hb}")
                S.barrier()

        def phase_hgrn(l):
            j = l // 2
            with ExitStack() as ps:
                P_A = [ps.enter_context(nc_psum(f"hga{i}", [64, 512], F32))[:, 0:64] for i in range(2)]
                P_K = [ps.enter_context(nc_psum(f"hgk{i}", [64, 1024], BF16))[:, 0:128] for i in range(2)]
                P_O = [ps.enter_context(nc_psum(f"hgo{i}", [128, 512], F32))[:, 0:64] for i in range(2)]
                P_SF = [ps.enter_context(nc_psum(f"hgs{i}", [128, 512], F32)) for i in range(2)]
                P_S = [x[:, 0:128] for x in P_SF]
                HQs = ps.enter_context(nc_sbuf("hq", [128, T], BF16))
                HGs = ps.enter_context(nc_sbuf("hg", [128, T], BF16))
                VVs = ps.enter_context(nc_sbuf("hv", [64, 32, 128], BF16))
                SQs = ps.enter_context(nc_sbuf("hsq", [128, T], F32))
                Z = ps.enter_context(nc_sbuf("hz", [128, T], F32))
                Bk = ps.enter_context(nc_sbuf("hbk", [128, T], F32))
                Cg = ps.enter_context(nc_sbuf("hcg", [128, T], F32))
                EA = [ps.enter_context(nc_sbuf(f"hea{d}", [128, T], F32)) for d in range(2)]
                QG = [WS[0][:, d * T:(d + 1) * T] for d in range(2)]
                KG = [WS[1][:, d * T:(d + 1) * T] for d in range(2)]
                KD = [ps.enter_context(nc_sbuf(f"hkd{d}", [128, T], BF16)) for d in range(2)]
                O32 = ps.enter_context(nc_sbuf("ho32", [128, T], F32))
                S32 = [ps.enter_context(nc_sbuf(f"hs32{d}", [128, 128], F32)) for d in range(2)]
                SBF = [ps.enter_context(nc_sbuf(f"hsbf{d}", [128, 128], BF16)) for d in range(2)]
                ATM = [ps.enter_context(nc_sbuf(f"hatm{d}", [64, 64], BF16)) for d in range(2)]
                KDT = [ps.enter_context(nc_sbuf(f"hkdt{d}", [64, 128], BF16)) for d in range(2)]
                LB = ps.enter_context(nc_sbuf("hlb", [128, 4], F32))
                SMK = ps.enter_context(nc_sbuf("hsmk", [128, T], BF16))
                YO = KD[0]
                RS = ps.enter_context(nc_sbuf("hrs", [128, BLK], F32))
                S.dma("sp", SMK[:], scanmask[:, :], [], ["hsmk"], "hsmk")
                for h in range(8):
                    S.dma("sp", HQs[:], HQ[h * 128:(h + 1) * 128, :], [("scr", id(HQ))], ["hq"], "hq")
                    S.dma("sp", HGs[:], HG[h * 128:(h + 1) * 128, :], [("scr", id(HG))], ["hg"], "hg")
                    S.dma("sp", VVs[:], HI.rearrange("(r p) f -> p r f", p=64)[:, :, h * 128:(h + 1) * 128],
                          [("scr", id(HI))], ["hv"], "hv")
                    if j == 0:
                        S.op("dve", lambda e: e.tensor_copy(out=LB[:, 0:1], in_=vcol("zero")), ["VEC"], ["hlb"])
                    else:
                        S.op("dve", lambda e: e.tensor_tensor(out=LB[:, 2:3], in0=vcol("lb_logit", 8 + h), in1=vcol("lb_logit", h),
                                                              op=ALU.subtract), ["VEC"], ["hlb2"])
                        S.op("act", lambda e: e.activation(out=LB[:, 0:1], in_=LB[:, 2:3], func=AF.Sigmoid), ["hlb2"], ["hlb"])
                    S.op("dve", lambda e: e.tensor_scalar(out=LB[:, 1:2], in0=LB[:, 0:1], scalar1=-1.0, scalar2=1.0,
                                                          op0=ALU.mult, op1=ALU.add), ["hlb"], ["hlb1"])
                    S.op("act", lambda e: e.activation(out=SQs[:], in_=HQs[:], func=AF.Silu), ["hq"], ["hsq"])
                    for d in range(2):
                        S.dma("sp", Z[:], HF[d * 1024 + h * 128: d * 1024 + (h + 1) * 128, :], [("scr", id(HF))], ["hz"], "hz")
                        S.op("act", lambda e: e.activation(out=EA[d][:], in_=Z[:], func=AF.Sigmoid), ["hz"], [("hea", d)])
                        S.op("dve", lambda e: e.tensor_scalar(out=EA[d][:], in0=EA[d][:], scalar1=LB[:, 1:2], scalar2=LB[:, 0:1],
                                                              op0=ALU.mult, op1=ALU.add), [("hea", d), "hlb", "hlb1"], [("hea", d)])
                        S.op("dve", lambda e: e.tensor_scalar(out=Bk[:], in0=EA[d][:], scalar1=-1.0, scalar2=1.0,
                                                              op0=ALU.mult, op1=ALU.add), [("hea", d)], ["hbk"])
                        S.op("act", lambda e: e.activation(out=EA[d][:], in_=EA[d][:], func=AF.Ln), [("hea", d)], [("hea", d)])
                        if d == 0:
                            S.op("dve", lambda e: e.tensor_tensor_scan(out=Cg[:], data0=SMK[:], data1=EA[d][:], initial=0.0,
                                                                       op0=ALU.mult, op1=ALU.add), [("hea", d), "hsmk"], ["hcg"])
                        else:
                            S.op("dve", lambda e: e.tensor_tensor_scan(out=Cg[:, ::-1], data0=SMK[:], data1=EA[d][:, ::-1], initial=0.0,
                                                                       op0=ALU.mult, op1=ALU.add), [("hea", d), "hsmk"], ["hcg"])
                        S.op("act", lambda e: e.activation(out=EA[d][:], in_=Cg[:], func=AF.Exp), ["hcg"], [("hea", d)])
                        S.op("act", lambda e: e.activation(out=Z[:], in_=Cg[:], func=AF.Exp, scale=-1.0), ["hcg"], ["hz"])
                        S.op("dve", lambda e: e.tensor_tensor(out=QG[d][:], in0=SQs[:], in1=EA[d][:], op=ALU.mult),
                             ["hsq", ("hea", d)], [("hqg", d)])
                        S.op("dve", lambda e: e.tensor_tensor(out=KG[d][:], in0=Bk[:], in1=Z[:], op=ALU.mult),
                             ["hbk", "hz"], [("hkg", d)])
                        e3 = EA[d][:].rearrange("p (c t) -> p c t", t=64)
                        eend = e3[:, :, 63:64] if d == 0 else e3[:, :, 0:1]
                        S.op("dve", lambda e: e.tensor_tensor(out=KD[d][:].rearrange("p (c t) -> p c t", t=64),
                                                              in0=KG[d][:].rearrange("p (c t) -> p c t", t=64),
                                                              in1=eend.to_broadcast([128, 32, 64]), op=ALU.mult),
                             [("hkg", d), ("hea", d)], [("hkd", d)])
                    for step in range(32):
                        for d in range(2):
                            c = step if d == 0 else 31 - step
                            tc_ = slice(c * 64, (c + 1) * 64)
                            eidx = c * 64 + 63 if d == 0 else c * 64
                            tri = TRIU if d == 0 else TRIL
                            S.op("pe", lambda e: e.matmul(P_A[d][:], KG[d][:, tc_], QG[d][:, tc_], start=True, stop=True),
                                 [("hkg", d), ("hqg", d)], [("hga", d)])
                            S.op("dve", lambda e: e.tensor_tensor(out=ATM[d][:], in0=P_A[d][:], in1=tri, op=ALU.mult),
                                 [("hga", d), "CM"], [("hatm", d)])
                            S.op("pe", lambda e: e.transpose(P_K[d][:], KD[d][:, tc_], CB[:, :]), [("hkd", d), "CB"], [("hgk", d)])
                            S.op("act", lambda e: e.activation(out=KDT[d][:], in_=P_K[d][:], func=AF.Copy), [("hgk", d)], [("hkdt", d)])
                            S.op("pe", lambda e: e.matmul(P_O[d][:], VVs[:, c, :], ATM[d][:], start=True, stop=(step == 0)),
                                 ["hv", ("hatm", d)], [("hgo", d)], inc=(step == 0))
                            if step > 0:
                                S.op("pe", lambda e: e.matmul(P_O[d][:], SBF[d][:], QG[d][:, tc_], start=False, stop=True),
                                     [("hsbf", d), ("hqg", d)], [("hgo", d)])
                            if step < 16:
                                S.op("act", lambda e: e.activation(out=O32[:, tc_], in_=P_O[d][:], func=AF.Copy),
                                     [("hgo", d)], [("ho32", c)])
                            else:
                                S.op("dve", lambda e: e.tensor_tensor(out=O32[:, tc_], in0=O32[:, tc_], in1=P_O[d][:], op=ALU.add),
                                     [("hgo", d), ("ho32", c)], [("ho32", c)])
                            if step < 31:
                                S.op("pe", lambda e: e.matmul(P_S[d][:], KDT[d][:], VVs[:, c, :], start=True, stop=True),
                                     [("hkdt", d), "hv"], [("hgs", d)])
                                if step == 0:
                                    S.op("dve", lambda e: e.tensor_copy(out=S32[d][:], in_=P_S[d][:]), [("hgs", d)], [("hs32", d)])
                                else:
                                    S.op("dve", lambda e: e.scalar_tensor_tensor(out=S32[d][:], in0=S32[d][:], scalar=EA[d][:, eidx:eidx + 1],
                                                                                  in1=P_S[d][:], op0=ALU.mult, op1=ALU.add),
                                         [("hgs", d), ("hs32", d), ("hea", d)], [("hs32", d)])
                                S.op("act", lambda e: e.activation(out=SBF[d][:], in_=S32[d][:], func=AF.Copy), [("hs32", d)], [("hsbf", d)])
                    allo = [("ho32", c) for c in range(32)]
                    S.op("act", lambda e: e.activation(out=Z[:], in_=O32[:], func=AF.Square), allo, ["hz"])
                    S.op("act", lambda e: e.activation(out=SQs[:], in_=HGs[:], func=AF.Silu), ["hg"], ["hsq"])
                    for n in range(NB):
                        cols = slice(n * BLK, (n + 1) * BLK)
                        S.op("pe", lambda e: e.matmul(P_SF[0][:], ONES_H, Z[:, cols], start=True, stop=True), ["hz", "CM"], [("hgs", 0)])
                        S.op("act", lambda e: e.activation(out=RS[:], in_=P_SF[0][:], func=AF.Sqrt, bias=vcol("eps_rms"), scale=1.0),
                             [("hgs", 0), "VEC"], ["hrs"])
                        S.op("dve", lambda e: e.reciprocal(out=RS[:], in_=RS[:]), ["hrs"], ["hrs"])
                        S.op("dve", lambda e: e.tensor_tensor(out=RS[:], in0=O32[:, cols], in1=RS[:], op=ALU.mult), allo + ["hrs"], ["hrs"])
                        S.op("dve", lambda e: e.scalar_tensor_tensor(out=YO[:, cols], in0=RS[:], scalar=vcol("hg_norm_w", j * 8 + h),
                                                                      in1=SQs[:, cols], op0=ALU.mult, op1=ALU.mult),
                             ["hrs", "hsq", "VEC"], [("hkd", 0)])
                    S.dma("sp", YT[1024 + h * 128:1024 + (h + 1) * 128, :], YO[:], [("hkd", 0)], [("scr", id(YT))], "hyo")
                S.barrier()

        def phase_pool(l):
            j = l // 2
            PADL = 8
            with ExitStack() as ps:
                PSB = [ps.enter_context(nc_psum(f"plp{i}", [128, 512], F32)) for i in range(2)]
                XP = [ps.enter_context(nc_sbuf(f"plx{i}", [128, T + 16], F32)) for i in range(2)]
                SA = ps.enter_context(nc_sbuf("plsa", [128, T + 16], F32))
                SBt = ps.enter_context(nc_sbuf("plsb", [128, T + 16], F32))
                PL = [ps.enter_context(nc_sbuf(f"plp{i}", [128, T], BF16)) for i in range(2)]
                PINV = ps.enter_context(nc_sbuf("plinv", [128, T], F32))
                WPs = ps.enter_context(nc_sbuf("plw", [128, 4, 2, 256], BF16))
                YO = [ps.enter_context(nc_sbuf(f"plyo{i}", [128, T], BF16)) for i in range(2)]
                S.dma("sp", WPs[:].rearrange("p g k d -> p (g k d)"), WPs_[j][:, :], [("wc", "wp", l)], ["plw"], "plw")
                for i in range(2):
                    S.op("dve", lambda e: e.memset(XP[i][:], 0.0), [], [("plx", i)])
                S.op("dve", lambda e: e.memset(SA[:], 0.0), [], ["plsa"])
                S.op("dve", lambda e: e.memset(SBt[:], 0.0), [], ["plsb"])
                yc = 0
                for g in range(4):
                    w = (2, 4, 8, 16)[g]
                    S.dma("sp", PINV[:], poolinv[:, g * T:(g + 1) * T], [], ["plinv"], "plinv")
                    for kc in range(2):
                        ch = g * 2 + kc
                        S.dma("sp", XP[kc][:, PADL:PADL + T], CX[ch * 128:(ch + 1) * 128, :], [("scr", id(CX))], [("plx", kc)], f"plx{kc}")
                        L = T + 16
                        cur, curtok, ln_ = XP[kc], ("plx", kc), L
                        bufs = [(SA, "plsa"), (SBt, "plsb")]
                        bi = 0; span = 1
                        while span < w:
                            dst, dtok = bufs[bi % 2]; bi += 1
                            nl = ln_ - span
                            S.op("dve", lambda e: e.tensor_tensor(out=dst[:, 0:nl], in0=cur[:, 0:nl], in1=cur[:, span:span + nl], op=ALU.add),
                                 [curtok], [dtok])
                            cur, curtok, ln_ = dst, dtok, nl
                            span *= 2
                        o0 = PADL - w // 2
                        dst, dtok = bufs[bi % 2]
                        S.op("dve", lambda e: e.tensor_tensor(out=dst[:, 0:T], in0=cur[:, o0:o0 + T], in1=PINV[:], op=ALU.mult),
                             [curtok, "plinv"], [dtok])
                        S.op("dve", lambda e: e.tensor_tensor(out=PL[kc][:], in0=dst[:, 0:T], in1=XP[kc][:, PADL:PADL + T], op=ALU.subtract),
                             [dtok, ("plx", kc)], [("plpl", kc)])
                    for dc in range(2):
                        yi = yc % 2; yc += 1
                        for n in range(NB):
                            pi = n % 2
                            for kc in range(2):
                                mm(PSB[pi][:], WPs[:, g, kc, dc * 128:(dc + 1) * 128], PL[kc][:, n * BLK:(n + 1) * BLK],
                                   kc == 0, kc == 1, ["plw", ("plpl", kc)], [("plps", pi)], kc == 1)
                            S.op("act", lambda e: e.activation(out=YO[yi][:, n * BLK:(n + 1) * BLK], in_=PSB[pi][:], func=AF.Identity,
                                                               scale=vcol("pool_scale", j * 8 + g * 2 + dc)),
                                 [("plps", pi), "VEC"], [("plyo", yi)])
                        r0 = g * 256 + dc * 128
                        S.dma("sp", YT[r0:r0 + 128, :], YO[yi][:], [("plyo", yi)], [("scr", id(YT))], f"plyo{yi}")
                S.barrier()

        def phase_gqa(l):
            j = l // 2
            scale = float(128 ** -0.5)
            with ExitStack() as ps:
                PS_S = ps.enter_context(nc_psum("gqs", [128, T], F32))
                PS_T = ps.enter_context(nc_psum("gqt", [128, 2048], BF16)).rearrange("p (a b) -> p a b", b=128)
                PS_O = ps.enter_context(nc_psum("gqo", [128, 512], F32))
                PS_X = ps.enter_context(nc_psum("gqx", [128, 512], F32))
                QTs = ps.enter_context(nc_sbuf("gq", [128, 8, T], BF16))
                KTs = ps.enter_context(nc_sbuf("gk", [128, 2, T], BF16))
                VVs = ps.enter_context(nc_sbuf("gv", [128, 16, 256], BF16))
                RC = ps.enter_context(nc_sbuf("grc", [128, T], F32))
                RSn = ps.enter_context(nc_sbuf("grs", [128, T], F32))
                ZZ = [ps.enter_context(nc_sbuf(f"gz{i}", [128, BLK], F32)) for i in range(2)]
                T1 = ps.enter_context(nc_sbuf("gt1", [128, BLK], F32))
                T2 = ps.enter_context(nc_sbuf("gt2", [128, BLK], F32))
                T3 = ps.enter_context(nc_sbuf("gt3", [128, BLK], F32))
                PB = [ps.enter_context(nc_sbuf(f"gp{i}", [128, T], BF16)) for i in range(2)]
                PTs = [ps.enter_context(nc_sbuf(f"gpt{i}", [128, 16, 128], BF16)) for i in range(2)]
                STt = [ps.enter_context(nc_sbuf(f"gst{i}", [128, 4], F32)) for i in range(2)]
                OB = [ps.enter_context(nc_sbuf(f"gob{i}", [128, 128], BF16)) for i in range(2)]
                YO = [ps.enter_context(nc_sbuf(f"gyo{i}", [128, T], BF16)) for i in range(2)]
                S.dma("sp", RC[:], ropeC[:, :], [], ["grc"], "grc")
                S.dma("sp", RSn[:], ropeS[:, :], [], ["grs"], "grs")
                S.dma("sp", VVs[:], DV.rearrange("(tt p) f -> p tt f", p=128), [("scr", id(DV))], ["gv"], "gv")
                zc = 0
                for hh in range(10):
                    gcol = vcol("q_norm", j) if hh < 8 else vcol("k_norm", j)
                    for n in range(NB):
                        cols = slice(n * BLK, (n + 1) * BLK)
                        zi = zc % 2; zc += 1
                        S.dma("sp", ZZ[zi][:], DQK[hh * 128:(hh + 1) * 128, cols], [("scr", id(DQK))], [("gz", zi)], f"gz{zi}")
                        S.op("act", lambda e: e.activation(out=T1[:], in_=ZZ[zi][:], func=AF.Square), [("gz", zi)], ["gt1"])
                        S.op("pe", lambda e: e.matmul(PS_O[:], ONES_H, T1[:], start=True, stop=True), ["gt1", "CM"], ["gqo"])
                        S.op("act", lambda e: e.activation(out=T2[:], in_=PS_O[:], func=AF.Sqrt, bias=vcol("eps_rms"), scale=1.0),
                             ["gqo", "VEC"], ["gt2"])
                        S.op("dve", lambda e: e.reciprocal(out=T2[:], in_=T2[:]), ["gt2"], ["gt2"])
                        S.op("dve", lambda e: e.scalar_tensor_tensor(out=T1[:], in0=ZZ[zi][:], scalar=gcol, in1=T2[:],
                                                                      op0=ALU.mult, op1=ALU.mult), [("gz", zi), "gt2", "VEC"], ["gt1"])
                        S.op("pe", lambda e: e.matmul(PS_X[:], ROTP, T1[:], start=True, stop=True), ["gt1", "CM"], ["gqx"])
                        S.op("dve", lambda e: e.tensor_tensor(out=T3[:], in0=PS_X[:], in1=RSn[:, cols], op=ALU.mult), ["gqx", "grs"], ["gt3"])
                        S.op("dve", lambda e: e.tensor_tensor(out=T2[:], in0=T1[:], in1=RC[:, cols], op=ALU.mult), ["gt1", "grc"], ["gt2"])
                        dst = QTs[:, hh, cols] if hh < 8 else KTs[:, hh - 8, cols]
                        S.op("dve", lambda e: e.tensor_tensor(out=dst, in0=T2[:], in1=T3[:], op=ALU.add), ["gt2", "gt3"], ["gqk"])
                iters = [(hq, qt) for hq in range(8) for qt in range(16)]

                def stage_a(i):
                    hq, qt = iters[i]; kv = hq // 4; i2 = i % 2
                    qs = slice(qt * 128, (qt + 1) * 128)
                    for kb in range(4):
                        S.op("pe", lambda e: e.matmul(PS_S[:, kb * 512:(kb + 1) * 512], QTs[:, hq, qs], KTs[:, kv, kb * 512:(kb + 1) * 512],
                                                      start=True, stop=True), ["gqk"], ["gqs"], inc=(kb == 3))
                    S.op("dve", lambda e: e.tensor_reduce(out=STt[i2][:, 0:1], in_=PS_S[:], axis=AX.X, op=ALU.max, negate=True),
                         ["gqs"], [("gst", i2)])
                    S.op("dve", lambda e: e.tensor_scalar(out=STt[i2][:, 1:2], in0=STt[i2][:, 0:1], scalar1=scale, scalar2=None, op0=ALU.mult),
                         [("gst", i2)], [("gst1", i2)])
                    S.op("act", lambda e: e.activation(out=PB[i2][:], in_=PS_S[:], func=AF.Exp, bias=STt[i2][:, 1:2], scale=scale,
                                                       accum_out=STt[i2][:, 2:3]),
                         ["gqs", ("gst1", i2)], [("gp", i2), ("gst2", i2)])
                    S.op("dve", lambda e: e.reciprocal(out=STt[i2][:, 3:4], in_=STt[i2][:, 2:3]), [("gst2", i2)], [("gst3", i2)])

                def stage_t(i):
                    i2 = i % 2
                    for kc in range(16):
                        S.op("pe", lambda e: e.transpose(PS_T[:, kc, :], PB[i2][:, kc * 128:(kc + 1) * 128], CB[:, :]),
                             [("gp", i2), "CB"], ["gqt"], inc=(kc == 15))
                    S.op("act", lambda e: e.activation(out=PTs[i2][:, 0:8, :], in_=PS_T[:, 0:8, :], func=AF.Copy), ["gqt"], [("gpt", i2)])
                    S.op("dve", lambda e: e.tensor_copy(out=PTs[i2][:, 8:16, :], in_=PS_T[:, 8:16, :]), ["gqt"], [("gpt", i2)])

                def stage_v(i):
                    hq, qt = iters[i]; kv = hq // 4; i2 = i % 2; yb = hq % 2
                    qs = slice(qt * 128, (qt + 1) * 128)
                    for kc in range(16):
                        S.op("pe", lambda e: e.matmul(PS_O[:, 0:128], PTs[i2][:, kc, :], VVs[:, kc, kv * 128:(kv + 1) * 128],
                                                      start=(kc == 0), stop=(kc == 15)), [("gpt", i2), "gv"], ["gqo"], inc=(kc == 15))
                    S.op("act", lambda e: e.activation(out=OB[i2][:], in_=PS_O[:, 0:128], func=AF.Identity, scale=STt[i2][:, 3:4]),
                         ["gqo", ("gst3", i2)], [("gob", i2)])
                    S.op("pe", lambda e: e.transpose(PS_X[:, 0:64].bitcast(BF16), OB[i2][:], CB[:, :]), [("gob", i2), "CB"], ["gqx"])
                    S.op("dve", lambda e: e.tensor_copy(out=YO[yb][:, qs], in_=PS_X[:, 0:64].bitcast(BF16)), ["gqx"], [("gyo", yb)])
                    if qt == 15:
                        S.dma("sp", YT[1024 + hq * 128:1024 + (hq + 1) * 128, :], YO[yb][:], [("gyo", yb)], [("scr", id(YT))], f"gyo{yb}")

                if GQA_PIPE:
                    stage_a(0)
                    for i in range(len(iters)):
                        stage_t(i)
                        if i + 1 < len(iters):
                            stage_a(i + 1)
                        stage_v(i)
                else:
                    for i in range(len(iters)):
                        stage_a(i); stage_t(i); stage_v(i)
                S.barrier()

        with ExitStack() as ps:
            XS = [ps.enter_context(nc_sbuf(f"ixs{i}", [128, T], F32)) for i in range(2)]
            for m in range(NKC):
                i2 = m % 2
                S.dma("sp", XS[i2][:], xT[m * 128:(m + 1) * 128, :], [], [("ixs", i2)], f"ixs{i2}")
                evac(XB[:, m, :], XS[i2][:], [("ixs", i2)], [("XB", n) for n in range(NB)])
            S.barrier()

        def ab_spec(b):
            if b < 8: return ("fm", BF16, AQK, b * 256)
            if b < 12: return ("tm", BF16, AV, (b - 8) * 256)
            if b < 16: return ("fm", BF16, HQ, (b - 12) * 256)
            if b < 24: return ("fm", F32, HF, (b - 16) * 256)
            if b < 28: return ("tm", BF16, HI, (b - 24) * 256)
            return ("fm", BF16, HG, (b - 28) * 256)

        def cd_spec(b):
            if b < 4: return ("fm", F32, CX, b * 256)
            if b < 9: return ("fm", F32, DQK, (b - 4) * 256)
            return ("tm", BF16, DV, 0)

        for l in range(n_layers):
            if l > 0:
                S.new_epoch()
            xres = xT if l == 0 else XR
            wout, wd = wset(l)[1], wset(l)[4]
            if l % 2 == 0:
                phase_proj(l, 32, ab_spec)
                if stop_after == ("proj", l): break
                phase_na(l)
                if stop_after == ("na", l): break
                phase_hgrn(l)
                if stop_after == ("hgrn", l): break
            else:
                phase_proj(l, 10, cd_spec)
                if stop_after == ("proj", l): break
                phase_pool(l)
                if stop_after == ("pool", l): break
                phase_gqa(l)
                if stop_after == ("gqa", l): break
            phase_proj_ln(l, YT, 16, wout, ("wc", "wout", l), xres, "ln_mix_g", "ln_mix_b")
            if stop_after == ("mixln", l): break
            phase_ffn1(l)
            if stop_after == ("ffn1", l): break
            last = (l == n_layers - 1)
            phase_proj_ln(l, HT, 44, wd, ("wc", "wd", l), XR, "ln_ffn_g", "ln_ffn_b", final_out=outT if last else None)

        S.barrier()
        fin = {k: v for k, v in S.latest.items()}
        S._wait(S.eng["sp"], fin)
    return nc


def _consts():
    cmat = np.zeros((128, 4 * 128 + 64), np.float32)
    cmat[:, 0:128] = 1.0 / 2048.0
    cmat[:, 128:256] = 1.0 / 128.0
    P = np.zeros((128, 128), np.float32)
    for i in range(64):
        P[2 * i + 1, 2 * i] = -1.0
        P[2 * i, 2 * i + 1] = 1.0
    cmat[:, 256:384] = P
    s = np.arange(64)[:, None]; t = np.arange(64)[None, :]
    cmat[0:64, 384:448] = (s <= t).astype(np.float32)
    cmat[0:64, 448:512] = (s >= t).astype(np.float32)
    ident = np.eye(128, dtype=np.float32).astype(ml_dtypes.bfloat16)
    smask = np.ones((128, T), np.float32); smask[:, ::64] = 0.0
    pos = np.arange(T)
    row = (pos // 64).astype(np.float32); col = (pos % 64).astype(np.float32)
    inv = (np.float32(10000.0) ** (-np.arange(32, dtype=np.float32) / np.float32(32))).astype(np.float32)
    ang = np.concatenate([row[:, None] * inv, col[:, None] * inv], axis=-1).astype(np.float32)
    cos = np.cos(ang).astype(np.float32); sin = np.sin(ang).astype(np.float32)
    ropeC = np.repeat(cos.T, 2, axis=0).astype(np.float32)
    ropeS = np.repeat(sin.T, 2, axis=0).astype(np.float32)
    pinv = np.zeros((4, T), np.float32)
    for gi, w in enumerate((2, 4, 8, 16)):
        lo = np.clip(pos - w // 2, 0, T); hi = np.clip(pos + w // 2, 0, T)
        pinv[gi] = 1.0 / (hi - lo).astype(np.float32)
    poolinv = np.broadcast_to(pinv.reshape(1, 4 * T), (128, 4 * T)).copy()
    return cmat, ident, smask, ropeC, ropeS, poolinv


def _na_bias(rpb):
    col = np.arange(64)
    cs = np.clip(col - 8, 0, 48)
    cmask = (col[None, :] >= cs[:, None]) & (col[None, :] < cs[:, None] + 16)
    cidx = np.clip(col[None, :] - col[:, None] + 15, 0, 30)
    out = np.full((2, 8, 64, 8, 8, 64), NEG, np.float32)
    for var in range(8):
        for jj in range(8):
            ridx = var + jj
            if ridx < 0 or ridx > 14:
                continue
            g = rpb[:, :, ridx, :][:, :, cidx]
            out[:, :, :, var, jj, :] = np.where(cmask[None, None], g, np.float32(NEG))
    return out.reshape(2, 8, 64, 8 * 512)


def _vecs(inp):
    v = np.zeros((128, NV), np.float32)

    def put(name, arr, off=0):
        a = np.asarray(arr, np.float32).reshape(-1, 128).T
        v[:, VC[name] + off: VC[name] + off + a.shape[1]] = a
    for l in range(4):
        put("ln_mix_g", inp["ln_mix_g"][l], l * 16); put("ln_mix_b", inp["ln_mix_b"][l], l * 16)
        put("ln_ffn_g", inp["ln_ffn_g"][l], l * 16); put("ln_ffn_b", inp["ln_ffn_b"][l], l * 16)
    for j in range(2):
        put("lb_logit", inp["hg_lb_logits"][j], j * 8)
        put("hg_norm_w", inp["hg_norm_w"][j], j * 8)
        put("pool_scale", inp["pool_scale"][j], j * 8)
        put("q_norm", inp["d_q_norm"][j], j); put("k_norm", inp["d_k_norm"][j], j)
    v[:, VC["eps_ln"]] = LN_EPS; v[:, VC["eps_rms"]] = RMS_EPS; v[:, VC["zero"]] = 0.0; v[:, VC["one"]] = 1.0
    return v


def make_in_maps(inp, cores):
    cmat, ident, smask, ropeC, ropeS, poolinv = _consts()
    f = lambda k: np.ascontiguousarray(np.asarray(inp[k], np.float32))
    shared = {
        "ab_w_in": f("ab_w_in"), "ab_w_out": f("ab_w_out"), "cd_w_in": f("cd_w_in"), "cd_w_out": f("cd_w_out"),
        "pool_w": f("pool_w"), "ffn_w_gate": f("ffn_w_gate"), "ffn_w_up": f("ffn_w_up"), "ffn_w_down": f("ffn_w_down"),
        "vecs": _vecs(inp), "na_bias": _na_bias(np.asarray(inp["na_rpb"], np.float32)),
        "cm": cmat, "cb": ident, "scanmask": smask.astype(ml_dtypes.bfloat16), "ropeC": ropeC, "ropeS": ropeS, "poolinv": poolinv,
    }
    x = np.asarray(inp["x"], np.float32)
    maps = []
    for c in cores:
        m = dict(shared); m["xT"] = np.ascontiguousarray(x[c].T)
        maps.append(m)
    return maps


def kernel(**inputs):
    nc = build_program()
    in_maps = make_in_maps(inputs, list(range(8)))
    res = run_bass_kernel_spmd(nc, in_maps, core_ids=list(range(8)))
    out = np.stack([np.ascontiguousarray(r["outT"].T) for r in res.results], axis=0)
    return out.astype(np.float32)
```

```python
import numpy as np
import ml_dtypes
from contextlib import ExitStack
import concourse.bass as bass
import concourse.mybir as mybir
from concourse.bass_utils import run_bass_kernel_spmd

F32, BF16 = mybir.dt.float32, mybir.dt.bfloat16
AF = mybir.ActivationFunctionType
ALU = mybir.AluOpType
AX = mybir.AxisListType

T = 2048; D = 2048; HID = 5632; NKC = 16; NB = 4; BLK = 512
DEPTH = 4
ALPHA = (2 * DEPTH) ** 0.25
LN_EPS = 1e-5; RMS_EPS = 1e-6
NEG = -30000.0
GQA_PIPE = True
GQA_NIT = 128

VC = {}
_c = 0
for _n, _w in (("ln_mix_g", 64), ("ln_mix_b", 64), ("ln_ffn_g", 64), ("ln_ffn_b", 64),
               ("lb_logit", 16), ("hg_norm_w", 16), ("pool_scale", 16), ("q_norm", 2), ("k_norm", 2),
               ("eps_ln", 1), ("eps_rms", 1), ("zero", 1), ("one", 1)):
    VC[_n] = _c; _c += _w
NV = _c


class Sched:
    def __init__(self, nc, es):
        self.nc = nc; self.es = es
        self.eng = {}
        for name, h in (("pe", nc.tensor), ("act", nc.scalar), ("dve", nc.vector), ("sp", nc.sync), ("pool", nc.gpsimd)):
            self.eng[name] = dict(h=h, sem=None, cnt=0, waited={}, name=name)
        self.sems = {}; self.latest = {}; self.tok = {}; self.nobar = set()
        self.epoch = -1
        self.new_epoch()

    def _newsem(self, name):
        s = self.es.enter_context(self.nc.semaphore(name))
        self.sems[name] = s; self.latest[name] = 0
        return name

    def new_epoch(self):
        self.epoch += 1
        for n in ("pe", "act", "dve"):
            e = self.eng[n]; e["sem"] = self._newsem(f"e{self.epoch}_{n}"); e["cnt"] = 0

    def _wait(self, e, needs):
        for k, v in needs.items():
            if v <= 0 or e["waited"].get(k, 0) >= v:
                continue
            e["h"].wait_ge(self.sems[k], v); e["waited"][k] = v

    def _needs(self, reads, writes, own, skip_own_raw):
        needs = {}
        for t in reads:
            w, r = self.tok.setdefault(t, ({}, {}))
            for k, v in w.items():
                if k == own and skip_own_raw:
                    continue
                if needs.get(k, 0) < v: needs[k] = v
        for t in writes:
            w, r = self.tok.setdefault(t, ({}, {}))
            for d in (w, r):
                for k, v in d.items():
                    if k == own:
                        continue
                    if needs.get(k, 0) < v: needs[k] = v
        return needs

    def op(self, ename, fn, reads=(), writes=(), inc=True):
        e = self.eng[ename]; own = e["sem"]
        self._wait(e, self._needs(reads, writes, own, ename == "pe"))
        ins = fn(e["h"])
        if inc:
            e["cnt"] += 1
            ins.then_inc(self.sems[own], 1)
            self.latest[own] = e["cnt"]
            val = e["cnt"]
        else:
            val = e["cnt"] + 1
        for t in reads: self.tok[t][1][own] = val
        for t in writes: self.tok[t][0][own] = val
        return ins

    def dma(self, qname, out, in_, reads, writes, key, nobar=False, wval=None):
        e = self.eng[qname]
        if key not in self.sems:
            self._newsem(key)
            if nobar: self.nobar.add(key)
        self._wait(e, self._needs(reads, writes, None, False))
        ins = e["h"].dma_start(out=out, in_=in_)
        self.latest[key] += 16
        ins.then_inc(self.sems[key], 16)
        v = self.latest[key]
        for t in reads: self.tok.setdefault(t, ({}, {}))[1][key] = v
        for t in writes: self.tok.setdefault(t, ({}, {}))[0][key] = v
        return v

    def barrier(self):
        need = {k: v for k, v in self.latest.items() if k not in self.nobar}
        for n in ("pe", "act", "dve", "sp"):
            self._wait(self.eng[n], need)


def build_program(debug=(), n_layers=DEPTH, stop_after=None):
    nc = bass.Bass("TRN2", target_bir_lowering=False)
    dbg = set(debug)

    def din(name, shape, dt=F32):
        return nc.dram_tensor(name, list(shape), dt, kind="ExternalInput").ap()

    def dscr(name, shape, dt):
        kind = "ExternalOutput" if name in dbg else "Internal"
        return nc.dram_tensor(name, list(shape), dt, kind=kind).ap()

    xT = din("xT", [D, T])
    ab_w_in = din("ab_w_in", [2, D, 8192]); ab_w_out = din("ab_w_out", [2, D, D])
    cd_w_in = din("cd_w_in", [2, D, 2560]); cd_w_out = din("cd_w_out", [2, D, D])
    pool_w = din("pool_w", [2, 4, 256, 256])
    w_gate = din("ffn_w_gate", [4, D, HID]); w_up = din("ffn_w_up", [4, D, HID]); w_down = din("ffn_w_down", [4, HID, D])
    vecs = din("vecs", [128, NV])
    na_bias = din("na_bias", [2, 8, 64, 8 * 512])
    cm = din("cm", [128, 4 * 128 + 64])
    cb = din("cb", [128, 128], BF16)
    scanmask = din("scanmask", [128, T], BF16)
    ropeC = din("ropeC", [128, T]); ropeS = din("ropeS", [128, T])
    poolinv = din("poolinv", [128, 4 * T])
    outT = nc.dram_tensor("outT", [D, T], F32, kind="ExternalOutput").ap()

    XR = dscr("XR", [D, T], F32)
    WINs = [dscr(f"WIN{l}", [32 if l % 2 == 0 else 10, 128, 16 * 256], BF16) for l in range(4)]
    WOUTs = [dscr(f"WOUT{l}", [16, 128, 16 * 128], BF16) for l in range(4)]
    WGs = [dscr(f"WG{l}", [22, 128, 16 * 256], BF16) for l in range(4)]
    WUs = [dscr(f"WU{l}", [22, 128, 16 * 256], BF16) for l in range(4)]
    WDs = [dscr(f"WD{l}", [16, 128, 44 * 128], BF16) for l in range(4)]
    WPs_ = [dscr(f"WP{j}", [128, 4 * 2 * 256], BF16) for j in range(2)]
    AQK = dscr("AQK", [2048, T], BF16)
    AV = dscr("AV", [T, 1024], BF16)
    HQ = dscr("HQ", [1024, T], BF16)
    HF = dscr("HF", [2048, T], F32)
    HI = dscr("HI", [T, 1024], BF16)
    HG = dscr("HG", [1024, T], BF16)
    CX = dscr("CX", [1024, T], F32)
    DQK = dscr("DQK", [1280, T], F32)
    DV = dscr("DV", [T, 256], BF16)
    YT = dscr("YT", [2048, T], BF16)
    HT = dscr("HT", [HID, T], BF16)

    with ExitStack() as es:
        S = Sched(nc, es)
        block = es.enter_context(nc.Block())

        def sb(name, shape, dt):
            return es.enter_context(nc.sbuf_tensor(name, list(shape), dt))

        _uid = [0]

        def nc_sbuf(name, shape, dt):
            _uid[0] += 1
            return nc.sbuf_tensor(f"{name}_u{_uid[0]}", shape, dt)

        def nc_psum(name, shape, dt):
            _uid[0] += 1
            return nc.psum_tensor(f"{name}_u{_uid[0]}", shape, dt)

        XB = sb("XB", [128, NKC, T], BF16)
        VEC = sb("VEC", [128, NV], F32)
        CM = sb("CM", [128, 4 * 128 + 64], F32)
        CB = sb("CB", [128, 128], BF16)
        WS = [sb(f"WS{i}", [128, 5632], BF16) for i in range(3)]
        ws_ctr = [0]

        S.dma("sp", VEC[:], vecs[:, :], [], ["VEC"], "c_vec")
        S.dma("sp", CM[:], cm[:, :], [], ["CM"], "c_cm")
        S.dma("sp", CB[:], cb[:, :], [], ["CB"], "c_cb")
        ONES_D = CM[:, 0:128]; ONES_H = CM[:, 128:256]; ROTP = CM[:, 256:384]
        TRIU = CM[0:64, 384:448]; TRIL = CM[0:64, 448:512]

        def vcol(name, i=0, n=1):
            return VEC[:, VC[name] + i: VC[name] + i + n]

        cast_jobs = []

        def cj(tokn, out, in_, nd):
            cast_jobs.append((tokn, out, in_, nd))

        def wset(l):
            return (WINs[l], WOUTs[l], WGs[l], WUs[l], WDs[l])

        for l in range(n_layers):
            j = l // 2
            win, wout, wg, wu, wd = wset(l)
            if l % 2 == 0:
                src = ab_w_in[j].rearrange("(kc p) (b c) -> b p kc c", p=128, c=256)
                for b in range(32):
                    cj(("win", l, b), win[b].rearrange("p (kc c) -> p kc c", c=256), src[b], 128)
                so = ab_w_out[j].rearrange("(kc p) (m c) -> m p kc c", p=128, c=128)
            else:
                src = cd_w_in[j].rearrange("(kc p) (b c) -> b p kc c", p=128, c=256)
                for b in range(10):
                    cj(("win", l, b), win[b].rearrange("p (kc c) -> p kc c", c=256), src[b], 128)
                cj(("wp", l), WPs_[j][:].rearrange("p (g kc d) -> p g kc d", g=4, kc=2),
                   pool_w[j].rearrange("g (kc p) d -> p g kc d", p=128), 64)
                so = cd_w_out[j].rearrange("(kc p) (m c) -> m p kc c", p=128, c=128)
            for m in range(16):
                cj(("wout", l), wout[m].rearrange("p (kc c) -> p kc c", c=128), so[m], 128)
            sg = w_gate[l].rearrange("(kc p) (b c) -> b p kc c", p=128, c=256)
            su = w_up[l].rearrange("(kc p) (b c) -> b p kc c", p=128, c=256)
            for b in range(22):
                cj(("wg", l), wg[b].rearrange("p (kc c) -> p kc c", c=256), sg[b], 128)
                cj(("wg", l), wu[b].rearrange("p (kc c) -> p kc c", c=256), su[b], 128)
            sd = w_down[l].rearrange("(kc p) (m c) -> m p kc c", p=128, c=128)
            for m in range(16):
                for hh in range(2):
                    cj(("wd", l), wd[m].rearrange("p (kc c) -> p kc c", c=128)[:, hh * 22:(hh + 1) * 22, :],
                       sd[m][:, hh * 22:(hh + 1) * 22, :], 176)
        S._newsem("cast"); S.nobar.add("cast")
        pool = S.eng["pool"]
        inflight = []; tot = 0
        ends = {}
        for i, (tokn, out, in_, nd) in enumerate(cast_jobs):
            while inflight and sum(x[1] for x in inflight) + nd > 850:
                v0, _ = inflight.pop(0)
                pool["h"].wait_ge(S.sems["cast"], v0)
            ins = None
            ins = pool["h"].dma_start(out=out, in_=in_)
            tot += 16
            ins.then_inc(S.sems["cast"], 16)
            inflight.append((tot, nd))
            ends[tokn] = tot
        S.latest["cast"] = tot
        cast_total = tot
        for tokn, v in ends.items():
            vv = min(cast_total, v + 16 * 8)
            S.tok.setdefault(("wc",) + tokn, ({}, {}))[0]["cast"] = vv

        def load_w(scr_ap, nel, tokc, shape3):
            i = ws_ctr[0] % 3; ws_ctr[0] += 1
            S.dma("sp", WS[i][:, 0:nel], scr_ap, [tokc], [("WS", i)], f"ws{i}")
            kc, c = shape3
            return WS[i][:, 0:nel].rearrange("p (kc c) -> p kc c", c=c), ("WS", i)

        evac_ctr = [0]

        def evac(out, in_, reads, writes, eng=None):
            if eng is None:
                eng = "act" if evac_ctr[0] % 2 == 0 else "dve"; evac_ctr[0] += 1
            if eng == "act":
                S.op("act", lambda e: e.activation(out=out, in_=in_, func=AF.Copy), reads, writes)
            else:
                S.op("dve", lambda e: e.tensor_copy(out=out, in_=in_), reads, writes)

        def mm(out, lhsT, rhs, start, stop, reads, writes, inc):
            S.op("pe", lambda e: e.matmul(out, lhsT, rhs, start=start, stop=stop), reads, writes, inc=inc)

        def phase_proj(l, nblk, spec):
            win = wset(l)[0]
            with ExitStack() as ps:
                PSB = [ps.enter_context(nc_psum(f"pp{i}", [128, 512], F32)) for i in range(4)]
                SGB = [ps.enter_context(nc_sbuf(f"sgb{i}", [128, T], BF16)) for i in range(2)]
                SGF = [ps.enter_context(nc_sbuf(f"sgf{i}", [128, T], F32)) for i in range(2)]
                SGT = [ps.enter_context(nc_sbuf(f"sgt{i}", [128, 16, 256], BF16)) for i in range(2)]
                pc = 0; sc = {"b": 0, "f": 0, "t": 0}
                nxt = load_w(win[0], 4096, ("wc", "win", l, 0), (16, 256))
                for b in range(nblk):
                    W, wtok = nxt
                    if b + 1 < nblk:
                        nxt = load_w(win[b + 1], 4096, ("wc", "win", l, b + 1), (16, 256))
                    mode, dt, dst, off = spec(b)
                    if mode == "fm":
                        for ci in range(2):
                            kk = "b" if dt == BF16 else "f"
                            si = sc[kk] % 2; sc[kk] += 1
                            stg = (SGB if dt == BF16 else SGF)[si]; stok = ("stg", kk, si)
                            for n in range(NB):
                                pt = PSB[pc % 4]; ptok = ("pp", pc % 4); pc += 1
                                for kc in range(NKC):
                                    mm(pt[:], W[:, kc, ci * 128:(ci + 1) * 128], XB[:, kc, n * BLK:(n + 1) * BLK],
                                       kc == 0, kc == NKC - 1, [wtok, ("XB", n)], [ptok], kc == NKC - 1)
                                evac(stg[:, n * BLK:(n + 1) * BLK], pt[:], [ptok], [stok])
                            r0 = off + ci * 128
                            S.dma("sp", dst[r0:r0 + 128, :], stg[:], [stok], [("scr", id(dst))], f"st{kk}{si}")
                    else:
                        si = sc["t"] % 2; sc["t"] += 1
                        stg = SGT[si]; stok = ("stg", "t", si)
                        for tt in range(16):
                            pt = PSB[pc % 4]; ptok = ("pp", pc % 4); pc += 1
                            for kc in range(NKC):
                                mm(pt[:, 0:256], XB[:, kc, tt * 128:(tt + 1) * 128], W[:, kc, :],
                                   kc == 0, kc == NKC - 1, [wtok, ("XB", tt // 4)], [ptok], kc == NKC - 1)
                            evac(stg[:, tt, :], pt[:, 0:256], [ptok], [stok])
                        S.dma("sp", dst.rearrange("(tt p) f -> p tt f", p=128)[:, :, off:off + 256], stg[:],
                              [stok], [("scr", id(dst))], f"stt{si}")
                S.barrier()

        def phase_proj_ln(l, src, nkc, wscr, wtokc, xres, gname, bname, final_out=None):
            with ExitStack() as ps:
                PSB = [ps.enter_context(nc_psum(f"lp{i}", [128, 512], F32)) for i in range(4)]
                PMEAN = ps.enter_context(nc_psum("lpm", [128, 512], F32))
                PMSQ = ps.enter_context(nc_psum("lpq", [128, 512], F32))
                SRC = ps.enter_context(nc_sbuf("lsrc", [128, nkc, BLK], BF16))
                ZB = ps.enter_context(nc_sbuf("lzb", [128, NKC, BLK], F32))
                XRS = [ps.enter_context(nc_sbuf(f"lxr{i}", [128, BLK], F32)) for i in range(3)]
                SQ = [ps.enter_context(nc_sbuf(f"lsq{i}", [128, BLK], F32)) for i in range(2)]
                TMP = [ps.enter_context(nc_sbuf(f"ltm{i}", [128, BLK], F32)) for i in range(2)]
                OST = [ps.enter_context(nc_sbuf(f"los{i}", [128, BLK], F32)) for i in range(2)]
                MEAN = ps.enter_context(nc_sbuf("lmean", [128, BLK], F32))
                M2 = ps.enter_context(nc_sbuf("lm2", [128, BLK], F32))
                RSTD = ps.enter_context(nc_sbuf("lrstd", [128, BLK], F32))
                pc = 0; xc = 0; oc = 0
                nel = nkc * 128
                for n in range(NB):
                    cols = slice(n * BLK, (n + 1) * BLK)
                    q4 = nkc // 4
                    for qi in range(4):
                        S.dma("sp", SRC[:, qi * q4:(qi + 1) * q4, :],
                              src.rearrange("(kc p) t -> p kc t", p=128)[:, qi * q4:(qi + 1) * q4, cols],
                              [("scr", id(src))], ["lsrc"], f"lsrc{qi}")
                    nxt = load_w(wscr[0], nel, wtokc, (nkc, 128))
                    for m in range(NKC):
                        W, wtok = nxt
                        if m + 1 < NKC:
                            nxt = load_w(wscr[m + 1], nel, wtokc, (nkc, 128))
                        xi = xc % 3; xc += 1
                        S.dma("sp", XRS[xi][:], xres[m * 128:(m + 1) * 128, cols], [("xr", n)], [("lxr", xi)], f"lxr{xi}")
                        pt = PSB[pc % 4]; ptok = ("lp", pc % 4); pc += 1
                        for kc in range(nkc):
                            mm(pt[:], W[:, kc, :], SRC[:, kc, :], kc == 0, kc == nkc - 1, [wtok, "lsrc"], [ptok], kc == nkc - 1)
                        S.op("dve", lambda e: e.scalar_tensor_tensor(out=ZB[:, m, :], in0=XRS[xi][:], scalar=float(ALPHA),
                                                                      in1=pt[:], op0=ALU.mult, op1=ALU.add),
                             [ptok, ("lxr", xi)], [("zb", m)])
                    for m in range(NKC):
                        si = m % 2
                        S.op("act", lambda e: e.activation(out=SQ[si][:], in_=ZB[:, m, :], func=AF.Square),
                             [("zb", m)], [("lsq", si)])
                        mm(PMEAN[:], ONES_D, ZB[:, m, :], m == 0, m == NKC - 1, [("zb", m), "CM"], ["lpm"], m == NKC - 1)
                        mm(PMSQ[:], ONES_D, SQ[si][:], m == 0, m == NKC - 1, [("lsq", si), "CM"], ["lpq"], True)
                    S.op("act", lambda e: e.activation(out=MEAN[:], in_=PMEAN[:], func=AF.Copy), ["lpm"], ["lmean"])
                    S.op("dve", lambda e: e.tensor_tensor(out=M2[:], in0=MEAN[:], in1=MEAN[:], op=ALU.mult), ["lmean"], ["lm2"])
                    S.op("dve", lambda e: e.tensor_tensor(out=M2[:], in0=PMSQ[:], in1=M2[:], op=ALU.subtract), ["lpq", "lm2"], ["lm2"])
                    S.op("act", lambda e: e.activation(out=M2[:], in_=M2[:], func=AF.Sqrt, bias=vcol("eps_ln"), scale=1.0),
                         ["lm2", "VEC"], ["lm2"])
                    S.op("dve", lambda e: e.reciprocal(out=RSTD[:], in_=M2[:]), ["lm2"], ["lrstd"])
                    for m in range(NKC):
                        ti = m % 2
                        S.op("dve", lambda e: e.tensor_tensor(out=TMP[ti][:], in0=ZB[:, m, :], in1=MEAN[:], op=ALU.subtract),
                             [("zb", m), "lmean"], [("ltm", ti)])
                        S.op("dve", lambda e: e.tensor_tensor(out=TMP[ti][:], in0=TMP[ti][:], in1=RSTD[:], op=ALU.mult),
                             [("ltm", ti), "lrstd"], [("ltm", ti)])
                        oi = oc % 2; oc += 1
                        S.op("act", lambda e: e.activation(out=OST[oi][:], in_=TMP[ti][:], func=AF.Identity,
                                                           scale=vcol(gname, l * 16 + m), bias=vcol(bname, l * 16 + m)),
                             [("ltm", ti), "VEC"], [("los", oi)])
                        if final_out is None:
                            S.op("act", lambda e: e.activation(out=XB[:, m, cols], in_=OST[oi][:], func=AF.Copy),
                                 [("los", oi)], [("XB", n)])
                            S.dma("sp", XR[m * 128:(m + 1) * 128, cols], OST[oi][:], [("los", oi)], [("xr", n)], f"los{oi}")
                        else:
                            S.dma("sp", final_out[m * 128:(m + 1) * 128, cols], OST[oi][:], [("los", oi)], [("out", n)], f"los{oi}")
                S.barrier()

        def phase_ffn1(l):
            wg, wu = wset(l)[2], wset(l)[3]
            with ExitStack() as ps:
                PG = [ps.enter_context(nc_psum(f"fg{i}", [128, 512], F32)) for i in range(2)]
                PU = [ps.enter_context(nc_psum(f"fu{i}", [128, 512], F32)) for i in range(2)]
                SGT = [ps.enter_context(nc_sbuf(f"fsg{i}", [128, BLK], F32)) for i in range(2)]
                HST = [ps.enter_context(nc_sbuf(f"fhs{i}", [128, T], BF16)) for i in range(2)]
                pc = 0; hc = 0

                def ld(sidx):
                    b, ci = sidx // 2, sidx % 2
                    i = ws_ctr[0] % 3; ws_ctr[0] += 1
                    gsrc = wg[b].rearrange("p (kc c) -> p kc c", c=256)[:, :, ci * 128:(ci + 1) * 128]
                    usrc = wu[b].rearrange("p (kc c) -> p kc c", c=256)[:, :, ci * 128:(ci + 1) * 128]
                    S.dma("sp", WS[i][:, 0:2048].rearrange("p (kc c) -> p kc c", c=128), gsrc, [("wc", "wg", l)], [("WS", i)], f"ws{i}")
                    S.dma("sp", WS[i][:, 2048:4096].rearrange("p (kc c) -> p kc c", c=128), usrc, [("wc", "wg", l)], [("WS", i)], f"wsu{i}")
                    return (WS[i][:, 0:2048].rearrange("p (kc c) -> p kc c", c=128),
                            WS[i][:, 2048:4096].rearrange("p (kc c) -> p kc c", c=128), ("WS", i))
                pend = [ld(0), ld(1)]
                for sidx in range(44):
                    if sidx + 2 < 44:
                        pend.append(ld(sidx + 2))
                    Wg, Wu, wtok = pend.pop(0)
                    hi = hc % 2; hc += 1
                    for n in range(NB):
                        pi = pc % 2; pc += 1
                        for kc in range(NKC):
                            mm(PG[pi][:], Wg[:, kc, :], XB[:, kc, n * BLK:(n + 1) * BLK],
                               kc == 0, kc == NKC - 1, [wtok, ("XB", n)], [("fg", pi)], kc == NKC - 1)
                        for kc in range(NKC):
                            mm(PU[pi][:], Wu[:, kc, :], XB[:, kc, n * BLK:(n + 1) * BLK],
                               kc == 0, kc == NKC - 1, [wtok, ("XB", n)], [("fu", pi)], kc == NKC - 1)
                        S.op("act", lambda e: e.activation(out=SGT[pi][:], in_=PG[pi][:], func=AF.Silu),
                             [("fg", pi)], [("fsg", pi)])
                        S.op("dve", lambda e: e.tensor_tensor(out=HST[hi][:, n * BLK:(n + 1) * BLK], in0=SGT[pi][:],
                                                              in1=PU[pi][:], op=ALU.mult),
                             [("fsg", pi), ("fu", pi)], [("fhs", hi)])
                    r0 = sidx * 128
                    S.dma("sp", HT[r0:r0 + 128, :], HST[hi][:], [("fhs", hi)], [("scr", id(HT))], f"fhs{hi}")
                S.barrier()

        def phase_na(l):
            j = l // 2
            with ExitStack() as ps:
                PS_S = [ps.enter_context(nc_psum(f"nas{i}", [64, 512], F32)) for i in range(2)]
                PS_T = [ps.enter_context(nc_psum(f"nat{i}", [64, 1024], BF16))[:, 0:512].rearrange("p (a b) -> p a b", b=64) for i in range(2)]
                PS_O = [ps.enter_context(nc_psum(f"nao{i}", [128, 512], F32))[:, 0:64] for i in range(2)]
                QT = [ps.enter_context(nc_sbuf(f"naq{i}", [128, T], BF16)) for i in range(2)]
                KT = [ps.enter_context(nc_sbuf(f"nak{i}", [128, T], BF16)) for i in range(2)]
                VV = [ps.enter_context(nc_sbuf(f"nav{i}", [64, 32, 128], BF16)) for i in range(2)]
                BI = [ps.enter_context(nc_sbuf(f"nab{i}", [64, 8 * 512], F32)) for i in range(2)]
                SS = [ps.enter_context(nc_sbuf(f"nass{i}", [64, 512], F32)) for i in range(2)]
                PP = [ps.enter_context(nc_sbuf(f"napp{i}", [64, 512], BF16)) for i in range(2)]
                PN = [ps.enter_context(nc_sbuf(f"napn{i}", [64, 512], BF16)) for i in range(2)]
                PT = [ps.enter_context(nc_sbuf(f"napt{i}", [64, 512], BF16)) for i in range(2)]
                ST = [ps.enter_context(nc_sbuf(f"nast{i}", [64, 4], F32)) for i in range(2)]
                YO = [ps.enter_context(nc_sbuf(f"nayo{i}", [128, T], BF16)) for i in range(2)]
                iters = [(h, r) for h in range(8) for r in range(32)]

                def load_head(h):
                    hb = h % 2
                    S.dma("sp", QT[hb][:], AQK[h * 128:(h + 1) * 128, :], [("scr", id(AQK))], [("naq", hb)], f"naq{hb}")
                    S.dma("sp", KT[hb][:], AQK[1024 + h * 128:1024 + (h + 1) * 128, :], [("scr", id(AQK))], [("nak", hb)], f"nak{hb}")
                    S.dma("sp", VV[hb][:], AV.rearrange("(r p) f -> p r f", p=64)[:, :, h * 128:(h + 1) * 128],
                          [("scr", id(AV))], [("nav", hb)], f"nav{hb}")
                    S.dma("sp", BI[hb][:], na_bias[j, h], [], [("nab", hb)], f"nab{hb}")

                def front(i):
                    h, r = iters[i]; hb = h % 2; i2 = i % 2
                    if r == 0:
                        load_head(h)
                    r0 = min(max(r - 4, 0), 24)
                    var = r0 - r + 7
                    k0 = r0 * 64
                    S.op("pe", lambda e: e.matmul(PS_S[i2][:], QT[hb][:, r * 64:(r + 1) * 64], KT[hb][:, k0:k0 + 512],
                                                  start=True, stop=True),
                         [("naq", hb), ("nak", hb)], [("nas", i2)])
                    S.op("dve", lambda e: e.scalar_tensor_tensor(out=SS[i2][:], in0=PS_S[i2][:], scalar=float(128 ** -0.5),
                                                                  in1=BI[hb][:, var * 512:(var + 1) * 512],
                                                                  op0=ALU.mult, op1=ALU.add),
                         [("nas", i2), ("nab", hb)], [("nass", i2)])
                    S.op("dve", lambda e: e.tensor_reduce(out=ST[i2][:, 0:1], in_=SS[i2][:], axis=AX.X, op=ALU.max, negate=True),
                         [("nass", i2)], [("nast", i2)])
                    S.op("act", lambda e: e.activation(out=PP[i2][:], in_=SS[i2][:], func=AF.Exp, bias=ST[i2][:, 0:1],
                                                       scale=1.0, accum_out=ST[i2][:, 1:2]),
                         [("nass", i2), ("nast", i2)], [("napp", i2), ("nast2", i2)])
                    S.op("dve", lambda e: e.reciprocal(out=ST[i2][:, 2:3], in_=ST[i2][:, 1:2]), [("nast2", i2)], [("nast3", i2)])
                    S.op("act", lambda e: e.activation(out=PN[i2][:], in_=PP[i2][:], func=AF.Identity, scale=ST[i2][:, 2:3]),
                         [("napp", i2), ("nast3", i2)], [("napn", i2)])

                def back(i):
                    h, r = iters[i]; hb = h % 2; i2 = i % 2
                    r0 = min(max(r - 4, 0), 24)
                    for jj in range(8):
                        S.op("pe", lambda e: e.transpose(PS_T[i2][:, jj, :], PN[i2][:, jj * 64:(jj + 1) * 64], CB[0:64, 0:64]),
                             [("napn", i2), "CB"], [("nat", i2)], inc=(jj == 7))
                    S.op("act", lambda e: e.activation(out=PT[i2][:], in_=PS_T[i2][:].rearrange("p a b -> p (a b)"), func=AF.Copy),
                         [("nat", i2)], [("napt", i2)])
                    for jj in range(8):
                        S.op("pe", lambda e: e.matmul(PS_O[i2][:], VV[hb][:, r0 + jj, :], PT[i2][:, jj * 64:(jj + 1) * 64],
                                                      start=(jj == 0), stop=(jj == 7)),
                             [("nav", hb), ("napt", i2)], [("nao", i2)], inc=(jj == 7))
                    S.op("dve", lambda e: e.tensor_copy(out=YO[hb][:, r * 64:(r + 1) * 64], in_=PS_O[i2][:]),
                         [("nao", i2)], [("nayo", hb)])
                    if r == 31:
                        S.dma("sp", YT[h * 128:(h + 1) * 128, :], YO[hb][:], [("nayo", hb)], [("scr", id(YT))], f"nayo{hb}")

                front(0)
                for i in range(len(iters)):
                    if i + 1 < len(iters):
                        front(i + 1)
                    back(i)
                S.barrier()

        def phase_hgrn(l):
            j = l // 2
            with ExitStack() as ps:
                P_A = [ps.enter_context(nc_psum(f"hga{i}", [64, 512], F32))[:, 0:64] for i in range(2)]
                P_K = [ps.enter_context(nc_psum(f"hgk{i}", [64, 1024], BF16))[:, 0:128] for i in range(2)]
                P_O = [ps.enter_context(nc_psum(f"hgo{i}", [128, 512], F32))[:, 0:64] for i in range(2)]
                P_SF = [ps.enter_context(nc_psum(f"hgs{i}", [128, 512], F32)) for i in range(2)]
                P_S = [x[:, 0:128] for x in P_SF]
                HQs = ps.enter_context(nc_sbuf("hq", [128, T], BF16))
                HGs = ps.enter_context(nc_sbuf("hg", [128, T], BF16))
                VVs = ps.enter_context(nc_sbuf("hv", [64, 32, 128], BF16))
                SQs = ps.enter_context(nc_sbuf("hsq", [128, T], F32))
                Z = ps.enter_context(nc_sbuf("hz", [128, T], F32))
                Bk = ps.enter_context(nc_sbuf("hbk", [128, T], F32))
                Cg = ps.enter_context(nc_sbuf("hcg", [128, T], F32))
                EA = [ps.enter_context(nc_sbuf(f"hea{d}", [128, T], F32)) for d in range(2)]
                QG = [WS[0][:, d * T:(d + 1) * T] for d in range(2)]
                KG = [WS[1][:, d * T:(d + 1) * T] for d in range(2)]
                KD = [ps.enter_context(nc_sbuf(f"hkd{d}", [128, T], BF16)) for d in range(2)]
                O32 = ps.enter_context(nc_sbuf("ho32", [128, T], F32))
                S32 = [ps.enter_context(nc_sbuf(f"hs32{d}", [128, 128], F32)) for d in range(2)]
                SBF = [ps.enter_context(nc_sbuf(f"hsbf{d}", [128, 128], BF16)) for d in range(2)]
                ATM = [ps.enter_context(nc_sbuf(f"hatm{d}", [64, 64], BF16)) for d in range(2)]
                KDT = [ps.enter_context(nc_sbuf(f"hkdt{d}", [64, 128], BF16)) for d in range(2)]
                LB = ps.enter_context(nc_sbuf("hlb", [128, 4], F32))
                SMK = ps.enter_context(nc_sbuf("hsmk", [128, T], BF16))
                YO = KD[0]
                RS = ps.enter_context(nc_sbuf("hrs", [128, BLK], F32))
                S.dma("sp", SMK[:], scanmask[:, :], [], ["hsmk"], "hsmk")
                for h in range(8):
                    S.dma("sp", HQs[:], HQ[h * 128:(h + 1) * 128, :], [("scr", id(HQ))], ["hq"], "hq")
                    S.dma("sp", HGs[:], HG[h * 128:(h + 1) * 128, :], [("scr", id(HG))], ["hg"], "hg")
                    S.dma("sp", VVs[:], HI.rearrange("(r p) f -> p r f", p=64)[:, :, h * 128:(h + 1) * 128],
                          [("scr", id(HI))], ["hv"], "hv")
                    if j == 0:
                        S.op("dve", lambda e: e.tensor_copy(out=LB[:, 0:1], in_=vcol("zero")), ["VEC"], ["hlb"])
                    else:
                        S.op("dve", lambda e: e.tensor_tensor(out=LB[:, 2:3], in0=vcol("lb_logit", 8 + h), in1=vcol("lb_logit", h),
                                                              op=ALU.subtract), ["VEC"], ["hlb2"])
                        S.op("act", lambda e: e.activation(out=LB[:, 0:1], in_=LB[:, 2:3], func=AF.Sigmoid), ["hlb2"], ["hlb"])
                    S.op("dve", lambda e: e.tensor_scalar(out=LB[:, 1:2], in0=LB[:, 0:1], scalar1=-1.0, scalar2=1.0,
                                                          op0=ALU.mult, op1=ALU.add), ["hlb"], ["hlb1"])
                    S.op("act", lambda e: e.activation(out=SQs[:], in_=HQs[:], func=AF.Silu), ["hq"], ["hsq"])
                    for d in range(2):
                        S.dma("sp", Z[:], HF[d * 1024 + h * 128: d * 1024 + (h + 1) * 128, :], [("scr", id(HF))], ["hz"], "hz")
                        S.op("act", lambda e: e.activation(out=EA[d][:], in_=Z[:], func=AF.Sigmoid), ["hz"], [("hea", d)])
                        S.op("dve", lambda e: e.tensor_scalar(out=EA[d][:], in0=EA[d][:], scalar1=LB[:, 1:2], scalar2=LB[:, 0:1],
                                                              op0=ALU.mult, op1=ALU.add), [("hea", d), "hlb", "hlb1"], [("hea", d)])
                        S.op("dve", lambda e: e.tensor_scalar(out=Bk[:], in0=EA[d][:], scalar1=-1.0, scalar2=1.0,
                                                              op0=ALU.mult, op1=ALU.add), [("hea", d)], ["hbk"])
                        S.op("act", lambda e: e.activation(out=EA[d][:], in_=EA[d][:], func=AF.Ln), [("hea", d)], [("hea", d)])
                        if d == 0:
                            S.op("dve", lambda e: e.tensor_tensor_scan(out=Cg[:], data0=SMK[:], data1=EA[d][:], initial=0.0,
                                                                       op0=ALU.mult, op1=ALU.add), [("hea", d), "hsmk"], ["hcg"])
                        else:
                            S.op("dve", lambda e: e.tensor_tensor_scan(out=Cg[:, ::-1], data0=SMK[:], data1=EA[d][:, ::-1], initial=0.0,
                                                                       op0=ALU.mult, op1=ALU.add), [("hea", d), "hsmk"], ["hcg"])
                        S.op("act", lambda e: e.activation(out=EA[d][:], in_=Cg[:], func=AF.Exp), ["hcg"], [("hea", d)])
                        S.op("act", lambda e: e.activation(out=Z[:], in_=Cg[:], func=AF.Exp, scale=-1.0), ["hcg"], ["hz"])
                        S.op("dve", lambda e: e.tensor_tensor(out=QG[d][:], in0=SQs[:], in1=EA[d][:], op=ALU.mult),
                             ["hsq", ("hea", d)], [("hqg", d)])
                        S.op("dve", lambda e: e.tensor_tensor(out=KG[d][:], in0=Bk[:], in1=Z[:], op=ALU.mult),
                             ["hbk", "hz"], [("hkg", d)])
                        e3 = EA[d][:].rearrange("p (c t) -> p c t", t=64)
                        eend = e3[:, :, 63:64] if d == 0 else e3[:, :, 0:1]
                        S.op("dve", lambda e: e.tensor_tensor(out=KD[d][:].rearrange("p (c t) -> p c t", t=64),
                                                              in0=KG[d][:].rearrange("p (c t) -> p c t", t=64),
                                                              in1=eend.to_broadcast([128, 32, 64]), op=ALU.mult),
                             [("hkg", d), ("hea", d)], [("hkd", d)])
                    for step in range(32):
                        for d in range(2):
                            c = step if d == 0 else 31 - step
                            tc_ = slice(c * 64, (c + 1) * 64)
                            eidx = c * 64 + 63 if d == 0 else c * 64
                            tri = TRIU if d == 0 else TRIL
                            S.op("pe", lambda e: e.matmul(P_A[d][:], KG[d][:, tc_], QG[d][:, tc_], start=True, stop=True),
                                 [("hkg", d), ("hqg", d)], [("hga", d)])
                            S.op("dve", lambda e: e.tensor_tensor(out=ATM[d][:], in0=P_A[d][:], in1=tri, op=ALU.mult),
                                 [("hga", d), "CM"], [("hatm", d)])
                            S.op("pe", lambda e: e.transpose(P_K[d][:], KD[d][:, tc_], CB[:, :]), [("hkd", d), "CB"], [("hgk", d)])
                            S.op("act", lambda e: e.activation(out=KDT[d][:], in_=P_K[d][:], func=AF.Copy), [("hgk", d)], [("hkdt", d)])
                            S.op("pe", lambda e: e.matmul(P_O[d][:], VVs[:, c, :], ATM[d][:], start=True, stop=(step == 0)),
                                 ["hv", ("hatm", d)], [("hgo", d)], inc=(step == 0))
                            if step > 0:
                                S.op("pe", lambda e: e.matmul(P_O[d][:], SBF[d][:], QG[d][:, tc_], start=False, stop=True),
                                     [("hsbf", d), ("hqg", d)], [("hgo", d)])
                            if step < 16:
                                S.op("act", lambda e: e.activation(out=O32[:, tc_], in_=P_O[d][:], func=AF.Copy),
                                     [("hgo", d)], [("ho32", c)])
                            else:
                                S.op("dve", lambda e: e.tensor_tensor(out=O32[:, tc_], in0=O32[:, tc_], in1=P_O[d][:], op=ALU.add),
                                     [("hgo", d), ("ho32", c)], [("ho32", c)])
                            if step < 31:
                                S.op("pe", lambda e: e.matmul(P_S[d][:], KDT[d][:], VVs[:, c, :], start=True, stop=True),
                                     [("hkdt", d), "hv"], [("hgs", d)])
                                if step == 0:
                                    S.op("dve", lambda e: e.tensor_copy(out=S32[d][:], in_=P_S[d][:]), [("hgs", d)], [("hs32", d)])
                                else:
                                    S.op("dve", lambda e: e.scalar_tensor_tensor(out=S32[d][:], in0=S32[d][:], scalar=EA[d][:, eidx:eidx + 1],
                                                                                  in1=P_S[d][:], op0=ALU.mult, op1=ALU.add),
                                         [("hgs", d), ("hs32", d), ("hea", d)], [("hs32", d)])
                                S.op("act", lambda e: e.activation(out=SBF[d][:], in_=S32[d][:], func=AF.Copy), [("hs32", d)], [("hsbf", d)])
                    allo = [("ho32", c) for c in range(32)]
                    S.op("act", lambda e: e.activation(out=Z[:], in_=O32[:], func=AF.Square), allo, ["hz"])
                    S.op("act", lambda e: e.activation(out=SQs[:], in_=HGs[:], func=AF.Silu), ["hg"], ["hsq"])
                    for n in range(NB):
                        cols = slice(n * BLK, (n + 1) * BLK)
                        S.op("pe", lambda e: e.matmul(P_SF[0][:], ONES_H, Z[:, cols], start=True, stop=True), ["hz", "CM"], [("hgs", 0)])
                        S.op("act", lambda e: e.activation(out=RS[:], in_=P_SF[0][:], func=AF.Sqrt, bias=vcol("eps_rms"), scale=1.0),
                             [("hgs", 0), "VEC"], ["hrs"])
                        S.op("dve", lambda e: e.reciprocal(out=RS[:], in_=RS[:]), ["hrs"], ["hrs"])
                        S.op("dve", lambda e: e.tensor_tensor(out=RS[:], in0=O32[:, cols], in1=RS[:], op=ALU.mult), allo + ["hrs"], ["hrs"])
                        S.op("dve", lambda e: e.scalar_tensor_tensor(out=YO[:, cols], in0=RS[:], scalar=vcol("hg_norm_w", j * 8 + h),
                                                                      in1=SQs[:, cols], op0=ALU.mult, op1=ALU.mult),
                             ["hrs", "hsq", "VEC"], [("hkd", 0)])
                    S.dma("sp", YT[1024 + h * 128:1024 + (h + 1) * 128, :], YO[:], [("hkd", 0)], [("scr", id(YT))], "hyo")
                S.barrier()

        def phase_pool(l):
            j = l // 2
            PADL = 8
            with ExitStack() as ps:
                PSB = [ps.enter_context(nc_psum(f"plp{i}", [128, 512], F32)) for i in range(2)]
                XP = [ps.enter_context(nc_sbuf(f"plx{i}", [128, T + 16], F32)) for i in range(2)]
                SA = ps.enter_context(nc_sbuf("plsa", [128, T + 16], F32))
                SBt = ps.enter_context(nc_sbuf("plsb", [128, T + 16], F32))
                PL = [ps.enter_context(nc_sbuf(f"plp{i}", [128, T], BF16)) for i in range(2)]
                PINV = ps.enter_context(nc_sbuf("plinv", [128, T], F32))
                WPs = ps.enter_context(nc_sbuf("plw", [128, 4, 2, 256], BF16))
                YO = [ps.enter_context(nc_sbuf(f"plyo{i}", [128, T], BF16)) for i in range(2)]
                S.dma("sp", WPs[:].rearrange("p g k d -> p (g k d)"), WPs_[j][:, :], [("wc", "wp", l)], ["plw"], "plw")
                for i in range(2):
                    S.op("dve", lambda e: e.memset(XP[i][:], 0.0), [], [("plx", i)])
                S.op("dve", lambda e: e.memset(SA[:], 0.0), [], ["plsa"])
                S.op("dve", lambda e: e.memset(SBt[:], 0.0), [], ["plsb"])
                yc = 0
                for g in range(4):
                    w = (2, 4, 8, 16)[g]
                    S.dma("sp", PINV[:], poolinv[:, g * T:(g + 1) * T], [], ["plinv"], "plinv")
                    for kc in range(2):
                        ch = g * 2 + kc
                        S.dma("sp", XP[kc][:, PADL:PADL + T], CX[ch * 128:(ch + 1) * 128, :], [("scr", id(CX))], [("plx", kc)], f"plx{kc}")
                        L = T + 16
                        cur, curtok, ln_ = XP[kc], ("plx", kc), L
                        bufs = [(SA, "plsa"), (SBt, "plsb")]
                        bi = 0; span = 1
                        while span < w:
                            dst, dtok = bufs[bi % 2]; bi += 1
                            nl = ln_ - span
                            S.op("dve", lambda e: e.tensor_tensor(out=dst[:, 0:nl], in0=cur[:, 0:nl], in1=cur[:, span:span + nl], op=ALU.add),
                                 [curtok], [dtok])
                            cur, curtok, ln_ = dst, dtok, nl
                            span *= 2
                        o0 = PADL - w // 2
                        dst, dtok = bufs[bi % 2]
                        S.op("dve", lambda e: e.tensor_tensor(out=dst[:, 0:T], in0=cur[:, o0:o0 + T], in1=PINV[:], op=ALU.mult),
                             [curtok, "plinv"], [dtok])
                        S.op("dve", lambda e: e.tensor_tensor(out=PL[kc][:], in0=dst[:, 0:T], in1=XP[kc][:, PADL:PADL + T], op=ALU.subtract),
                             [dtok, ("plx", kc)], [("plpl", kc)])
                    for dc in range(2):
                        yi = yc % 2; yc += 1
                        for n in range(NB):
                            pi = n % 2
                            for kc in range(2):
                                mm(PSB[pi][:], WPs[:, g, kc, dc * 128:(dc + 1) * 128], PL[kc][:, n * BLK:(n + 1) * BLK],
                                   kc == 0, kc == 1, ["plw", ("plpl", kc)], [("plps", pi)], kc == 1)
                            S.op("act", lambda e: e.activation(out=YO[yi][:, n * BLK:(n + 1) * BLK], in_=PSB[pi][:], func=AF.Identity,
                                                               scale=vcol("pool_scale", j * 8 + g * 2 + dc)),
                                 [("plps", pi), "VEC"], [("plyo", yi)])
                        r0 = g * 256 + dc * 128
                        S.dma("sp", YT[r0:r0 + 128, :], YO[yi][:], [("plyo", yi)], [("scr", id(YT))], f"plyo{yi}")
                S.barrier()

        def phase_gqa(l):
            j = l // 2
            scale = float(128 ** -0.5)
            with ExitStack() as ps:
                PS_S = ps.enter_context(nc_psum("gqs", [128, T], F32))
                PS_T = ps.enter_context(nc_psum("gqt", [128, 2048], BF16)).rearrange("p (a b) -> p a b", b=128)
                PS_O = ps.enter_context(nc_psum("gqo", [128, 512], F32))
                PS_X = ps.enter_context(nc_psum("gqx", [128, 512], F32))
                QTs = ps.enter_context(nc_sbuf("gq", [128, 8, T], BF16))
                KTs = ps.enter_context(nc_sbuf("gk", [128, 2, T], BF16))
                VVs = ps.enter_context(nc_sbuf("gv", [128, 16, 256], BF16))
                RC = ps.enter_context(nc_sbuf("grc", [128, T], F32))
                RSn = ps.enter_context(nc_sbuf("grs", [128, T], F32))
                ZZ = [ps.enter_context(nc_sbuf(f"gz{i}", [128, BLK], F32)) for i in range(2)]
                T1 = ps.enter_context(nc_sbuf("gt1", [128, BLK], F32))
                T2 = ps.enter_context(nc_sbuf("gt2", [128, BLK], F32))
                T3 = ps.enter_context(nc_sbuf("gt3", [128, BLK], F32))
                PB = [ps.enter_context(nc_sbuf(f"gp{i}", [128, T], BF16)) for i in range(2)]
                PTs = [ps.enter_context(nc_sbuf(f"gpt{i}", [128, 16, 128], BF16)) for i in range(2)]
                STt = [ps.enter_context(nc_sbuf(f"gst{i}", [128, 4], F32)) for i in range(2)]
                OB = [ps.enter_context(nc_sbuf(f"gob{i}", [128, 128], BF16)) for i in range(2)]
                YO = [ps.enter_context(nc_sbuf(f"gyo{i}", [128, T], BF16)) for i in range(2)]
                S.dma("sp", RC[:], ropeC[:, :], [], ["grc"], "grc")
                S.dma("sp", RSn[:], ropeS[:, :], [], ["grs"], "grs")
                S.dma("sp", VVs[:], DV.rearrange("(tt p) f -> p tt f", p=128), [("scr", id(DV))], ["gv"], "gv")
                zc = 0
                for hh in range(10):
                    gcol = vcol("q_norm", j) if hh < 8 else vcol("k_norm", j)
                    for n in range(NB):
                        cols = slice(n * BLK, (n + 1) * BLK)
                        zi = zc % 2; zc += 1
                        S.dma("sp", ZZ[zi][:], DQK[hh * 128:(hh + 1) * 128, cols], [("scr", id(DQK))], [("gz", zi)], f"gz{zi}")
                        S.op("act", lambda e: e.activation(out=T1[:], in_=ZZ[zi][:], func=AF.Square), [("gz", zi)], ["gt1"])
                        S.op("pe", lambda e: e.matmul(PS_O[:], ONES_H, T1[:], start=True, stop=True), ["gt1", "CM"], ["gqo"])
                        S.op("act", lambda e: e.activation(out=T2[:], in_=PS_O[:], func=AF.Sqrt, bias=vcol("eps_rms"), scale=1.0),
                             ["gqo", "VEC"], ["gt2"])
                        S.op("dve", lambda e: e.reciprocal(out=T2[:], in_=T2[:]), ["gt2"], ["gt2"])
                        S.op("dve", lambda e: e.scalar_tensor_tensor(out=T1[:], in0=ZZ[zi][:], scalar=gcol, in1=T2[:],
                                                                      op0=ALU.mult, op1=ALU.mult), [("gz", zi), "gt2", "VEC"], ["gt1"])
                        S.op("pe", lambda e: e.matmul(PS_X[:], ROTP, T1[:], start=True, stop=True), ["gt1", "CM"], ["gqx"])
                        S.op("dve", lambda e: e.tensor_tensor(out=T3[:], in0=PS_X[:], in1=RSn[:, cols], op=ALU.mult), ["gqx", "grs"], ["gt3"])
                        S.op("dve", lambda e: e.tensor_tensor(out=T2[:], in0=T1[:], in1=RC[:, cols], op=ALU.mult), ["gt1", "grc"], ["gt2"])
                        dst = QTs[:, hh, cols] if hh < 8 else KTs[:, hh - 8, cols]
                        S.op("dve", lambda e: e.tensor_tensor(out=dst, in0=T2[:], in1=T3[:], op=ALU.add), ["gt2", "gt3"], ["gqk"])
                iters = [(hq, qt) for hq in range(8) for qt in range(16)][:GQA_NIT]

                def stage_a(i):
                    hq, qt = iters[i]; kv = hq // 4; i2 = i % 2
                    qs = slice(qt * 128, (qt + 1) * 128)
                    for kb in range(4):
                        S.op("pe", lambda e: e.matmul(PS_S[:, kb * 512:(kb + 1) * 512], QTs[:, hq, qs], KTs[:, kv, kb * 512:(kb + 1) * 512],
                                                      start=True, stop=True), ["gqk"], ["gqs"], inc=(kb == 3))
                    S.op("dve", lambda e: e.tensor_reduce(out=STt[i2][:, 0:1], in_=PS_S[:], axis=AX.X, op=ALU.max, negate=True),
                         ["gqs"], [("gst", i2)])
                    S.op("dve", lambda e: e.tensor_scalar(out=STt[i2][:, 1:2], in0=STt[i2][:, 0:1], scalar1=scale, scalar2=None, op0=ALU.mult),
                         [("gst", i2)], [("gst1", i2)])
                    S.op("act", lambda e: e.activation(out=PB[i2][:], in_=PS_S[:], func=AF.Exp, bias=STt[i2][:, 1:2], scale=scale,
                                                       accum_out=STt[i2][:, 2:3]),
                         ["gqs", ("gst1", i2)], [("gp", i2), ("gst2", i2)])
                    S.op("dve", lambda e: e.reciprocal(out=STt[i2][:, 3:4], in_=STt[i2][:, 2:3]), [("gst2", i2)], [("gst3", i2)])

                def stage_t(i):
                    i2 = i % 2
                    for kc in range(16):
                        S.op("pe", lambda e: e.transpose(PS_T[:, kc, :], PB[i2][:, kc * 128:(kc + 1) * 128], CB[:, :]),
                             [("gp", i2), "CB"], ["gqt"], inc=(kc == 15))
                    S.op("act", lambda e: e.activation(out=PTs[i2][:, 0:8, :], in_=PS_T[:, 0:8, :], func=AF.Copy), ["gqt"], [("gpt", i2)])
                    S.op("dve", lambda e: e.tensor_copy(out=PTs[i2][:, 8:16, :], in_=PS_T[:, 8:16, :]), ["gqt"], [("gpt", i2)])

                def stage_v(i):
                    hq, qt = iters[i]; kv = hq // 4; i2 = i % 2; yb = hq % 2
                    qs = slice(qt * 128, (qt + 1) * 128)
                    for kc in range(16):
                        S.op("pe", lambda e: e.matmul(PS_O[:, 0:128], PTs[i2][:, kc, :], VVs[:, kc, kv * 128:(kv + 1) * 128],
                                                      start=(kc == 0), stop=(kc == 15)), [("gpt", i2), "gv"], ["gqo"], inc=(kc == 15))
                    S.op("act", lambda e: e.activation(out=OB[i2][:], in_=PS_O[:, 0:128], func=AF.Identity, scale=STt[i2][:, 3:4]),
                         ["gqo", ("gst3", i2)], [("gob", i2)])
                    S.op("pe", lambda e: e.transpose(PS_X[:, 0:64].bitcast(BF16), OB[i2][:], CB[:, :]), [("gob", i2), "CB"], ["gqx"])
                    S.op("dve", lambda e: e.tensor_copy(out=YO[yb][:, qs], in_=PS_X[:, 0:64].bitcast(BF16)), ["gqx"], [("gyo", yb)])
                    if qt == 15:
                        S.dma("sp", YT[1024 + hq * 128:1024 + (hq + 1) * 128, :], YO[yb][:], [("gyo", yb)], [("scr", id(YT))], f"gyo{yb}")

                if GQA_PIPE:
                    stage_a(0)
                    for i in range(len(iters)):
                        stage_t(i)
                        if i + 1 < len(iters):
                            stage_a(i + 1)
                        stage_v(i)
                else:
                    for i in range(len(iters)):
                        stage_a(i); stage_t(i); stage_v(i)
                S.barrier()

        with ExitStack() as ps:
            XS = [ps.enter_context(nc_sbuf(f"ixs{i}", [128, T], F32)) for i in range(2)]
            for m in range(NKC):
                i2 = m % 2
                S.dma("sp", XS[i2][:], xT[m * 128:(m + 1) * 128, :], [], [("ixs", i2)], f"ixs{i2}")
                evac(XB[:, m, :], XS[i2][:], [("ixs", i2)], [("XB", n) for n in range(NB)])
            S.barrier()

        def ab_spec(b):
            if b < 8: return ("fm", BF16, AQK, b * 256)
            if b < 12: return ("tm", BF16, AV, (b - 8) * 256)
            if b < 16: return ("fm", BF16, HQ, (b - 12) * 256)
            if b < 24: return ("fm", F32, HF, (b - 16) * 256)
            if b < 28: return ("tm", BF16, HI, (b - 24) * 256)
            return ("fm", BF16, HG, (b - 28) * 256)

        def cd_spec(b):
            if b < 4: return ("fm", F32, CX, b * 256)
            if b < 9: return ("fm", F32, DQK, (b - 4) * 256)
            return ("tm", BF16, DV, 0)

        if stop_after == 'only_gqa':
            phase_gqa(1)
        for l in range(n_layers):
            if l > 0:
                S.new_epoch()
            xres = xT if l == 0 else XR
            wout, wd = wset(l)[1], wset(l)[4]
            if l % 2 == 0:
                phase_proj(l, 32, ab_spec)
                if stop_after == ("proj", l): break
                phase_na(l)
                if stop_after == ("na", l): break
                phase_hgrn(l)
                if stop_after == ("hgrn", l): break
            else:
                phase_proj(l, 10, cd_spec)
                if stop_after == ("proj", l): break
                phase_pool(l)
                if stop_after == ("pool", l): break
                phase_gqa(l)
                if stop_after == ("gqa", l): break
            phase_proj_ln(l, YT, 16, wout, ("wc", "wout", l), xres, "ln_mix_g", "ln_mix_b")
            if stop_after == ("mixln", l): break
            phase_ffn1(l)
            if stop_after == ("ffn1", l): break
            last = (l == n_layers - 1)
            phase_proj_ln(l, HT, 44, wd, ("wc", "wd", l), XR, "ln_ffn_g", "ln_ffn_b", final_out=outT if last else None)

        S.barrier()
        fin = {k: v for k, v in S.latest.items()}
        S._wait(S.eng["sp"], fin)
    return nc


def _consts():
    cmat = np.zeros((128, 4 * 128 + 64), np.float32)
    cmat[:, 0:128] = 1.0 / 2048.0
    cmat[:, 128:256] = 1.0 / 128.0
    P = np.zeros((128, 128), np.float32)
    for i in range(64):
        P[2 * i + 1, 2 * i] = -1.0
        P[2 * i, 2 * i + 1] = 1.0
    cmat[:, 256:384] = P
    s = np.arange(64)[:, None]; t = np.arange(64)[None, :]
    cmat[0:64, 384:448] = (s <= t).astype(np.float32)
    cmat[0:64, 448:512] = (s >= t).astype(np.float32)
    ident = np.eye(128, dtype=np.float32).astype(ml_dtypes.bfloat16)
    smask = np.ones((128, T), np.float32); smask[:, ::64] = 0.0
    pos = np.arange(T)
    row = (pos // 64).astype(np.float32); col = (pos % 64).astype(np.float32)
    inv = (np.float32(10000.0) ** (-np.arange(32, dtype=np.float32) / np.float32(32))).astype(np.float32)
    ang = np.concatenate([row[:, None] * inv, col[:, None] * inv], axis=-1).astype(np.float32)
    cos = np.cos(ang).astype(np.float32); sin = np.sin(ang).astype(np.float32)
    ropeC = np.repeat(cos.T, 2, axis=0).astype(np.float32)
    ropeS = np.repeat(sin.T, 2, axis=0).astype(np.float32)
    pinv = np.zeros((4, T), np.float32)
    for gi, w in enumerate((2, 4, 8, 16)):
        lo = np.clip(pos - w // 2, 0, T); hi = np.clip(pos + w // 2, 0, T)
        pinv[gi] = 1.0 / (hi - lo).astype(np.float32)
    poolinv = np.broadcast_to(pinv.reshape(1, 4 * T), (128, 4 * T)).copy()
    return cmat, ident, smask, ropeC, ropeS, poolinv


def _na_bias(rpb):
    col = np.arange(64)
    cs = np.clip(col - 8, 0, 48)
    cmask = (col[None, :] >= cs[:, None]) & (col[None, :] < cs[:, None] + 16)
    cidx = np.clip(col[None, :] - col[:, None] + 15, 0, 30)
    out = np.full((2, 8, 64, 8, 8, 64), NEG, np.float32)
    for var in range(8):
        for jj in range(8):
            ridx = var + jj
            if ridx < 0 or ridx > 14:
                continue
            g = rpb[:, :, ridx, :][:, :, cidx]
            out[:, :, :, var, jj, :] = np.where(cmask[None, None], g, np.float32(NEG))
    return out.reshape(2, 8, 64, 8 * 512)


def _vecs(inp):
    v = np.zeros((128, NV), np.float32)

    def put(name, arr, off=0):
        a = np.asarray(arr, np.float32).reshape(-1, 128).T
        v[:, VC[name] + off: VC[name] + off + a.shape[1]] = a
    for l in range(4):
        put("ln_mix_g", inp["ln_mix_g"][l], l * 16); put("ln_mix_b", inp["ln_mix_b"][l], l * 16)
        put("ln_ffn_g", inp["ln_ffn_g"][l], l * 16); put("ln_ffn_b", inp["ln_ffn_b"][l], l * 16)
    for j in range(2):
        put("lb_logit", inp["hg_lb_logits"][j], j * 8)
        put("hg_norm_w", inp["hg_norm_w"][j], j * 8)
        put("pool_scale", inp["pool_scale"][j], j * 8)
        put("q_norm", inp["d_q_norm"][j], j); put("k_norm", inp["d_k_norm"][j], j)
    v[:, VC["eps_ln"]] = LN_EPS; v[:, VC["eps_rms"]] = RMS_EPS; v[:, VC["zero"]] = 0.0; v[:, VC["one"]] = 1.0
    return v


def make_in_maps(inp, cores):
    cmat, ident, smask, ropeC, ropeS, poolinv = _consts()
    f = lambda k: np.ascontiguousarray(np.asarray(inp[k], np.float32))
    shared = {
        "ab_w_in": f("ab_w_in"), "ab_w_out": f("ab_w_out"), "cd_w_in": f("cd_w_in"), "cd_w_out": f("cd_w_out"),
        "pool_w": f("pool_w"), "ffn_w_gate": f("ffn_w_gate"), "ffn_w_up": f("ffn_w_up"), "ffn_w_down": f("ffn_w_down"),
        "vecs": _vecs(inp), "na_bias": _na_bias(np.asarray(inp["na_rpb"], np.float32)),
        "cm": cmat, "cb": ident, "scanmask": smask.astype(ml_dtypes.bfloat16), "ropeC": ropeC, "ropeS": ropeS, "poolinv": poolinv,
    }
    x = np.asarray(inp["x"], np.float32)
    maps = []
    for c in cores:
        m = dict(shared); m["xT"] = np.ascontiguousarray(x[c].T)
        maps.append(m)
    return maps


def kernel(**inputs):
    nc = build_program()
    in_maps = make_in_maps(inputs, list(range(8)))
    res = run_bass_kernel_spmd(nc, in_maps, core_ids=list(range(8)))
    out = np.stack([np.ascontiguousarray(r["outT"].T) for r in res.results], axis=0)
    return out.astype(np.float32)
```

```python
import numpy as np
import ml_dtypes
from contextlib import ExitStack
import concourse.bass as bass
import concourse.mybir as mybir
from concourse.bass_utils import run_bass_kernel_spmd

F32, BF16 = mybir.dt.float32, mybir.dt.bfloat16
AF = mybir.ActivationFunctionType
ALU = mybir.AluOpType
AX = mybir.AxisListType

T = 2048; D = 2048; HID = 5632; NKC = 16; NB = 4; BLK = 512
DEPTH = 4
ALPHA = (2 * DEPTH) ** 0.25
LN_EPS = 1e-5; RMS_EPS = 1e-6
NEG = -30000.0
GQA_PIPE = True
GQA_NIT = 128

VC = {}
_c = 0
for _n, _w in (("ln_mix_g", 64), ("ln_mix_b", 64), ("ln_ffn_g", 64), ("ln_ffn_b", 64),
               ("lb_logit", 16), ("hg_norm_w", 16), ("pool_scale", 16), ("q_norm", 2), ("k_norm", 2),
               ("eps_ln", 1), ("eps_rms", 1), ("zero", 1), ("one", 1)):
    VC[_n] = _c; _c += _w
NV = _c


class Sched:
    def __init__(self, nc, es):
        self.nc = nc; self.es = es
        self.eng = {}
        for name, h in (("pe", nc.tensor), ("act", nc.scalar), ("dve", nc.vector), ("sp", nc.sync), ("pool", nc.gpsimd)):
            self.eng[name] = dict(h=h, sem=None, cnt=0, waited={}, name=name)
        self.sems = {}; self.latest = {}; self.tok = {}; self.nobar = set()
        self.epoch = -1
        self.new_epoch()

    def _newsem(self, name):
        s = self.es.enter_context(self.nc.semaphore(name))
        self.sems[name] = s; self.latest[name] = 0
        return name

    def new_epoch(self):
        self.epoch += 1
        for n in ("pe", "act", "dve"):
            e = self.eng[n]; e["sem"] = self._newsem(f"e{self.epoch}_{n}"); e["cnt"] = 0

    def _wait(self, e, needs):
        for k, v in needs.items():
            if v <= 0 or e["waited"].get(k, 0) >= v:
                continue
            e["h"].wait_ge(self.sems[k], v); e["waited"][k] = v

    def _needs(self, reads, writes, own, skip_own_raw):
        needs = {}
        for t in reads:
            w, r = self.tok.setdefault(t, ({}, {}))
            for k, v in w.items():
                if k == own and skip_own_raw:
                    continue
                if needs.get(k, 0) < v: needs[k] = v
        for t in writes:
            w, r = self.tok.setdefault(t, ({}, {}))
            for d in (w, r):
                for k, v in d.items():
                    if k == own:
                        continue
                    if needs.get(k, 0) < v: needs[k] = v
        return needs

    def op(self, ename, fn, reads=(), writes=(), inc=True):
        e = self.eng[ename]; own = e["sem"]
        self._wait(e, self._needs(reads, writes, own, ename == "pe"))
        ins = fn(e["h"])
        if inc:
            e["cnt"] += 1
            ins.then_inc(self.sems[own], 1)
            self.latest[own] = e["cnt"]
            val = e["cnt"]
        else:
            val = e["cnt"] + 1
        for t in reads: self.tok[t][1][own] = val
        for t in writes: self.tok[t][0][own] = val
        return ins

    def dma(self, qname, out, in_, reads, writes, key, nobar=False, wval=None):
        e = self.eng[qname]
        if key not in self.sems:
            self._newsem(key)
            if nobar: self.nobar.add(key)
        self._wait(e, self._needs(reads, writes, None, False))
        ins = e["h"].dma_start(out=out, in_=in_)
        self.latest[key] += 16
        ins.then_inc(self.sems[key], 16)
        v = self.latest[key]
        for t in reads: self.tok.setdefault(t, ({}, {}))[1][key] = v
        for t in writes: self.tok.setdefault(t, ({}, {}))[0][key] = v
        return v

    def barrier(self):
        need = {k: v for k, v in self.latest.items() if k not in self.nobar}
        for n in ("pe", "act", "dve", "sp"):
            self._wait(self.eng[n], need)


def build_program(debug=(), n_layers=DEPTH, stop_after=None):
    nc = bass.Bass("TRN2", target_bir_lowering=False)
    dbg = set(debug)

    def din(name, shape, dt=F32):
        return nc.dram_tensor(name, list(shape), dt, kind="ExternalInput").ap()

    def dscr(name, shape, dt):
        kind = "ExternalOutput" if name in dbg else "Internal"
        return nc.dram_tensor(name, list(shape), dt, kind=kind).ap()

    xT = din("xT", [D, T])
    ab_w_in = din("ab_w_in", [2, D, 8192]); ab_w_out = din("ab_w_out", [2, D, D])
    cd_w_in = din("cd_w_in", [2, D, 2560]); cd_w_out = din("cd_w_out", [2, D, D])
    pool_w = din("pool_w", [2, 4, 256, 256])
    w_gate = din("ffn_w_gate", [4, D, HID]); w_up = din("ffn_w_up", [4, D, HID]); w_down = din("ffn_w_down", [4, HID, D])
    vecs = din("vecs", [128, NV])
    na_bias = din("na_bias", [2, 8, 64, 8 * 512])
    cm = din("cm", [128, 4 * 128 + 64])
    cb = din("cb", [128, 128], BF16)
    scanmask = din("scanmask", [128, T], BF16)
    ropeC = din("ropeC", [128, T]); ropeS = din("ropeS", [128, T])
    poolinv = din("poolinv", [128, 4 * T])
    outT = nc.dram_tensor("outT", [D, T], F32, kind="ExternalOutput").ap()

    XR = dscr("XR", [D, T], F32)
    WINs = [dscr(f"WIN{l}", [32 if l % 2 == 0 else 10, 128, 16 * 256], BF16) for l in range(4)]
    WOUTs = [dscr(f"WOUT{l}", [16, 128, 16 * 128], BF16) for l in range(4)]
    WGs = [dscr(f"WG{l}", [22, 128, 16 * 256], BF16) for l in range(4)]
    WUs = [dscr(f"WU{l}", [22, 128, 16 * 256], BF16) for l in range(4)]
    WDs = [dscr(f"WD{l}", [16, 128, 44 * 128], BF16) for l in range(4)]
    WPs_ = [dscr(f"WP{j}", [128, 4 * 2 * 256], BF16) for j in range(2)]
    AQK = dscr("AQK", [2048, T], BF16)
    AV = dscr("AV", [T, 1024], BF16)
    HQ = dscr("HQ", [1024, T], BF16)
    HF = dscr("HF", [2048, T], F32)
    HI = dscr("HI", [T, 1024], BF16)
    HG = dscr("HG", [1024, T], BF16)
    CX = dscr("CX", [1024, T], F32)
    DQK = dscr("DQK", [1280, T], F32)
    DV = dscr("DV", [T, 256], BF16)
    YT = dscr("YT", [2048, T], BF16)
    HT = dscr("HT", [HID, T], BF16)

    with ExitStack() as es:
        S = Sched(nc, es)
        block = es.enter_context(nc.Block())

        def sb(name, shape, dt):
            return es.enter_context(nc.sbuf_tensor(name, list(shape), dt))

        _uid = [0]

        def nc_sbuf(name, shape, dt):
            _uid[0] += 1
            return nc.sbuf_tensor(f"{name}_u{_uid[0]}", shape, dt)

        def nc_psum(name, shape, dt):
            _uid[0] += 1
            return nc.psum_tensor(f"{name}_u{_uid[0]}", shape, dt)

        XB = sb("XB", [128, NKC, T], BF16)
        VEC = sb("VEC", [128, NV], F32)
        CM = sb("CM", [128, 4 * 128 + 64], F32)
        CB = sb("CB", [128, 128], BF16)
        WS = [sb(f"WS{i}", [128, 5632], BF16) for i in range(3)]
        ws_ctr = [0]

        S.dma("sp", VEC[:], vecs[:, :], [], ["VEC"], "c_vec")
        S.dma("sp", CM[:], cm[:, :], [], ["CM"], "c_cm")
        S.dma("sp", CB[:], cb[:, :], [], ["CB"], "c_cb")
        ONES_D = CM[:, 0:128]; ONES_H = CM[:, 128:256]; ROTP = CM[:, 256:384]
        TRIU = CM[0:64, 384:448]; TRIL = CM[0:64, 448:512]

        def vcol(name, i=0, n=1):
            return VEC[:, VC[name] + i: VC[name] + i + n]

        cast_jobs = []

        def cj(tokn, out, in_, nd):
            cast_jobs.append((tokn, out, in_, nd))

        def wset(l):
            return (WINs[l], WOUTs[l], WGs[l], WUs[l], WDs[l])

        for l in range(n_layers):
            j = l // 2
            win, wout, wg, wu, wd = wset(l)
            if l % 2 == 0:
                src = ab_w_in[j].rearrange("(kc p) (b c) -> b p kc c", p=128, c=256)
                for b in range(32):
                    cj(("win", l, b), win[b].rearrange("p (kc c) -> p kc c", c=256), src[b], 128)
                so = ab_w_out[j].rearrange("(kc p) (m c) -> m p kc c", p=128, c=128)
            else:
                src = cd_w_in[j].rearrange("(kc p) (b c) -> b p kc c", p=128, c=256)
                for b in range(10):
                    cj(("win", l, b), win[b].rearrange("p (kc c) -> p kc c", c=256), src[b], 128)
                cj(("wp", l), WPs_[j][:].rearrange("p (g kc d) -> p g kc d", g=4, kc=2),
                   pool_w[j].rearrange("g (kc p) d -> p g kc d", p=128), 64)
                so = cd_w_out[j].rearrange("(kc p) (m c) -> m p kc c", p=128, c=128)
            for m in range(16):
                cj(("wout", l), wout[m].rearrange("p (kc c) -> p kc c", c=128), so[m], 128)
            sg = w_gate[l].rearrange("(kc p) (b c) -> b p kc c", p=128, c=256)
            su = w_up[l].rearrange("(kc p) (b c) -> b p kc c", p=128, c=256)
            for b in range(22):
                cj(("wg", l), wg[b].rearrange("p (kc c) -> p kc c", c=256), sg[b], 128)
                cj(("wg", l), wu[b].rearrange("p (kc c) -> p kc c", c=256), su[b], 128)
            sd = w_down[l].rearrange("(kc p) (m c) -> m p kc c", p=128, c=128)
            for m in range(16):
                for hh in range(2):
                    cj(("wd", l), wd[m].rearrange("p (kc c) -> p kc c", c=128)[:, hh * 22:(hh + 1) * 22, :],
                       sd[m][:, hh * 22:(hh + 1) * 22, :], 176)
        S._newsem("cast"); S.nobar.add("cast")
        pool = S.eng["pool"]
        inflight = []; tot = 0
        ends = {}
        for i, (tokn, out, in_, nd) in enumerate(cast_jobs):
            while inflight and sum(x[1] for x in inflight) + nd > 850:
                v0, _ = inflight.pop(0)
                pool["h"].wait_ge(S.sems["cast"], v0)
            ins = None
            ins = pool["h"].dma_start(out=out, in_=in_)
            tot += 16
            ins.then_inc(S.sems["cast"], 16)
            inflight.append((tot, nd))
            ends[tokn] = tot
        S.latest["cast"] = tot
        cast_total = tot
        for tokn, v in ends.items():
            vv = min(cast_total, v + 16 * 8)
            S.tok.setdefault(("wc",) + tokn, ({}, {}))[0]["cast"] = vv

        def load_w(scr_ap, nel, tokc, shape3):
            i = ws_ctr[0] % 3; ws_ctr[0] += 1
            S.dma("sp", WS[i][:, 0:nel], scr_ap, [tokc], [("WS", i)], f"ws{i}")
            kc, c = shape3
            return WS[i][:, 0:nel].rearrange("p (kc c) -> p kc c", c=c), ("WS", i)

        evac_ctr = [0]

        def evac(out, in_, reads, writes, eng=None):
            if eng is None:
                eng = "act" if evac_ctr[0] % 2 == 0 else "dve"; evac_ctr[0] += 1
            if eng == "act":
                S.op("act", lambda e: e.activation(out=out, in_=in_, func=AF.Copy), reads, writes)
            else:
                S.op("dve", lambda e: e.tensor_copy(out=out, in_=in_), reads, writes)

        def mm(out, lhsT, rhs, start, stop, reads, writes, inc):
            S.op("pe", lambda e: e.matmul(out, lhsT, rhs, start=start, stop=stop), reads, writes, inc=inc)

        def phase_proj(l, nblk, spec):
            win = wset(l)[0]
            with ExitStack() as ps:
                PSB = [ps.enter_context(nc_psum(f"pp{i}", [128, 512], F32)) for i in range(4)]
                SGB = [ps.enter_context(nc_sbuf(f"sgb{i}", [128, T], BF16)) for i in range(2)]
                SGF = [ps.enter_context(nc_sbuf(f"sgf{i}", [128, T], F32)) for i in range(2)]
                SGT = [ps.enter_context(nc_sbuf(f"sgt{i}", [128, 16, 256], BF16)) for i in range(2)]
                pc = 0; sc = {"b": 0, "f": 0, "t": 0}
                nxt = load_w(win[0], 4096, ("wc", "win", l, 0), (16, 256))
                for b in range(nblk):
                    W, wtok = nxt
                    if b + 1 < nblk:
                        nxt = load_w(win[b + 1], 4096, ("wc", "win", l, b + 1), (16, 256))
                    mode, dt, dst, off = spec(b)
                    if mode == "fm":
                        for ci in range(2):
                            kk = "b" if dt == BF16 else "f"
                            si = sc[kk] % 2; sc[kk] += 1
                            stg = (SGB if dt == BF16 else SGF)[si]; stok = ("stg", kk, si)
                            for n in range(NB):
                                pt = PSB[pc % 4]; ptok = ("pp", pc % 4); pc += 1
                                for kc in range(NKC):
                                    mm(pt[:], W[:, kc, ci * 128:(ci + 1) * 128], XB[:, kc, n * BLK:(n + 1) * BLK],
                                       kc == 0, kc == NKC - 1, [wtok, ("XB", n)], [ptok], kc == NKC - 1)
                                evac(stg[:, n * BLK:(n + 1) * BLK], pt[:], [ptok], [stok])
                            r0 = off + ci * 128
                            S.dma("sp", dst[r0:r0 + 128, :], stg[:], [stok], [("scr", id(dst))], f"st{kk}{si}")
                    else:
                        si = sc["t"] % 2; sc["t"] += 1
                        stg = SGT[si]; stok = ("stg", "t", si)
                        for tt in range(16):
                            pt = PSB[pc % 4]; ptok = ("pp", pc % 4); pc += 1
                            for kc in range(NKC):
                                mm(pt[:, 0:256], XB[:, kc, tt * 128:(tt + 1) * 128], W[:, kc, :],
                                   kc == 0, kc == NKC - 1, [wtok, ("XB", tt // 4)], [ptok], kc == NKC - 1)
                            evac(stg[:, tt, :], pt[:, 0:256], [ptok], [stok])
                        S.dma("sp", dst.rearrange("(tt p) f -> p tt f", p=128)[:, :, off:off + 256], stg[:],
                              [stok], [("scr", id(dst))], f"stt{si}")
                S.barrier()

        def phase_proj_ln(l, src, nkc, wscr, wtokc, xres, gname, bname, final_out=None):
            with ExitStack() as ps:
                PSB = [ps.enter_context(nc_psum(f"lp{i}", [128, 512], F32)) for i in range(4)]
                PMEAN = ps.enter_context(nc_psum("lpm", [128, 512], F32))
                PMSQ = ps.enter_context(nc_psum("lpq", [128, 512], F32))
                SRC = ps.enter_context(nc_sbuf("lsrc", [128, nkc, BLK], BF16))
                ZB = ps.enter_context(nc_sbuf("lzb", [128, NKC, BLK], F32))
                XRS = [ps.enter_context(nc_sbuf(f"lxr{i}", [128, BLK], F32)) for i in range(3)]
                SQ = [ps.enter_context(nc_sbuf(f"lsq{i}", [128, BLK], F32)) for i in range(2)]
                TMP = [ps.enter_context(nc_sbuf(f"ltm{i}", [128, BLK], F32)) for i in range(2)]
                OST = [ps.enter_context(nc_sbuf(f"los{i}", [128, BLK], F32)) for i in range(2)]
                MEAN = ps.enter_context(nc_sbuf("lmean", [128, BLK], F32))
                M2 = ps.enter_context(nc_sbuf("lm2", [128, BLK], F32))
                RSTD = ps.enter_context(nc_sbuf("lrstd", [128, BLK], F32))
                pc = 0; xc = 0; oc = 0
                nel = nkc * 128
                for n in range(NB):
                    cols = slice(n * BLK, (n + 1) * BLK)
                    q4 = nkc // 4
                    for qi in range(4):
                        S.dma("sp", SRC[:, qi * q4:(qi + 1) * q4, :],
                              src.rearrange("(kc p) t -> p kc t", p=128)[:, qi * q4:(qi + 1) * q4, cols],
                              [("scr", id(src))], ["lsrc"], f"lsrc{qi}")
                    nxt = load_w(wscr[0], nel, wtokc, (nkc, 128))
                    for m in range(NKC):
                        W, wtok = nxt
                        if m + 1 < NKC:
                            nxt = load_w(wscr[m + 1], nel, wtokc, (nkc, 128))
                        xi = xc % 3; xc += 1
                        S.dma("sp", XRS[xi][:], xres[m * 128:(m + 1) * 128, cols], [("xr", n)], [("lxr", xi)], f"lxr{xi}")
                        pt = PSB[pc % 4]; ptok = ("lp", pc % 4); pc += 1
                        for kc in range(nkc):
                            mm(pt[:], W[:, kc, :], SRC[:, kc, :], kc == 0, kc == nkc - 1, [wtok, "lsrc"], [ptok], kc == nkc - 1)
                        S.op("dve", lambda e: e.scalar_tensor_tensor(out=ZB[:, m, :], in0=XRS[xi][:], scalar=float(ALPHA),
                                                                      in1=pt[:], op0=ALU.mult, op1=ALU.add),
                             [ptok, ("lxr", xi)], [("zb", m)])
                    for m in range(NKC):
                        si = m % 2
                        S.op("act", lambda e: e.activation(out=SQ[si][:], in_=ZB[:, m, :], func=AF.Square),
                             [("zb", m)], [("lsq", si)])
                        mm(PMEAN[:], ONES_D, ZB[:, m, :], m == 0, m == NKC - 1, [("zb", m), "CM"], ["lpm"], m == NKC - 1)
                        mm(PMSQ[:], ONES_D, SQ[si][:], m == 0, m == NKC - 1, [("lsq", si), "CM"], ["lpq"], True)
                    S.op("act", lambda e: e.activation(out=MEAN[:], in_=PMEAN[:], func=AF.Copy), ["lpm"], ["lmean"])
                    S.op("dve", lambda e: e.tensor_tensor(out=M2[:], in0=MEAN[:], in1=MEAN[:], op=ALU.mult), ["lmean"], ["lm2"])
                    S.op("dve", lambda e: e.tensor_tensor(out=M2[:], in0=PMSQ[:], in1=M2[:], op=ALU.subtract), ["lpq", "lm2"], ["lm2"])
                    S.op("act", lambda e: e.activation(out=M2[:], in_=M2[:], func=AF.Sqrt, bias=vcol("eps_ln"), scale=1.0),
                         ["lm2", "VEC"], ["lm2"])
                    S.op("dve", lambda e: e.reciprocal(out=RSTD[:], in_=M2[:]), ["lm2"], ["lrstd"])
                    for m in range(NKC):
                        ti = m % 2
                        S.op("dve", lambda e: e.tensor_tensor(out=TMP[ti][:], in0=ZB[:, m, :], in1=MEAN[:], op=ALU.subtract),
                             [("zb", m), "lmean"], [("ltm", ti)])
                        S.op("dve", lambda e: e.tensor_tensor(out=TMP[ti][:], in0=TMP[ti][:], in1=RSTD[:], op=ALU.mult),
                             [("ltm", ti), "lrstd"], [("ltm", ti)])
                        oi = oc % 2; oc += 1
                        S.op("act", lambda e: e.activation(out=OST[oi][:], in_=TMP[ti][:], func=AF.Identity,
                                                           scale=vcol(gname, l * 16 + m), bias=vcol(bname, l * 16 + m)),
                             [("ltm", ti), "VEC"], [("los", oi)])
                        if final_out is None:
                            S.op("act", lambda e: e.activation(out=XB[:, m, cols], in_=OST[oi][:], func=AF.Copy),
                                 [("los", oi)], [("XB", n)])
                            S.dma("sp", XR[m * 128:(m + 1) * 128, cols], OST[oi][:], [("los", oi)], [("xr", n)], f"los{oi}")
                        else:
                            S.dma("sp", final_out[m * 128:(m + 1) * 128, cols], OST[oi][:], [("los", oi)], [("out", n)], f"los{oi}")
                S.barrier()

        def phase_ffn1(l):
            wg, wu = wset(l)[2], wset(l)[3]
            with ExitStack() as ps:
                PG = [ps.enter_context(nc_psum(f"fg{i}", [128, 512], F32)) for i in range(2)]
                PU = [ps.enter_context(nc_psum(f"fu{i}", [128, 512], F32)) for i in range(2)]
                SGT = [ps.enter_context(nc_sbuf(f"fsg{i}", [128, BLK], F32)) for i in range(2)]
                HST = [ps.enter_context(nc_sbuf(f"fhs{i}", [128, T], BF16)) for i in range(2)]
                pc = 0; hc = 0

                def ld(sidx):
                    b, ci = sidx // 2, sidx % 2
                    i = ws_ctr[0] % 3; ws_ctr[0] += 1
                    gsrc = wg[b].rearrange("p (kc c) -> p kc c", c=256)[:, :, ci * 128:(ci + 1) * 128]
                    usrc = wu[b].rearrange("p (kc c) -> p kc c", c=256)[:, :, ci * 128:(ci + 1) * 128]
                    S.dma("sp", WS[i][:, 0:2048].rearrange("p (kc c) -> p kc c", c=128), gsrc, [("wc", "wg", l)], [("WS", i)], f"ws{i}")
                    S.dma("sp", WS[i][:, 2048:4096].rearrange("p (kc c) -> p kc c", c=128), usrc, [("wc", "wg", l)], [("WS", i)], f"wsu{i}")
                    return (WS[i][:, 0:2048].rearrange("p (kc c) -> p kc c", c=128),
                            WS[i][:, 2048:4096].rearrange("p (kc c) -> p kc c", c=128), ("WS", i))
                pend = [ld(0), ld(1)]
                for sidx in range(44):
                    if sidx + 2 < 44:
                        pend.append(ld(sidx + 2))
                    Wg, Wu, wtok = pend.pop(0)
                    hi = hc % 2; hc += 1
                    for n in range(NB):
                        pi = pc % 2; pc += 1
                        for kc in range(NKC):
                            mm(PG[pi][:], Wg[:, kc, :], XB[:, kc, n * BLK:(n + 1) * BLK],
                               kc == 0, kc == NKC - 1, [wtok, ("XB", n)], [("fg", pi)], kc == NKC - 1)
                        for kc in range(NKC):
                            mm(PU[pi][:], Wu[:, kc, :], XB[:, kc, n * BLK:(n + 1) * BLK],
                               kc == 0, kc == NKC - 1, [wtok, ("XB", n)], [("fu", pi)], kc == NKC - 1)
                        S.op("act", lambda e: e.activation(out=SGT[pi][:], in_=PG[pi][:], func=AF.Silu),
                             [("fg", pi)], [("fsg", pi)])
                        S.op("dve", lambda e: e.tensor_tensor(out=HST[hi][:, n * BLK:(n + 1) * BLK], in0=SGT[pi][:],
                                                              in1=PU[pi][:], op=ALU.mult),
                             [("fsg", pi), ("fu", pi)], [("fhs", hi)])
                    r0 = sidx * 128
                    S.dma("sp", HT[r0:r0 + 128, :], HST[hi][:], [("fhs", hi)], [("scr", id(HT))], f"fhs{hi}")
                S.barrier()

        def phase_na(l):
            j = l // 2
            with ExitStack() as ps:
                PS_S = [ps.enter_context(nc_psum(f"nas{i}", [64, 512], F32)) for i in range(2)]
                PS_T = [ps.enter_context(nc_psum(f"nat{i}", [64, 1024], BF16))[:, 0:512].rearrange("p (a b) -> p a b", b=64) for i in range(2)]
                PS_O = [ps.enter_context(nc_psum(f"nao{i}", [128, 512], F32))[:, 0:64] for i in range(2)]
                QT = [ps.enter_context(nc_sbuf(f"naq{i}", [128, T], BF16)) for i in range(2)]
                KT = [ps.enter_context(nc_sbuf(f"nak{i}", [128, T], BF16)) for i in range(2)]
                VV = [ps.enter_context(nc_sbuf(f"nav{i}", [64, 32, 128], BF16)) for i in range(2)]
                BI = [ps.enter_context(nc_sbuf(f"nab{i}", [64, 8 * 512], F32)) for i in range(2)]
                SS = [ps.enter_context(nc_sbuf(f"nass{i}", [64, 512], F32)) for i in range(2)]
                PP = [ps.enter_context(nc_sbuf(f"napp{i}", [64, 512], BF16)) for i in range(2)]
                PN = [ps.enter_context(nc_sbuf(f"napn{i}", [64, 512], BF16)) for i in range(2)]
                PT = [ps.enter_context(nc_sbuf(f"napt{i}", [64, 512], BF16)) for i in range(2)]
                ST = [ps.enter_context(nc_sbuf(f"nast{i}", [64, 4], F32)) for i in range(2)]
                YO = [ps.enter_context(nc_sbuf(f"nayo{i}", [128, T], BF16)) for i in range(2)]
                iters = [(h, r) for h in range(8) for r in range(32)]

                def load_head(h):
                    hb = h % 2
                    S.dma("sp", QT[hb][:], AQK[h * 128:(h + 1) * 128, :], [("scr", id(AQK))], [("naq", hb)], f"naq{hb}")
                    S.dma("sp", KT[hb][:], AQK[1024 + h * 128:1024 + (h + 1) * 128, :], [("scr", id(AQK))], [("nak", hb)], f"nak{hb}")
                    S.dma("sp", VV[hb][:], AV.rearrange("(r p) f -> p r f", p=64)[:, :, h * 128:(h + 1) * 128],
                          [("scr", id(AV))], [("nav", hb)], f"nav{hb}")
                    S.dma("sp", BI[hb][:], na_bias[j, h], [], [("nab", hb)], f"nab{hb}")

                def front(i):
                    h, r = iters[i]; hb = h % 2; i2 = i % 2
                    if r == 0:
                        load_head(h)
                    r0 = min(max(r - 4, 0), 24)
                    var = r0 - r + 7
                    k0 = r0 * 64
                    S.op("pe", lambda e: e.matmul(PS_S[i2][:], QT[hb][:, r * 64:(r + 1) * 64], KT[hb][:, k0:k0 + 512],
                                                  start=True, stop=True),
                         [("naq", hb), ("nak", hb)], [("nas", i2)])
                    S.op("dve", lambda e: e.scalar_tensor_tensor(out=SS[i2][:], in0=PS_S[i2][:], scalar=float(128 ** -0.5),
                                                                  in1=BI[hb][:, var * 512:(var + 1) * 512],
                                                                  op0=ALU.mult, op1=ALU.add),
                         [("nas", i2), ("nab", hb)], [("nass", i2)])
                    S.op("dve", lambda e: e.tensor_reduce(out=ST[i2][:, 0:1], in_=SS[i2][:], axis=AX.X, op=ALU.max, negate=True),
                         [("nass", i2)], [("nast", i2)])
                    S.op("act", lambda e: e.activation(out=PP[i2][:], in_=SS[i2][:], func=AF.Exp, bias=ST[i2][:, 0:1],
                                                       scale=1.0, accum_out=ST[i2][:, 1:2]),
                         [("nass", i2), ("nast", i2)], [("napp", i2), ("nast2", i2)])
                    S.op("dve", lambda e: e.reciprocal(out=ST[i2][:, 2:3], in_=ST[i2][:, 1:2]), [("nast2", i2)], [("nast3", i2)])
                    S.op("act", lambda e: e.activation(out=PN[i2][:], in_=PP[i2][:], func=AF.Identity, scale=ST[i2][:, 2:3]),
                         [("napp", i2), ("nast3", i2)], [("napn", i2)])

                def back(i):
                    h, r = iters[i]; hb = h % 2; i2 = i % 2
                    r0 = min(max(r - 4, 0), 24)
                    for jj in range(8):
                        S.op("pe", lambda e: e.transpose(PS_T[i2][:, jj, :], PN[i2][:, jj * 64:(jj + 1) * 64], CB[0:64, 0:64]),
                             [("napn", i2), "CB"], [("nat", i2)], inc=(jj == 7))
                    S.op("act", lambda e: e.activation(out=PT[i2][:], in_=PS_T[i2][:].rearrange("p a b -> p (a b)"), func=AF.Copy),
                         [("nat", i2)], [("napt", i2)])
                    for jj in range(8):
                        S.op("pe", lambda e: e.matmul(PS_O[i2][:], VV[hb][:, r0 + jj, :], PT[i2][:, jj * 64:(jj + 1) * 64],
                                                      start=(jj == 0), stop=(jj == 7)),
                             [("nav", hb), ("napt", i2)], [("nao", i2)], inc=(jj == 7))
                    S.op("dve", lambda e: e.tensor_copy(out=YO[hb][:, r * 64:(r + 1) * 64], in_=PS_O[i2][:]),
                         [("nao", i2)], [("nayo", hb)])
                    if r == 31:
                        S.dma("sp", YT[h * 128:(h + 1) * 128, :], YO[hb][:], [("nayo", hb)], [("scr", id(YT))], f"nayo{hb}")

                front(0)
                for i in range(len(iters)):
                    if i + 1 < len(iters):
                        front(i + 1)
                    back(i)
                S.barrier()

        def phase_hgrn(l):
            j = l // 2
            with ExitStack() as ps:
                P_AF = [ps.enter_context(nc_psum(f"hga{i}", [64, 512], F32)) for i in range(2)]
                P_A = [[x[:, b * 64:(b + 1) * 64] for b in range(2)] for x in P_AF]
                P_KF = [ps.enter_context(nc_psum(f"hgk{i}", [64, 1024], BF16)) for i in range(2)]
                P_K = [[x[:, b * 128:(b + 1) * 128] for b in range(2)] for x in P_KF]
                P_OF = [ps.enter_context(nc_psum(f"hgo{i}", [128, 512], F32)) for i in range(2)]
                P_O = [[x[:, 0:64] for b in range(2)] for x in P_OF]
                P_SF = [ps.enter_context(nc_psum(f"hgs{i}", [128, 512], F32)) for i in range(2)]
                P_S = [[x[:, b * 128:(b + 1) * 128] for b in range(2)] for x in P_SF]
                HQs = ps.enter_context(nc_sbuf("hq", [128, T], BF16))
                HGs = ps.enter_context(nc_sbuf("hg", [128, T], BF16))
                VVs = ps.enter_context(nc_sbuf("hv", [64, 32, 128], BF16))
                SQs = ps.enter_context(nc_sbuf("hsq", [128, T], F32))
                Z = ps.enter_context(nc_sbuf("hz", [128, T], F32))
                Bk = ps.enter_context(nc_sbuf("hbk", [128, T], F32))
                Cg = ps.enter_context(nc_sbuf("hcg", [128, T], F32))
                EA = [ps.enter_context(nc_sbuf(f"hea{d}", [128, T], F32)) for d in range(2)]
                QG = [WS[0][:, d * T:(d + 1) * T] for d in range(2)]
                KG = [WS[1][:, d * T:(d + 1) * T] for d in range(2)]
                KD = [ps.enter_context(nc_sbuf(f"hkd{d}", [128, T], BF16)) for d in range(2)]
                O32 = ps.enter_context(nc_sbuf("ho32", [128, T], F32))
                S32 = [ps.enter_context(nc_sbuf(f"hs32{d}", [128, 128], F32)) for d in range(2)]
                SBF = [[ps.enter_context(nc_sbuf(f"hsbf{d}_{b}", [128, 128], BF16)) for b in range(2)] for d in range(2)]
                ATM = [[ps.enter_context(nc_sbuf(f"hatm{d}_{b}", [64, 64], BF16)) for b in range(2)] for d in range(2)]
                KDT = [[ps.enter_context(nc_sbuf(f"hkdt{d}_{b}", [64, 128], BF16)) for b in range(2)] for d in range(2)]
                LB = ps.enter_context(nc_sbuf("hlb", [128, 4], F32))
                SMK = ps.enter_context(nc_sbuf("hsmk", [128, T], BF16))
                YO = KD[0]
                RS = ps.enter_context(nc_sbuf("hrs", [128, BLK], F32))
                S.dma("sp", SMK[:], scanmask[:, :], [], ["hsmk"], "hsmk")
                for h in range(8):
                    S.dma("sp", HQs[:], HQ[h * 128:(h + 1) * 128, :], [("scr", id(HQ))], ["hq"], "hq")
                    S.dma("sp", HGs[:], HG[h * 128:(h + 1) * 128, :], [("scr", id(HG))], ["hg"], "hg")
                    S.dma("sp", VVs[:], HI.rearrange("(r p) f -> p r f", p=64)[:, :, h * 128:(h + 1) * 128],
                          [("scr", id(HI))], ["hv"], "hv")
                    if j == 0:
                        S.op("dve", lambda e: e.tensor_copy(out=LB[:, 0:1], in_=vcol("zero")), ["VEC"], ["hlb"])
                    else:
                        S.op("dve", lambda e: e.tensor_tensor(out=LB[:, 2:3], in0=vcol("lb_logit", 8 + h), in1=vcol("lb_logit", h),
                                                              op=ALU.subtract), ["VEC"], ["hlb2"])
                        S.op("act", lambda e: e.activation(out=LB[:, 0:1], in_=LB[:, 2:3], func=AF.Sigmoid), ["hlb2"], ["hlb"])
                    S.op("dve", lambda e: e.tensor_scalar(out=LB[:, 1:2], in0=LB[:, 0:1], scalar1=-1.0, scalar2=1.0,
                                                          op0=ALU.mult, op1=ALU.add), ["hlb"], ["hlb1"])
                    S.op("act", lambda e: e.activation(out=SQs[:], in_=HQs[:], func=AF.Silu), ["hq"], ["hsq"])
                    for d in range(2):
                        S.dma("sp", Z[:], HF[d * 1024 + h * 128: d * 1024 + (h + 1) * 128, :], [("scr", id(HF))], ["hz"], "hz")
                        S.op("act", lambda e: e.activation(out=EA[d][:], in_=Z[:], func=AF.Sigmoid), ["hz"], [("hea", d)])
                        S.op("dve", lambda e: e.tensor_scalar(out=EA[d][:], in0=EA[d][:], scalar1=LB[:, 1:2], scalar2=LB[:, 0:1],
                                                              op0=ALU.mult, op1=ALU.add), [("hea", d), "hlb", "hlb1"], [("hea", d)])
                        S.op("dve", lambda e: e.tensor_scalar(out=Bk[:], in0=EA[d][:], scalar1=-1.0, scalar2=1.0,
                                                              op0=ALU.mult, op1=ALU.add), [("hea", d)], ["hbk"])
                        S.op("act", lambda e: e.activation(out=EA[d][:], in_=EA[d][:], func=AF.Ln), [("hea", d)], [("hea", d)])
                        if d == 0:
                            S.op("dve", lambda e: e.tensor_tensor_scan(out=Cg[:], data0=SMK[:], data1=EA[d][:], initial=0.0,
                                                                       op0=ALU.mult, op1=ALU.add), [("hea", d), "hsmk"], ["hcg"])
                        else:
                            S.op("dve", lambda e: e.tensor_tensor_scan(out=Cg[:, ::-1], data0=SMK[:], data1=EA[d][:, ::-1], initial=0.0,
                                                                       op0=ALU.mult, op1=ALU.add), [("hea", d), "hsmk"], ["hcg"])
                        S.op("act", lambda e: e.activation(out=EA[d][:], in_=Cg[:], func=AF.Exp), ["hcg"], [("hea", d)])
                        S.op("act", lambda e: e.activation(out=Z[:], in_=Cg[:], func=AF.Exp, scale=-1.0), ["hcg"], ["hz"])
                        S.op("dve", lambda e: e.tensor_tensor(out=QG[d][:], in0=SQs[:], in1=EA[d][:], op=ALU.mult),
                             ["hsq", ("hea", d)], [("hqg", d)])
                        S.op("dve", lambda e: e.tensor_tensor(out=KG[d][:], in0=Bk[:], in1=Z[:], op=ALU.mult),
                             ["hbk", "hz"], [("hkg", d)])
                        e3 = EA[d][:].rearrange("p (c t) -> p c t", t=64)
                        eend = e3[:, :, 63:64] if d == 0 else e3[:, :, 0:1]
                        S.op("dve", lambda e: e.tensor_tensor(out=KD[d][:].rearrange("p (c t) -> p c t", t=64),
                                                              in0=KG[d][:].rearrange("p (c t) -> p c t", t=64),
                                                              in1=eend.to_broadcast([128, 32, 64]), op=ALU.mult),
                             [("hkg", d), ("hea", d)], [("hkd", d)])
                    def geo(step, d):
                        c = step if d == 0 else 31 - step
                        return c, slice(c * 64, (c + 1) * 64), (c * 64 + 63 if d == 0 else c * 64), (TRIU if d == 0 else TRIL)

                    def part1(step, d):
                        c, tc_, eidx, tri = geo(step, d); b = step % 2
                        S.op("pe", lambda e: e.matmul(P_A[d][b], KG[d][:, tc_], QG[d][:, tc_], start=True, stop=True),
                             [("hkg", d), ("hqg", d)], [("hga", d, b)])
                        S.op("dve", lambda e: e.tensor_tensor(out=ATM[d][b][:], in0=P_A[d][b], in1=tri, op=ALU.mult),
                             [("hga", d, b), "CM"], [("hatm", d, b)])
                        S.op("pe", lambda e: e.transpose(P_K[d][b], KD[d][:, tc_], CB[:, :]), [("hkd", d), "CB"], [("hgk", d, b)])
                        S.op("act", lambda e: e.activation(out=KDT[d][b][:], in_=P_K[d][b], func=AF.Copy), [("hgk", d, b)], [("hkdt", d, b)])

                    def part2a(step, d):
                        c, tc_, eidx, tri = geo(step, d); b = step % 2
                        if step < 31:
                            S.op("pe", lambda e: e.matmul(P_S[d][b], KDT[d][b][:], VVs[:, c, :], start=True, stop=True),
                                 [("hkdt", d, b), "hv"], [("hgs", d, b)])
                            if step == 0:
                                S.op("dve", lambda e: e.tensor_copy(out=S32[d][:], in_=P_S[d][b]), [("hgs", d, b)], [("hs32", d)])
                            else:
                                S.op("dve", lambda e: e.scalar_tensor_tensor(out=S32[d][:], in0=S32[d][:], scalar=EA[d][:, eidx:eidx + 1],
                                                                              in1=P_S[d][b], op0=ALU.mult, op1=ALU.add),
                                     [("hgs", d, b), ("hs32", d), ("hea", d)], [("hs32", d)])
                            S.op("act", lambda e: e.activation(out=SBF[d][b][:], in_=S32[d][:], func=AF.Copy), [("hs32", d)], [("hsbf", d, b)])

                    def part2b(step, d):
                        c, tc_, eidx, tri = geo(step, d); b = step % 2
                        S.op("pe", lambda e: e.matmul(P_O[d][b], VVs[:, c, :], ATM[d][b][:], start=True, stop=(step == 0)),
                             ["hv", ("hatm", d, b)], [("hgo", d, 0)], inc=(step == 0))
                        if step > 0:
                            S.op("pe", lambda e: e.matmul(P_O[d][b], SBF[d][(step - 1) % 2][:], QG[d][:, tc_], start=False, stop=True),
                                 [("hsbf", d, (step - 1) % 2), ("hqg", d)], [("hgo", d, 0)])
                        if step < 16:
                            S.op("act", lambda e: e.activation(out=O32[:, tc_], in_=P_O[d][b], func=AF.Copy),
                                 [("hgo", d, 0)], [("ho32", c)])
                        else:
                            S.op("dve", lambda e: e.tensor_tensor(out=O32[:, tc_], in0=O32[:, tc_], in1=P_O[d][b], op=ALU.add),
                                 [("hgo", d, 0), ("ho32", c)], [("ho32", c)])

                    part1(0, 0); part1(0, 1)
                    for step in range(32):
                        part2a(step, 0); part2a(step, 1)
                        if step + 1 < 32:
                            part1(step + 1, 0); part1(step + 1, 1)
                        part2b(step, 0); part2b(step, 1)
                    allo = [("ho32", c) for c in range(32)]
                    S.op("act", lambda e: e.activation(out=Z[:], in_=O32[:], func=AF.Square), allo, ["hz"])
                    S.op("act", lambda e: e.activation(out=SQs[:], in_=HGs[:], func=AF.Silu), ["hg"], ["hsq"])
                    for n in range(NB):
                        cols = slice(n * BLK, (n + 1) * BLK)
                        S.op("pe", lambda e: e.matmul(P_SF[0][:], ONES_H, Z[:, cols], start=True, stop=True), ["hz", "CM"], [("hgs", 0, 0), ("hgs", 0, 1)])
                        S.op("act", lambda e: e.activation(out=RS[:], in_=P_SF[0][:], func=AF.Sqrt, bias=vcol("eps_rms"), scale=1.0),
                             [("hgs", 0, 0), ("hgs", 0, 1), "VEC"], ["hrs"])
                        S.op("dve", lambda e: e.reciprocal(out=RS[:], in_=RS[:]), ["hrs"], ["hrs"])
                        S.op("dve", lambda e: e.tensor_tensor(out=RS[:], in0=O32[:, cols], in1=RS[:], op=ALU.mult), allo + ["hrs"], ["hrs"])
                        S.op("dve", lambda e: e.scalar_tensor_tensor(out=YO[:, cols], in0=RS[:], scalar=vcol("hg_norm_w", j * 8 + h),
                                                                      in1=SQs[:, cols], op0=ALU.mult, op1=ALU.mult),
                             ["hrs", "hsq", "VEC"], [("hkd", 0)])
                    S.dma("sp", YT[1024 + h * 128:1024 + (h + 1) * 128, :], YO[:], [("hkd", 0)], [("scr", id(YT))], "hyo")
                S.barrier()

        def phase_pool(l):
            j = l // 2
            PADL = 8
            with ExitStack() as ps:
                PSB = [ps.enter_context(nc_psum(f"plp{i}", [128, 512], F32)) for i in range(2)]
                XP = [ps.enter_context(nc_sbuf(f"plx{i}", [128, T + 16], F32)) for i in range(2)]
                SA = ps.enter_context(nc_sbuf("plsa", [128, T + 16], F32))
                SBt = ps.enter_context(nc_sbuf("plsb", [128, T + 16], F32))
                PL = [ps.enter_context(nc_sbuf(f"plp{i}", [128, T], BF16)) for i in range(2)]
                PINV = ps.enter_context(nc_sbuf("plinv", [128, T], F32))
                WPs = ps.enter_context(nc_sbuf("plw", [128, 4, 2, 256], BF16))
                YO = [ps.enter_context(nc_sbuf(f"plyo{i}", [128, T], BF16)) for i in range(2)]
                S.dma("sp", WPs[:].rearrange("p g k d -> p (g k d)"), WPs_[j][:, :], [("wc", "wp", l)], ["plw"], "plw")
                for i in range(2):
                    S.op("dve", lambda e: e.memset(XP[i][:], 0.0), [], [("plx", i)])
                S.op("dve", lambda e: e.memset(SA[:], 0.0), [], ["plsa"])
                S.op("dve", lambda e: e.memset(SBt[:], 0.0), [], ["plsb"])
                yc = 0
                for g in range(4):
                    w = (2, 4, 8, 16)[g]
                    S.dma("sp", PINV[:], poolinv[:, g * T:(g + 1) * T], [], ["plinv"], "plinv")
                    for kc in range(2):
                        ch = g * 2 + kc
                        S.dma("sp", XP[kc][:, PADL:PADL + T], CX[ch * 128:(ch + 1) * 128, :], [("scr", id(CX))], [("plx", kc)], f"plx{kc}")
                        L = T + 16
                        cur, curtok, ln_ = XP[kc], ("plx", kc), L
                        bufs = [(SA, "plsa"), (SBt, "plsb")]
                        bi = 0; span = 1
                        while span < w:
                            dst, dtok = bufs[bi % 2]; bi += 1
                            nl = ln_ - span
                            S.op("dve", lambda e: e.tensor_tensor(out=dst[:, 0:nl], in0=cur[:, 0:nl], in1=cur[:, span:span + nl], op=ALU.add),
                                 [curtok], [dtok])
                            cur, curtok, ln_ = dst, dtok, nl
                            span *= 2
                        o0 = PADL - w // 2
                        dst, dtok = bufs[bi % 2]
                        S.op("dve", lambda e: e.tensor_tensor(out=dst[:, 0:T], in0=cur[:, o0:o0 + T], in1=PINV[:], op=ALU.mult),
                             [curtok, "plinv"], [dtok])
                        S.op("dve", lambda e: e.tensor_tensor(out=PL[kc][:], in0=dst[:, 0:T], in1=XP[kc][:, PADL:PADL + T], op=ALU.subtract),
                             [dtok, ("plx", kc)], [("plpl", kc)])
                    for dc in range(2):
                        yi = yc % 2; yc += 1
                        for n in range(NB):
                            pi = n % 2
                            for kc in range(2):
                                mm(PSB[pi][:], WPs[:, g, kc, dc * 128:(dc + 1) * 128], PL[kc][:, n * BLK:(n + 1) * BLK],
                                   kc == 0, kc == 1, ["plw", ("plpl", kc)], [("plps", pi)], kc == 1)
                            S.op("act", lambda e: e.activation(out=YO[yi][:, n * BLK:(n + 1) * BLK], in_=PSB[pi][:], func=AF.Identity,
                                                               scale=vcol("pool_scale", j * 8 + g * 2 + dc)),
                                 [("plps", pi), "VEC"], [("plyo", yi)])
                        r0 = g * 256 + dc * 128
                        S.dma("sp", YT[r0:r0 + 128, :], YO[yi][:], [("plyo", yi)], [("scr", id(YT))], f"plyo{yi}")
                S.barrier()

        def phase_gqa(l):
            j = l // 2
            scale = float(128 ** -0.5)
            with ExitStack() as ps:
                PS_S = ps.enter_context(nc_psum("gqs", [128, T], F32))
                PS_T = ps.enter_context(nc_psum("gqt", [128, 2048], BF16)).rearrange("p (a b) -> p a b", b=128)
                PS_O = ps.enter_context(nc_psum("gqo", [128, 512], F32))
                PS_X = ps.enter_context(nc_psum("gqx", [128, 512], F32))
                QTs = ps.enter_context(nc_sbuf("gq", [128, 8, T], BF16))
                KTs = ps.enter_context(nc_sbuf("gk", [128, 2, T], BF16))
                VVs = ps.enter_context(nc_sbuf("gv", [128, 16, 256], BF16))
                RC = ps.enter_context(nc_sbuf("grc", [128, T], F32))
                RSn = ps.enter_context(nc_sbuf("grs", [128, T], F32))
                ZZ = [ps.enter_context(nc_sbuf(f"gz{i}", [128, BLK], F32)) for i in range(2)]
                T1 = ps.enter_context(nc_sbuf("gt1", [128, BLK], F32))
                T2 = ps.enter_context(nc_sbuf("gt2", [128, BLK], F32))
                T3 = ps.enter_context(nc_sbuf("gt3", [128, BLK], F32))
                PB = [ps.enter_context(nc_sbuf(f"gp{i}", [128, T], BF16)) for i in range(2)]
                PTs = [ps.enter_context(nc_sbuf(f"gpt{i}", [128, 16, 128], BF16)) for i in range(2)]
                STt = [ps.enter_context(nc_sbuf(f"gst{i}", [128, 4], F32)) for i in range(2)]
                OB = [ps.enter_context(nc_sbuf(f"gob{i}", [128, 128], BF16)) for i in range(2)]
                YO = [ps.enter_context(nc_sbuf(f"gyo{i}", [128, T], BF16)) for i in range(2)]
                S.dma("sp", RC[:], ropeC[:, :], [], ["grc"], "grc")
                S.dma("sp", RSn[:], ropeS[:, :], [], ["grs"], "grs")
                S.dma("sp", VVs[:], DV.rearrange("(tt p) f -> p tt f", p=128), [("scr", id(DV))], ["gv"], "gv")
                zc = 0
                for hh in range(10):
                    gcol = vcol("q_norm", j) if hh < 8 else vcol("k_norm", j)
                    for n in range(NB):
                        cols = slice(n * BLK, (n + 1) * BLK)
                        zi = zc % 2; zc += 1
                        S.dma("sp", ZZ[zi][:], DQK[hh * 128:(hh + 1) * 128, cols], [("scr", id(DQK))], [("gz", zi)], f"gz{zi}")
                        S.op("act", lambda e: e.activation(out=T1[:], in_=ZZ[zi][:], func=AF.Square), [("gz", zi)], ["gt1"])
                        S.op("pe", lambda e: e.matmul(PS_O[:], ONES_H, T1[:], start=True, stop=True), ["gt1", "CM"], ["gqo"])
                        S.op("act", lambda e: e.activation(out=T2[:], in_=PS_O[:], func=AF.Sqrt, bias=vcol("eps_rms"), scale=1.0),
                             ["gqo", "VEC"], ["gt2"])
                        S.op("dve", lambda e: e.reciprocal(out=T2[:], in_=T2[:]), ["gt2"], ["gt2"])
                        S.op("dve", lambda e: e.scalar_tensor_tensor(out=T1[:], in0=ZZ[zi][:], scalar=gcol, in1=T2[:],
                                                                      op0=ALU.mult, op1=ALU.mult), [("gz", zi), "gt2", "VEC"], ["gt1"])
                        S.op("pe", lambda e: e.matmul(PS_X[:], ROTP, T1[:], start=True, stop=True), ["gt1", "CM"], ["gqx"])
                        S.op("dve", lambda e: e.tensor_tensor(out=T3[:], in0=PS_X[:], in1=RSn[:, cols], op=ALU.mult), ["gqx", "grs"], ["gt3"])
                        S.op("dve", lambda e: e.tensor_tensor(out=T2[:], in0=T1[:], in1=RC[:, cols], op=ALU.mult), ["gt1", "grc"], ["gt2"])
                        dst = QTs[:, hh, cols] if hh < 8 else KTs[:, hh - 8, cols]
                        S.op("dve", lambda e: e.tensor_tensor(out=dst, in0=T2[:], in1=T3[:], op=ALU.add), ["gt2", "gt3"], ["gqk"])
                iters = [(hq, qt) for hq in range(8) for qt in range(16)][:GQA_NIT]

                def stage_a(i):
                    hq, qt = iters[i]; kv = hq // 4; i2 = i % 2
                    qs = slice(qt * 128, (qt + 1) * 128)
                    for kb in range(4):
                        S.op("pe", lambda e: e.matmul(PS_S[:, kb * 512:(kb + 1) * 512], QTs[:, hq, qs], KTs[:, kv, kb * 512:(kb + 1) * 512],
                                                      start=True, stop=True), ["gqk"], ["gqs"], inc=(kb == 3))
                    S.op("dve", lambda e: e.tensor_reduce(out=STt[i2][:, 0:1], in_=PS_S[:], axis=AX.X, op=ALU.max, negate=True),
                         ["gqs"], [("gst", i2)])
                    S.op("dve", lambda e: e.tensor_scalar(out=STt[i2][:, 1:2], in0=STt[i2][:, 0:1], scalar1=scale, scalar2=None, op0=ALU.mult),
                         [("gst", i2)], [("gst1", i2)])
                    S.op("act", lambda e: e.activation(out=PB[i2][:], in_=PS_S[:], func=AF.Exp, bias=STt[i2][:, 1:2], scale=scale,
                                                       accum_out=STt[i2][:, 2:3]),
                         ["gqs", ("gst1", i2)], [("gp", i2), ("gst2", i2)])
                    S.op("dve", lambda e: e.reciprocal(out=STt[i2][:, 3:4], in_=STt[i2][:, 2:3]), [("gst2", i2)], [("gst3", i2)])

                def stage_t(i):
                    i2 = i % 2
                    for kc in range(16):
                        S.op("pe", lambda e: e.transpose(PS_T[:, kc, :], PB[i2][:, kc * 128:(kc + 1) * 128], CB[:, :]),
                             [("gp", i2), "CB"], ["gqt"], inc=(kc == 15))
                    S.op("act", lambda e: e.activation(out=PTs[i2][:, 0:8, :], in_=PS_T[:, 0:8, :], func=AF.Copy), ["gqt"], [("gpt", i2)])
                    S.op("dve", lambda e: e.tensor_copy(out=PTs[i2][:, 8:16, :], in_=PS_T[:, 8:16, :]), ["gqt"], [("gpt", i2)])

                def stage_v(i):
                    hq, qt = iters[i]; kv = hq // 4; i2 = i % 2; yb = hq % 2
                    qs = slice(qt * 128, (qt + 1) * 128)
                    for kc in range(16):
                        S.op("pe", lambda e: e.matmul(PS_O[:, 0:128], PTs[i2][:, kc, :], VVs[:, kc, kv * 128:(kv + 1) * 128],
                                                      start=(kc == 0), stop=(kc == 15)), [("gpt", i2), "gv"], ["gqo"], inc=(kc == 15))
                    S.op("act", lambda e: e.activation(out=OB[i2][:], in_=PS_O[:, 0:128], func=AF.Identity, scale=STt[i2][:, 3:4]),
                         ["gqo", ("gst3", i2)], [("gob", i2)])
                    S.op("pe", lambda e: e.transpose(PS_X[:, 0:64].bitcast(BF16), OB[i2][:], CB[:, :]), [("gob", i2), "CB"], ["gqx"])
                    S.op("dve", lambda e: e.tensor_copy(out=YO[yb][:, qs], in_=PS_X[:, 0:64].bitcast(BF16)), ["gqx"], [("gyo", yb)])
                    if qt == 15:
                        S.dma("sp", YT[1024 + hq * 128:1024 + (hq + 1) * 128, :], YO[yb][:], [("gyo", yb)], [("scr", id(YT))], f"gyo{yb}")

                if GQA_PIPE:
                    stage_a(0)
                    for i in range(len(iters)):
                        stage_t(i)
                        if i + 1 < len(iters):
                            stage_a(i + 1)
                        stage_v(i)
                else:
                    for i in range(len(iters)):
                        stage_a(i); stage_t(i); stage_v(i)
                S.barrier()

        with ExitStack() as ps:
            XS = [ps.enter_context(nc_sbuf(f"ixs{i}", [128, T], F32)) for i in range(2)]
            for m in range(NKC):
                i2 = m % 2
                S.dma("sp", XS[i2][:], xT[m * 128:(m + 1) * 128, :], [], [("ixs", i2)], f"ixs{i2}")
                evac(XB[:, m, :], XS[i2][:], [("ixs", i2)], [("XB", n) for n in range(NB)])
            S.barrier()

        def ab_spec(b):
            if b < 8: return ("fm", BF16, AQK, b * 256)
            if b < 12: return ("tm", BF16, AV, (b - 8) * 256)
            if b < 16: return ("fm", BF16, HQ, (b - 12) * 256)
            if b < 24: return ("fm", F32, HF, (b - 16) * 256)
            if b < 28: return ("tm", BF16, HI, (b - 24) * 256)
            return ("fm", BF16, HG, (b - 28) * 256)

        def cd_spec(b):
            if b < 4: return ("fm", F32, CX, b * 256)
            if b < 9: return ("fm", F32, DQK, (b - 4) * 256)
            return ("tm", BF16, DV, 0)

        if stop_after == 'only_gqa':
            phase_gqa(1)
        for l in range(n_layers):
            if l > 0:
                S.new_epoch()
            xres = xT if l == 0 else XR
            wout, wd = wset(l)[1], wset(l)[4]
            if l % 2 == 0:
                phase_proj(l, 32, ab_spec)
                if stop_after == ("proj", l): break
                phase_na(l)
                if stop_after == ("na", l): break
                phase_hgrn(l)
                if stop_after == ("hgrn", l): break
            else:
                phase_proj(l, 10, cd_spec)
                if stop_after == ("proj", l): break
                phase_pool(l)
                if stop_after == ("pool", l): break
                phase_gqa(l)
                if stop_after == ("gqa", l): break
            phase_proj_ln(l, YT, 16, wout, ("wc", "wout", l), xres, "ln_mix_g", "ln_mix_b")
            if stop_after == ("mixln", l): break
            phase_ffn1(l)
            if stop_after == ("ffn1", l): break
            last = (l == n_layers - 1)
            phase_proj_ln(l, HT, 44, wd, ("wc", "wd", l), XR, "ln_ffn_g", "ln_ffn_b", final_out=outT if last else None)

        S.barrier()
        fin = {k: v for k, v in S.latest.items()}
        S._wait(S.eng["sp"], fin)
    return nc


def _consts():
    cmat = np.zeros((128, 4 * 128 + 64), np.float32)
    cmat[:, 0:128] = 1.0 / 2048.0
    cmat[:, 128:256] = 1.0 / 128.0
    P = np.zeros((128, 128), np.float32)
    for i in range(64):
        P[2 * i + 1, 2 * i] = -1.0
        P[2 * i, 2 * i + 1] = 1.0
    cmat[:, 256:384] = P
    s = np.arange(64)[:, None]; t = np.arange(64)[None, :]
    cmat[0:64, 384:448] = (s <= t).astype(np.float32)
    cmat[0:64, 448:512] = (s >= t).astype(np.float32)
    ident = np.eye(128, dtype=np.float32).astype(ml_dtypes.bfloat16)
    smask = np.ones((128, T), np.float32); smask[:, ::64] = 0.0
    pos = np.arange(T)
    row = (pos // 64).astype(np.float32); col = (pos % 64).astype(np.float32)
    inv = (np.float32(10000.0) ** (-np.arange(32, dtype=np.float32) / np.float32(32))).astype(np.float32)
    ang = np.concatenate([row[:, None] * inv, col[:, None] * inv], axis=-1).astype(np.float32)
    cos = np.cos(ang).astype(np.float32); sin = np.sin(ang).astype(np.float32)
    ropeC = np.repeat(cos.T, 2, axis=0).astype(np.float32)
    ropeS = np.repeat(sin.T, 2, axis=0).astype(np.float32)
    pinv = np.zeros((4, T), np.float32)
    for gi, w in enumerate((2, 4, 8, 16)):
        lo = np.clip(pos - w // 2, 0, T); hi = np.clip(pos + w // 2, 0, T)
        pinv[gi] = 1.0 / (hi - lo).astype(np.float32)
    poolinv = np.broadcast_to(pinv.reshape(1, 4 * T), (128, 4 * T)).copy()
    return cmat, ident, smask, ropeC, ropeS, poolinv


def _na_bias(rpb):
    col = np.arange(64)
    cs = np.clip(col - 8, 0, 48)
    cmask = (col[None, :] >= cs[:, None]) & (col[None, :] < cs[:, None] + 16)
    cidx = np.clip(col[None, :] - col[:, None] + 15, 0, 30)
    out = np.full((2, 8, 64, 8, 8, 64), NEG, np.float32)
    for var in range(8):
        for jj in range(8):
            ridx = var + jj
            if ridx < 0 or ridx > 14:
                continue
            g = rpb[:, :, ridx, :][:, :, cidx]
            out[:, :, :, var, jj, :] = np.where(cmask[None, None], g, np.float32(NEG))
    return out.reshape(2, 8, 64, 8 * 512)


def _vecs(inp):
    v = np.zeros((128, NV), np.float32)

    def put(name, arr, off=0):
        a = np.asarray(arr, np.float32).reshape(-1, 128).T
        v[:, VC[name] + off: VC[name] + off + a.shape[1]] = a
    for l in range(4):
        put("ln_mix_g", inp["ln_mix_g"][l], l * 16); put("ln_mix_b", inp["ln_mix_b"][l], l * 16)
        put("ln_ffn_g", inp["ln_ffn_g"][l], l * 16); put("ln_ffn_b", inp["ln_ffn_b"][l], l * 16)
    for j in range(2):
        put("lb_logit", inp["hg_lb_logits"][j], j * 8)
        put("hg_norm_w", inp["hg_norm_w"][j], j * 8)
        put("pool_scale", inp["pool_scale"][j], j * 8)
        put("q_norm", inp["d_q_norm"][j], j); put("k_norm", inp["d_k_norm"][j], j)
    v[:, VC["eps_ln"]] = LN_EPS; v[:, VC["eps_rms"]] = RMS_EPS; v[:, VC["zero"]] = 0.0; v[:, VC["one"]] = 1.0
    return v


def make_in_maps(inp, cores):
    cmat, ident, smask, ropeC, ropeS, poolinv = _consts()
    f = lambda k: np.ascontiguousarray(np.asarray(inp[k], np.float32))
    shared = {
        "ab_w_in": f("ab_w_in"), "ab_w_out": f("ab_w_out"), "cd_w_in": f("cd_w_in"), "cd_w_out": f("cd_w_out"),
        "pool_w": f("pool_w"), "ffn_w_gate": f("ffn_w_gate"), "ffn_w_up": f("ffn_w_up"), "ffn_w_down": f("ffn_w_down"),
        "vecs": _vecs(inp), "na_bias": _na_bias(np.asarray(inp["na_rpb"], np.float32)),
        "cm": cmat, "cb": ident, "scanmask": smask.astype(ml_dtypes.bfloat16), "ropeC": ropeC, "ropeS": ropeS, "poolinv": poolinv,
    }
    x = np.asarray(inp["x"], np.float32)
    maps = []
    for c in cores:
        m = dict(shared); m["xT"] = np.ascontiguousarray(x[c].T)
        maps.append(m)
    return maps


def kernel(**inputs):
    nc = build_program()
    in_maps = make_in_maps(inputs, list(range(8)))
    res = run_bass_kernel_spmd(nc, in_maps, core_ids=list(range(8)))
    out = np.stack([np.ascontiguousarray(r["outT"].T) for r in res.results], axis=0)
    return out.astype(np.float32)
```
